# Optimizing a Trainium2 kernel written in Bass

```python
import math, functools
import jax, jax.numpy as jnp
from jax import lax
import numpy as np

D_MODEL = 2048
BATCH = 1
SEQ = 8192
DEPTH = 2
DEC_BATCH = 8
DEC_SEQ = 64
PAST_LEN = 2048

CHUNK = 64
N_A_LAYERS = DEPTH // 2
N_B_LAYERS = DEPTH - N_A_LAYERS

GLA_HEADS = 4
GLA_DK = (D_MODEL // 2) // GLA_HEADS
GLA_DV = D_MODEL // GLA_HEADS
GLA_HK = GLA_HEADS * GLA_DK
GLA_HV = GLA_HEADS * GLA_DV
GLA_LOWRANK = 16
GLA_TAU = 16.0
GLA_IN = 2 * GLA_HK + 2 * GLA_HV + GLA_LOWRANK

FOX_HEADS = 16
FOX_HD = D_MODEL // FOX_HEADS
FOX_FORGET_BIAS = 3.0
Q_BLOCK = 128

D_FF = ((8 * D_MODEL // 3 + 255) // 256) * 256
EPS = 1e-6

kernel_name = 'streaming_gla_fox_yoco'


def rmsnorm(x, g):
    xf = x.astype(jnp.float32)
    y = xf * lax.rsqrt(jnp.mean(xf * xf, axis=-1, keepdims=True) + EPS)
    return (y * g.astype(jnp.float32)).astype(x.dtype)


def swiglu(u, w_in, w_out):
    gate, up = jnp.split(u @ w_in, 2, axis=-1)
    return (jax.nn.silu(gate) * up) @ w_out


def gla_scan(q, k, v, g, s0):
    B, L, H, DK = q.shape
    DV = v.shape[-1]
    C = CHUNK if L % CHUNK == 0 else L
    n = L // C

    def chunks(a):
        return a.astype(jnp.float32).reshape(B, n, C, H, a.shape[-1]).transpose(1, 0, 3, 2, 4)

    causal = jnp.tril(jnp.ones((C, C), dtype=bool))

    def step(S, inp):
        qc, kc, vc, gc = inp
        b = jnp.cumsum(gc, axis=2)
        o_inter = jnp.einsum('bhtd,bhdv->bhtv', qc * jnp.exp(b), S)
        diff = b[:, :, :, None, :] - b[:, :, None, :, :]
        decay = jnp.exp(jnp.where(causal[:, :, None], diff, -jnp.inf))
        att = jnp.einsum('bhtd,bhsd,bhtsd->bhts', qc, kc, decay)
        o_intra = jnp.einsum('bhts,bhsv->bhtv', att, vc)
        b_last = b[:, :, -1:, :]
        S_new = jnp.exp(b_last[:, :, 0, :])[..., None] * S + jnp.einsum(
            'bhsd,bhsv->bhdv', kc * jnp.exp(b_last - b), vc)
        return S_new, o_inter + o_intra

    S, o = lax.scan(step, s0.astype(jnp.float32), (chunks(q), chunks(k), chunks(v), chunks(g)))
    o = o.transpose(1, 0, 3, 2, 4).reshape(B, L, H, DV)
    return o, S


def gla_mixer(u, s0, w_in, w_gk2, b_gk, g_out, w_out):
    B, L, _ = u.shape
    proj = u @ w_in
    q, k, v, gate, gk_low = jnp.split(
        proj, [GLA_HK, 2 * GLA_HK, 2 * GLA_HK + GLA_HV, 2 * GLA_HK + 2 * GLA_HV], axis=-1)
    gk = jax.nn.log_sigmoid((gk_low @ w_gk2 + b_gk).astype(jnp.float32)) / GLA_TAU
    q = q.reshape(B, L, GLA_HEADS, GLA_DK) * (GLA_DK ** -0.5)
    k = k.reshape(B, L, GLA_HEADS, GLA_DK)
    v = v.reshape(B, L, GLA_HEADS, GLA_DV)
    gk = gk.reshape(B, L, GLA_HEADS, GLA_DK)
    o, S = gla_scan(q, k, v, gk, s0)
    gate = gate.reshape(B, L, GLA_HEADS, GLA_DV).astype(jnp.float32)
    o = rmsnorm(o, g_out) * jax.nn.silu(gate)
    return o.reshape(B, L, GLA_HV).astype(u.dtype) @ w_out, S


def shared_kv(h, kv_g, kv_w, kv_b_f, kv_g_k):
    B, L, _ = h.shape
    u = rmsnorm(h, kv_g)
    k, v, f = jnp.split(u @ kv_w, [D_MODEL, 2 * D_MODEL], axis=-1)
    k = rmsnorm(k.reshape(B, L, FOX_HEADS, FOX_HD), kv_g_k)
    v = v.reshape(B, L, FOX_HEADS, FOX_HD)
    logf = jax.nn.log_sigmoid((f + kv_b_f).astype(jnp.float32))
    return k, v, logf


def fox_block(qb, cqb, qi, k, v, ckT, k_idx):
    s = jnp.einsum('bqhd,bkhd->bhqk', qb.astype(jnp.float32), k.astype(jnp.float32)) * (FOX_HD ** -0.5)
    s = s + (cqb.transpose(0, 2, 1)[:, :, :, None] - ckT[:, :, None, :])
    s = jnp.where(k_idx[None, :] <= qi[:, None], s, -jnp.inf)
    p = jax.nn.softmax(s, axis=-1)
    return jnp.einsum('bhqk,bkhd->bqhd', p, v.astype(jnp.float32)).astype(v.dtype)


def fox_mixer(u, k_all, v_all, c_all, w_q, g_q, w_o):
    B, L, _ = u.shape
    Lk = k_all.shape[1]
    q = rmsnorm((u @ w_q).reshape(B, L, FOX_HEADS, FOX_HD), g_q)
    cq = c_all[:, Lk - L:]
    ckT = c_all.transpose(0, 2, 1)
    q_idx = jnp.arange(Lk - L, Lk)
    k_idx = jnp.arange(Lk)
    if L > Q_BLOCK and L % Q_BLOCK == 0:
        nb = L // Q_BLOCK
        qb = q.reshape(B, nb, Q_BLOCK, FOX_HEADS, FOX_HD).swapaxes(0, 1)
        cqb = cq.reshape(B, nb, Q_BLOCK, FOX_HEADS).swapaxes(0, 1)
        qib = q_idx.reshape(nb, Q_BLOCK)
        out = lax.map(lambda a: fox_block(a[0], a[1], a[2], k_all, v_all, ckT, k_idx), (qb, cqb, qib))
        out = out.swapaxes(0, 1).reshape(B, L, FOX_HEADS, FOX_HD)
    else:
        out = fox_block(q, cq, q_idx, k_all, v_all, ckT, k_idx)
    return out.reshape(B, L, D_MODEL) @ w_o


def trunk(x, gla_s0, past_k, past_v, past_logf, g_mix, g_ffn, gla_w_in, gla_w_gk2, gla_b_gk,
          gla_g_out, gla_w_out, kv_g, kv_w, kv_b_f, kv_g_k, fox_w_q, fox_g_q, fox_w_o,
          ffn_w_in, ffn_w_out):
    h = x
    gla_states = []
    k_new = v_new = logf_new = None
    k_all = v_all = c_all = None
    for i in range(DEPTH):
        u = rmsnorm(h, g_mix[i])
        if i < N_A_LAYERS:
            a, s = gla_mixer(u, gla_s0[i], gla_w_in[i], gla_w_gk2[i], gla_b_gk[i], gla_g_out[i], gla_w_out[i])
            gla_states.append(s)
        else:
            if i == N_A_LAYERS:
                k_new, v_new, logf_new = shared_kv(h, kv_g, kv_w, kv_b_f, kv_g_k)
                if past_k is None:
                    k_all, v_all, logf_all = k_new, v_new, logf_new
                else:
                    k_all = jnp.concatenate([past_k.astype(k_new.dtype), k_new], axis=1)
                    v_all = jnp.concatenate([past_v.astype(v_new.dtype), v_new], axis=1)
                    logf_all = jnp.concatenate([past_logf.astype(jnp.float32), logf_new], axis=1)
                c_all = jnp.cumsum(logf_all, axis=1)
            j = i - N_A_LAYERS
            a = fox_mixer(u, k_all, v_all, c_all, fox_w_q[j], fox_g_q[j], fox_w_o[j])
        h = h + a
        h = h + swiglu(rmsnorm(h, g_ffn[i]), ffn_w_in[i], ffn_w_out[i])
    return h, jnp.stack(gla_states).astype(x.dtype), k_new, v_new, logf_new.astype(x.dtype)


def setup_inputs(seed: int = 0) -> dict:
    key = jax.random.key(seed)
    ks = jax.random.split(key, 22)

    def nrm(k, shape, scale):
        return jax.random.normal(k, shape, jnp.float32) * scale

    return {
        'x_prompt': nrm(ks[0], (BATCH, SEQ, D_MODEL), 1.0),
        'x_sample': nrm(ks[1], (DEC_BATCH, DEC_SEQ, D_MODEL), 1.0),
        'state_gla': nrm(ks[2], (N_A_LAYERS, DEC_BATCH, GLA_HEADS, GLA_DK, GLA_DV), 0.1),
        'cache_k': nrm(ks[3], (DEC_BATCH, PAST_LEN, FOX_HEADS, FOX_HD), 1.0),
        'cache_v': nrm(ks[4], (DEC_BATCH, PAST_LEN, FOX_HEADS, FOX_HD), 1.0),
        'cache_logf': jax.nn.log_sigmoid(FOX_FORGET_BIAS + nrm(ks[5], (DEC_BATCH, PAST_LEN, FOX_HEADS), 1.0)),
        'g_mix': 1.0 + nrm(ks[6], (DEPTH, D_MODEL), 0.02),
        'g_ffn': 1.0 + nrm(ks[7], (DEPTH, D_MODEL), 0.02),
        'gla_w_in': nrm(ks[8], (N_A_LAYERS, D_MODEL, GLA_IN), D_MODEL ** -0.5),
        'gla_w_gk2': nrm(ks[9], (N_A_LAYERS, GLA_LOWRANK, GLA_HK), GLA_LOWRANK ** -0.5),
        'gla_b_gk': nrm(ks[10], (N_A_LAYERS, GLA_HK), 0.02),
        'gla_g_out': 1.0 + nrm(ks[11], (N_A_LAYERS, GLA_DV), 0.02),
        'gla_w_out': nrm(ks[12], (N_A_LAYERS, GLA_HV, D_MODEL), GLA_HV ** -0.5),
        'kv_g': 1.0 + nrm(ks[13], (D_MODEL,), 0.02),
        'kv_w': nrm(ks[14], (D_MODEL, 2 * D_MODEL + FOX_HEADS), D_MODEL ** -0.5),
        'kv_b_f': FOX_FORGET_BIAS + nrm(ks[15], (FOX_HEADS,), 0.1),
        'kv_g_k': 1.0 + nrm(ks[16], (FOX_HD,), 0.02),
        'fox_w_q': nrm(ks[17], (N_B_LAYERS, D_MODEL, D_MODEL), D_MODEL ** -0.5),
        'fox_g_q': 1.0 + nrm(ks[18], (N_B_LAYERS, FOX_HD), 0.02),
        'fox_w_o': nrm(ks[19], (N_B_LAYERS, D_MODEL, D_MODEL), D_MODEL ** -0.5),
        'ffn_w_in': nrm(ks[20], (DEPTH, D_MODEL, 2 * D_FF), D_MODEL ** -0.5),
        'ffn_w_out': nrm(ks[21], (DEPTH, D_FF, D_MODEL), D_FF ** -0.5),
    }


def reference(x_prompt, x_sample, state_gla, cache_k, cache_v, cache_logf, g_mix, g_ffn,
              gla_w_in, gla_w_gk2, gla_b_gk, gla_g_out, gla_w_out, kv_g, kv_w, kv_b_f, kv_g_k,
              fox_w_q, fox_g_q, fox_w_o, ffn_w_in, ffn_w_out):
    s0_prompt = jnp.zeros((N_A_LAYERS, x_prompt.shape[0], GLA_HEADS, GLA_DK, GLA_DV), x_prompt.dtype)
    y_prompt, sg_p, k_p, v_p, lf_p = trunk(
        x_prompt, s0_prompt, None, None, None, g_mix, g_ffn, gla_w_in, gla_w_gk2, gla_b_gk,
        gla_g_out, gla_w_out, kv_g, kv_w, kv_b_f, kv_g_k, fox_w_q, fox_g_q, fox_w_o, ffn_w_in, ffn_w_out)
    y_sample, sg_s, k_s, v_s, lf_s = trunk(
        x_sample, state_gla, cache_k, cache_v, cache_logf, g_mix, g_ffn, gla_w_in, gla_w_gk2, gla_b_gk,
        gla_g_out, gla_w_out, kv_g, kv_w, kv_b_f, kv_g_k, fox_w_q, fox_g_q, fox_w_o, ffn_w_in, ffn_w_out)
    return (y_prompt, y_sample, sg_p, k_p, v_p, lf_p, sg_s, k_s, v_s, lf_s)
```

```python
import numpy as np
import ml_dtypes
from contextlib import ExitStack
import concourse.bass as bass
import concourse.mybir as mybir
from concourse.bass_utils import run_bass_kernel_spmd

F32 = mybir.dt.float32
BF16 = mybir.dt.bfloat16
ALU = mybir.AluOpType
AF = mybir.ActivationFunctionType
AX = mybir.AxisListType

NCORES = 8
D = 2048
KC = 16
NTOK = 1088
TILES = [(t * 128, 128) for t in range(8)] + [(1024, 64)]
GROUPS = [(0, 512), (512, 512), (1024, 64)]
DFF = 5632
EPS = 1e-6
NEG = -30000.0
GLA_IN = 6160
STRICT = True

C_IDENT = 0
C_TRI = 128
C_RESET = 256
C_GAINS = C_RESET + NTOK
C_BGK = C_GAINS + 80
C_GOUT = C_BGK + 8
C_GK = C_GOUT + 512
C_GQ = C_GK + 128
C_BF = C_GQ + 128
C_CMASK = C_BF + 16
C_RANKB = C_CMASK + 8
C_WQ = C_RANKB + 64
C_SEL = C_WQ + 128
C_ONES = C_SEL + 512
NCONST = C_ONES + 128


class Buf:
    __slots__ = ("name", "lw", "rd")

    def __init__(self, name):
        self.name = name
        self.lw = []
        self.rd = {}


class Sched:
    ENG = ("tensor", "vector", "scalar", "gpsimd", "sync")
    DQ = ("sync", "gpsimd")
    R = 8

    def __init__(self, nc, es):
        self.nc = nc
        self.streams = {e: [] for e in self.ENG}
        self.sem = {}
        for e in ("tensor", "vector", "scalar", "gpsimd"):
            self.sem[e] = es.enter_context(nc.semaphore("s_" + e))
        for q in self.DQ:
            for s in range(self.R):
                self.sem[(q, s)] = es.enter_context(nc.semaphore("d_%s%d" % (q, s)))
        self.cnt = {e: 0 for e in self.ENG}
        self.seen = {e: {} for e in self.ENG}
        self.dn = {q: 0 for q in self.DQ}
        self.last = {}
        self.ncc = 0
        self.es = es
        self.ccb = Buf("ccchain")

    def _wait(self, eng, ev):
        k, v = ev
        if self.seen[eng].get(k, 0) >= v:
            return
        self.seen[eng][k] = v
        self.streams[eng].append(("wait", self.sem[k], v))

    def _deps(self, eng, reads, writes):
        for b in reads:
            for ev in b.lw:
                if ev[0] == eng and eng == "tensor":
                    continue
                self._wait(eng, ev)
        strict = STRICT and eng != "tensor"
        for b in writes:
            for ev in b.lw:
                if ev[0] != eng or strict:
                    self._wait(eng, ev)
            for k, v in b.rd.items():
                if k != eng or strict:
                    self._wait(eng, (k, v))

    def _mark(self, evs, reads, writes):
        for b in reads:
            for ev in evs:
                b.rd[ev[0]] = max(b.rd.get(ev[0], 0), ev[1])
        for b in writes:
            b.lw = list(evs)
            b.rd = {}

    def op(self, eng, meth, reads=(), writes=(), **kw):
        self._deps(eng, reads, writes)
        self.cnt[eng] += 1
        ev = (eng, self.cnt[eng])
        self.last[eng] = self.cnt[eng]
        self.streams[eng].append(("op", meth, kw, self.sem[eng], 1))
        self._mark([ev], reads, writes)
        return ev

    def dma(self, q, pairs, reads=(), writes=(), acc=()):
        self._deps(q, reads, writes)
        evs = []
        for (o, i) in pairs:
            n = self.dn[q]
            self.dn[q] += 1
            slot, rnd = n % self.R, n // self.R
            key = (q, slot)
            if rnd > 0:
                self._wait(q, (key, 16 * rnd))
            self.streams[q].append(("op", "dma_start", dict(out=o, in_=i), self.sem[key], 16))
            ev = (key, 16 * (rnd + 1))
            self.last[key] = ev[1]
            evs.append(ev)
        self._mark(evs, reads, writes)
        for b in acc:
            b.lw = b.lw + list(evs)
        return evs

    def collective(self, in_ap, out_ap, reads=(), writes=()):
        q = "gpsimd"
        reads = list(reads) + [self.ccb]
        writes = list(writes) + [self.ccb]
        self._deps(q, reads, writes)
        sem = self.es.enter_context(self.nc.semaphore("cc%d" % self.ncc))
        key = ("cc", self.ncc)
        self.ncc += 1
        self.sem[key] = sem
        self.streams[q].append(("cc", in_ap, out_ap, sem))
        ev = (key, 1)
        self.last[key] = 1
        self._mark([ev], reads, writes)

    def barrier(self):
        for e in self.ENG:
            for k, v in list(self.last.items()):
                if k != e:
                    self._wait(e, (k, v))

    def emit(self, block):
        nc = self.nc

        def run(e, stream):
            for it in stream:
                if it[0] == "wait":
                    e.wait_ge(it[1], it[2])
                elif it[0] == "op":
                    getattr(e, it[1])(**it[2]).then_inc(it[3], it[4])
                else:
                    e.collective_compute(
                        "AllGather", ALU.bypass, replica_groups=[list(range(NCORES))],
                        ins=[it[1]], outs=[it[2]]).then_inc(it[3])

        @block.tensor
        def _(e):
            run(e, self.streams["tensor"])

        @block.vector
        def _(e):
            run(e, self.streams["vector"])

        @block.scalar
        def _(e):
            run(e, self.streams["scalar"])

        @block.gpsimd
        def _(e):
            run(e, self.streams["gpsimd"])

        @block.sync
        def _(e):
            run(e, self.streams["sync"])


class _Skip(Exception):
    pass


def build(stage=99, skip=()):
    nc = bass.Bass("TRN2", target_bir_lowering=False)

    def din(name, shape, dt=F32):
        return nc.dram_tensor(name, list(shape), dt, kind="ExternalInput").ap()

    def dout(name, shape, dt=F32):
        return nc.dram_tensor(name, list(shape), dt, kind="ExternalOutput").ap()

    def dint(name, shape, dt=F32):
        return nc.dram_tensor(name, list(shape), dt)

    x_all = din("x_all", [NTOK, D])
    st_s = din("st_s", [4, 256, 512])
    ck_c = din("ck_c", [2048, 2048])
    cv_c = din("cv_c", [2048, 2048])
    cl_c = din("cl_c", [2048, 16])
    constf = din("constf", [128, NCONST])
    constb = din("constb", [128, 256], BF16)
    gla_w_in = din("gla_w_in", [D, GLA_IN])
    gla_w_gk2 = din("gla_w_gk2", [16, 1024])
    gla_w_out = din("gla_w_out", [D, D])
    kv_w = din("kv_w", [D, 4112])
    fox_w_q = din("fox_w_q", [D, D])
    fox_w_o = din("fox_w_o", [D, D])
    ffn_w_in = din("ffn_w_in", [2, D, 2 * DFF])
    ffn_w_out = din("ffn_w_out", [2, DFF, D])

    y_all = dout("y_all", [NTOK, D])
    k_o = dout("k_o", [NTOK, D])
    v_o = dout("v_o", [NTOK, D])
    lf_o = dout("lf_o", [NTOK, 16])
    sg_p = dout("sg_p", [4, 256, 512])
    sg_s = dout("sg_s", [4, 256, 512])

    xg_in = [dint("xg_in%d" % h, [128, 1026]) for h in range(4)]
    xg_out = [dint("xg_out%d" % h, [1024, 1026]) for h in range(4)]
    kt_in = dint("kt_in", [2048, 1024], BF16)
    kt_out = dint("kt_out", [8 * 2048, 1024], BF16)
    kts_in = dint("kts_in", [2048, 64], BF16)
    v_in = dint("v_in", [1024, 2048], BF16)
    vs_in = dint("vs_in", [64, 2048], BF16)
    v_out = dint("v_out", [8192, 2048], BF16)
    lf_in = dint("lf_in", [1024, 16])
    lfs_in = dint("lfs_in", [64, 16])
    lf_out = dint("lf_out", [8192, 16])

    es = ExitStack()
    with es:
        S = Sched(nc, es)

        sbn = {"i": 0}

        def sb(name, shape, dt, scope=None):
            sbn["i"] += 1
            t = (scope or es).enter_context(nc.sbuf_tensor("%s_%d" % (name, sbn["i"]), list(shape), dt))
            return t, Buf(name)

        cf, cf_b = sb("cf", [128, NCONST], F32)
        cb, cb_b = sb("cb", [128, 256], BF16)
        uT, uT_b = sb("uT", [128, KC, NTOK], BF16)
        NW = 2
        wp = [sb("wp%d" % i, [128, KC, 512], BF16) for i in range(NW)]
        aux, aux_b = sb("aux", [128, KC * NTOK], BF16)
        oT, oT_b = aux[:, :].rearrange("p (c t) -> p c t", c=KC), aux_b
        wstate = {"i": 0}
        ss, ss_b = sb("ss", [128, 8], F32)
        rstd, rstd_b = sb("rstd", [128, 8], F32)

        ident_f = cf[:, C_IDENT:C_IDENT + 128]
        tri_f = cf[:, C_TRI:C_TRI + 128]
        resetm = cf[:, C_RESET:C_RESET + NTOK]
        ones_f = cf[:, C_ONES:C_ONES + 128]
        ident_b = cb[:, 0:128]
        negm_b = cb[:, 128:256]

        def gain(i, c):
            return cf[:, C_GAINS + 16 * i + c:C_GAINS + 16 * i + c + 1]

        pbank = []
        for i in range(6):
            t = es.enter_context(nc.psum_tensor("pb%d" % i, [128, 512], F32))
            pbank.append((t, Buf("pb%d" % i)))
        tbank = []
        for i in range(2):
            t = es.enter_context(nc.psum_tensor("tb%d" % i, [128, 1024], BF16))
            tbank.append((t, Buf("tb%d" % i)))
        pstate = {"i": 0, "t": 0}

        def bank(lo=0, hi=6):
            i = pstate["i"]
            if i < lo or i >= hi:
                i = lo
            pstate["i"] = i + 1 if i + 1 < hi else lo
            return pbank[i]

        def tb():
            i = pstate["t"]
            pstate["t"] = 1 - i
            return tbank[i]

        S.dma("sync", [(cf[:, :], constf[:, :])], writes=[cf_b])
        S.dma("sync", [(cb[:, :], constb[:, :])], writes=[cb_b])

        def mm(ps, ps_b, lhsT, rhs, start, stop, reads):
            S.op("tensor", "matmul", reads=reads, writes=[ps_b], out=ps, lhsT=lhsT, rhs=rhs,
                 start=start, stop=stop)

        def tr(ps, ps_b, in_, ident, reads):
            S.op("tensor", "transpose", reads=reads, writes=[ps_b], out=ps, in_=in_, identity=ident)

        def load_slab(w2d, specs, kc=KC, q="gpsimd"):
            i = wstate["i"]
            wstate["i"] = (i + 1) % NW
            t, b = wp[i]
            wv = w2d.rearrange("(kc p) c -> p kc c", p=128)
            pairs = []
            hk = max(kc // 2, 1)
            for (c0, n, d0) in specs:
                for k0 in range(0, kc, hk):
                    pairs.append((t[:, k0:k0 + hk, d0:d0 + n], wv[:, k0:k0 + hk, c0:c0 + n]))
            S.dma(q, pairs, writes=[b])
            return t, b

        def rstd_from_ss(n, ncol, denom):
            S.op("vector", "tensor_scalar", reads=[ss_b], writes=[rstd_b], out=rstd[:n, :ncol], in0=ss[:n, :ncol],
                 scalar1=1.0 / denom, scalar2=EPS, op0=ALU.mult, op1=ALU.add)
            S.op("scalar", "activation", reads=[rstd_b], writes=[rstd_b], out=rstd[:n, :ncol], in_=rstd[:n, :ncol],
                 func=AF.Ln)
            S.op("scalar", "activation", reads=[rstd_b], writes=[rstd_b], out=rstd[:n, :ncol], in_=rstd[:n, :ncol],
                 func=AF.Exp, scale=-0.5)

        def norm_phase(get_tile, gain_idx):
            with ExitStack() as sc:
                xn, xn_b = sb("xn", [128, D], BF16, sc)
                junk, junk_b = sb("junk", [128, D], BF16, sc)
                for t, (c0, n) in enumerate(TILES):
                    src, src_b = get_tile(t, n)
                    S.op("scalar", "activation", reads=[src_b], writes=[junk_b, ss_b], out=junk[:n, :], in_=src,
                         func=AF.Square, accum_out=ss[:n, 0:1])
                    rstd_from_ss(n, 1, float(D))
                    S.op("vector", "tensor_scalar", reads=[src_b, rstd_b], writes=[xn_b], out=xn[:n, :], in0=src,
                         scalar1=rstd[:n, 0:1], scalar2=None, op0=ALU.mult)
                    for half in range(2):
                        tp, tp_b = tb()
                        for j in range(8):
                            c = half * 8 + j
                            tr(tp[:, j * 128:j * 128 + n], tp_b, xn[:n, c * 128:(c + 1) * 128], ident_b[:n, :n],
                               [xn_b, cb_b])
                        for j in range(8):
                            c = half * 8 + j
                            eng = "scalar" if half == 0 else "vector"
                            if eng == "scalar":
                                S.op("scalar", "mul", reads=[tp_b, cf_b], writes=[uT_b], out=uT[:, c, c0:c0 + n],
                                     in_=tp[:, j * 128:j * 128 + n], mul=gain(gain_idx, c))
                            else:
                                S.op("vector", "tensor_scalar", reads=[tp_b, cf_b], writes=[uT_b],
                                     out=uT[:, c, c0:c0 + n], in0=tp[:, j * 128:j * 128 + n],
                                     scalar1=gain(gain_idx, c), scalar2=None, op0=ALU.mult)
            S.barrier()

        def mm_tok(act, act_b, slab, slab_b, ncols, evac, kcn=KC, col0=0, tiles=None):
            for t, (c0, n) in enumerate(TILES):
                if tiles is not None and t not in tiles:
                    continue
                ps, ps_b = bank()
                for kc in range(kcn):
                    mm(ps[:n, :ncols], ps_b, act[:, kc, c0:c0 + n], slab[:, kc, col0:col0 + ncols], kc == 0,
                       kc == kcn - 1, [act_b, slab_b])
                evac(t, c0, n, ps, ps_b)

        def mm_feat(act, act_b, slab, slab_b, col0, m, evac, kcn=KC):
            for (g0, gn) in GROUPS:
                ps, ps_b = bank()
                for kc in range(kcn):
                    mm(ps[:m, :gn], ps_b, slab[:, kc, col0:col0 + m], act[:, kc, g0:g0 + gn], kc == 0, kc == kcn - 1,
                       [act_b, slab_b])
                evac(g0, gn, ps, ps_b)


        with ExitStack() as sc0:
            xs = [sb("xs%d" % i, [128, D], F32, sc0) for i in range(2)]

            def get_x(t, n):
                xt, xb = xs[t % 2]
                c0 = TILES[t][0]
                S.dma("sync", [(xt[:n, :], x_all[c0:c0 + n, :])], writes=[xb])
                return xt[:n, :], xb

            norm_phase(get_x, 0)
        S.barrier()

        gla_sc = ExitStack()
        try:
          with gla_sc:
              if "gla" in skip:
                  raise _Skip()
              gklT, gklT_b = sb("gklT", [16, NTOK], BF16, gla_sc)
              wgk2, wgk2_b = sb("wgk2", [16, 1024], BF16, gla_sc)
              negb, negb_b = sb("negb", [128, 8], F32, gla_sc)
              qtT, qtT_b = sb("qtT", [128, 2, NTOK], BF16, gla_sc)
              ktT, ktT_b = sb("ktT", [128, 2, NTOK], BF16, gla_sc)
              khT, khT_b = sb("khT", [128, 2, NTOK], BF16, gla_sc)
              khat, khat_b = sb("khat", [128, 9, 256], BF16, gla_sc)
              v_sb, v_sbb = sb("v_sb", [128, 9, 512], BF16, gla_sc)
              g_sb, g_sbb = sb("g_sb", [128, 9, 512], BF16, gla_sc)
              eb = [sb("eb%d" % i, [128, NTOK], F32, gla_sc) for i in range(2)]
              tA, tA_b = sb("tA", [128, NTOK], F32, gla_sc)
              tB, tB_b = sb("tB", [128, NTOK], F32, gla_sc)
              tC, tC_b = sb("tC", [128, NTOK], F32, gla_sc)
              xsb, xsb_b = sb("xsb", [128, 1026], F32, gla_sc)
              xr = [sb("xr%d" % i, [128, 1026], F32, gla_sc) for i in range(2)]
              sin_f = [sb("sinf%d" % i, [128, 512], F32, gla_sc) for i in range(2)]
              s_bf = [sb("sbf%d" % i, [128, 512], BF16, gla_sc) for i in range(2)]
              ss_f = [sb("ssf%d" % i, [128, 512], F32, gla_sc) for i in range(2)]
              attT, attT_b = sb("attT", [128, 128], BF16, gla_sc)
              otmp, otmp_b = sb("otmp", [128, 512], F32, gla_sc)
              og, og_b = sb("og", [128, 512], BF16, gla_sc)
              junk2, junk2_b = sb("junk2", [128, 512], BF16, gla_sc)
              dm, dm_b = sb("dm", [128, 2], F32, gla_sc)

              S.dma("gpsimd", [(wgk2[:, :], gla_w_gk2[:, :])], writes=[wgk2_b])
              S.op("vector", "tensor_scalar", reads=[cf_b], writes=[negb_b], out=negb[:, :],
                   in0=cf[:, C_BGK:C_BGK + 8], scalar1=-1.0, scalar2=None, op0=ALU.mult)

              slab, slab_b = load_slab(gla_w_in, [(6144, 16, 0)])

              def ev_gkl(g0, gn, ps, ps_b):
                  S.op("scalar", "copy", reads=[ps_b], writes=[gklT_b], out=gklT[:, g0:g0 + gn], in_=ps[:16, :gn])

              mm_feat(uT, uT_b, slab, slab_b, 0, 16, ev_gkl)

              xs_f = [(xsb[:, 0:512], xsb_b), (xsb[:, 512:1024], xsb_b)]

              def gla_chunk(h, t, c0, n, Sf, Sb, with_out, first):
                  if with_out:
                      pa, pa_b = bank()
                      for dc in range(2):
                          mm(pa[:n, :n], pa_b, ktT[:, dc, c0:c0 + n], qtT[:, dc, c0:c0 + n], dc == 0, dc == 1,
                             [ktT_b, qtT_b])
                      S.op("vector", "tensor_tensor", reads=[pa_b, cf_b], writes=[attT_b], out=attT[:n, :n],
                           in0=pa[:n, :n], in1=tri_f[:n, :n], op=ALU.mult)
                      po, po_b = bank()
                      mm(po[:n, :], po_b, attT[:n, :n], v_sb[:n, t, :], True, False, [attT_b, v_sbb])
                      for dc in range(2):
                          mm(po[:n, :], po_b, qtT[:, dc, c0:c0 + n], Sb[dc][0][:, :], False, dc == 1,
                             [qtT_b, Sb[dc][1]])
                      S.op("scalar", "activation", reads=[po_b], writes=[junk2_b, ss_b], out=junk2[:n, :],
                           in_=po[:n, :], func=AF.Square, accum_out=ss[:n, 0:1])
                      rstd_from_ss(n, 1, 512.0)
                      S.op("vector", "scalar_tensor_tensor", reads=[po_b, rstd_b, cf_b], writes=[otmp_b],
                           out=otmp[:n, :], in0=po[:n, :], scalar=rstd[:n, 0:1], in1=cf[:n, C_GOUT:C_GOUT + 512],
                           op0=ALU.mult, op1=ALU.mult)
                      S.op("vector", "tensor_tensor", reads=[otmp_b, g_sbb], writes=[og_b], out=og[:n, :],
                           in0=otmp[:n, :], in1=g_sb[:n, t, :], op=ALU.mult)
                      tp, tp_b = tb()
                      for j in range(4):
                          tr(tp[:, j * 128:j * 128 + n], tp_b, og[:n, j * 128:(j + 1) * 128], ident_b[:n, :n],
                             [og_b, cb_b])
                      for j in range(4):
                          S.op("scalar", "copy", reads=[tp_b], writes=[oT_b], out=oT[:, 4 * h + j, c0:c0 + n],
                               in_=tp[:, j * 128:j * 128 + n])
                  for dc in range(2):
                      pd, pd_b = bank()
                      mm(pd[:, :], pd_b, khat[:n, t, dc * 128:(dc + 1) * 128], v_sb[:n, t, :], True, True,
                         [khat_b, v_sbb])
                      sf, sf_b = Sf[dc]
                      if first:
                          S.op("vector", "tensor_copy", reads=[pd_b], writes=[sf_b], out=sf, in_=pd[:, :])
                      else:
                          S.op("vector", "scalar_tensor_tensor", reads=[pd_b, sf_b, eb[dc][1]], writes=[sf_b],
                               out=sf, in0=sf, scalar=eb[dc][0][:, c0 + n - 1:c0 + n], in1=pd[:, :],
                               op0=ALU.mult, op1=ALU.add)
                      if with_out:
                          S.op("scalar", "copy", reads=[sf_b], writes=[Sb[dc][1]], out=Sb[dc][0][:, :], in_=sf)

              for h in range(4):
                  slab, slab_b = load_slab(gla_w_in, [(h * 256, 256, 0), (1024 + h * 256, 256, 256)])
                  for dc in range(2):
                      fc = 2 * h + dc
                      ebt, ebb = eb[dc]
                      for (g0, gn) in GROUPS:
                          ps, ps_b = bank()
                          mm(ps[:, :gn], ps_b, wgk2[:, fc * 128:(fc + 1) * 128], gklT[:, g0:g0 + gn], True, True,
                             [wgk2_b, gklT_b])
                          S.op("scalar", "activation", reads=[ps_b, negb_b], writes=[tA_b], out=tA[:, g0:g0 + gn],
                               in_=ps[:, :gn], func=AF.Exp, scale=-1.0, bias=negb[:, fc:fc + 1])
                      S.op("scalar", "activation", reads=[tA_b], writes=[tB_b], out=tB[:, :], in_=tA[:, :],
                           func=AF.Ln, bias=1.0)
                      S.op("vector", "tensor_tensor_scan", reads=[tB_b, cf_b], writes=[tC_b], out=tC[:, :],
                           data0=resetm, data1=tB[:, :], initial=0.0, op0=ALU.mult, op1=ALU.add)
                      S.op("scalar", "activation", reads=[tC_b], writes=[ebb], out=ebt[:, :], in_=tC[:, :],
                           func=AF.Exp, scale=-1.0 / 16.0)
                      S.op("scalar", "activation", reads=[tC_b], writes=[tA_b], out=tA[:, :], in_=tC[:, :],
                           func=AF.Exp, scale=1.0 / 16.0)

                      def ev_q(g0, gn, ps, ps_b, dc=dc, ebt=ebt, ebb=ebb):
                          S.op("vector", "scalar_tensor_tensor", reads=[ps_b, ebb], writes=[qtT_b],
                               out=qtT[:, dc, g0:g0 + gn], in0=ps[:, :gn], scalar=0.0625, in1=ebt[:, g0:g0 + gn],
                               op0=ALU.mult, op1=ALU.mult)

                      def ev_k(g0, gn, ps, ps_b, dc=dc, ebt=ebt, ebb=ebb):
                          S.op("vector", "tensor_tensor", reads=[ps_b, tA_b], writes=[ktT_b],
                               out=ktT[:, dc, g0:g0 + gn], in0=ps[:, :gn], in1=tA[:, g0:g0 + gn], op=ALU.mult)
                          for (c0, n) in TILES:
                              if c0 < g0 or c0 >= g0 + gn:
                                  continue
                              S.op("vector", "scalar_tensor_tensor", reads=[ps_b, tA_b, ebb], writes=[khT_b],
                                   out=khT[:, dc, c0:c0 + n], in0=ps[:, c0 - g0:c0 - g0 + n],
                                   scalar=ebt[:, c0 + n - 1:c0 + n], in1=tA[:, c0:c0 + n], op0=ALU.mult, op1=ALU.mult)

                      mm_feat(uT, uT_b, slab, slab_b, dc * 128, 128, ev_q)
                      mm_feat(uT, uT_b, slab, slab_b, 256 + dc * 128, 128, ev_k)
                  for t, (c0, n) in enumerate(TILES):
                      tp, tp_b = tb()
                      for dc in range(2):
                          tr(tp[:n, dc * 128:(dc + 1) * 128], tp_b, khT[:, dc, c0:c0 + n], ident_b[:, :],
                             [khT_b, cb_b])
                      S.op("scalar", "copy", reads=[tp_b], writes=[khat_b], out=khat[:n, t, :], in_=tp[:n, 0:256])
                  slab, slab_b = load_slab(gla_w_in, [(2048 + h * 512, 512, 0)])

                  def ev_v(t, c0, n, ps, ps_b):
                      S.op("scalar", "copy", reads=[ps_b], writes=[v_sbb], out=v_sb[:n, t, :], in_=ps[:n, :])

                  mm_tok(uT, uT_b, slab, slab_b, 512, ev_v)
                  slab, slab_b = load_slab(gla_w_in, [(4096 + h * 512, 512, 0)])

                  def ev_g(t, c0, n, ps, ps_b):
                      S.op("scalar", "activation", reads=[ps_b], writes=[otmp_b], out=otmp[:n, :], in_=ps[:n, :],
                           func=AF.Exp, scale=-1.0)
                      S.op("vector", "tensor_scalar", reads=[otmp_b], writes=[otmp_b], out=otmp[:n, :],
                           in0=otmp[:n, :], scalar1=1.0, scalar2=None, op0=ALU.add)
                      S.op("vector", "reciprocal", reads=[otmp_b], writes=[otmp_b], out=otmp[:n, :], in_=otmp[:n, :])
                      S.op("vector", "tensor_tensor", reads=[otmp_b, ps_b], writes=[g_sbb], out=g_sb[:n, t, :],
                           in0=otmp[:n, :], in1=ps[:n, :], op=ALU.mult)

                  mm_tok(uT, uT_b, slab, slab_b, 512, ev_g)

                  for t in range(8):
                      c0, n = TILES[t]
                      gla_chunk(h, t, c0, n, xs_f, None, False, t == 0)
                  for dc in range(2):
                      ebt, ebb = eb[dc]
                      S.op("vector", "tensor_copy", reads=[ebb], writes=[xsb_b], out=xsb[:, 1024 + dc:1025 + dc],
                           in_=ebt[:, 127:128])
                      for t in range(1, 8):
                          S.op("vector", "tensor_tensor", reads=[ebb, xsb_b], writes=[xsb_b],
                               out=xsb[:, 1024 + dc:1025 + dc], in0=xsb[:, 1024 + dc:1025 + dc],
                               in1=ebt[:, t * 128 + 127:t * 128 + 128], op=ALU.mult)
                  xin_b, xout_b = Buf("xin"), Buf("xout")
                  S.dma("sync", [(xg_in[h].ap()[:, :], xsb[:, :])], reads=[xsb_b], writes=[xin_b])
                  S.collective(xg_in[h].ap().opt(), xg_out[h].ap().opt(), reads=[xin_b], writes=[xout_b])
                  for j in range(7):
                      xt, xb = xr[j % 2]
                      S.dma("sync", [(xt[:, :], xg_out[h].ap()[j * 128:(j + 1) * 128, :])], reads=[xout_b], writes=[xb])
                      mj = cf[:, C_CMASK + j:C_CMASK + j + 1]
                      S.op("vector", "tensor_scalar", reads=[xb, cf_b], writes=[dm_b], out=dm[:, :],
                           in0=xt[:, 1024:1026], scalar1=-1.0, scalar2=mj, op0=ALU.add, op1=ALU.mult)
                      S.op("vector", "tensor_scalar", reads=[dm_b], writes=[dm_b], out=dm[:, :], in0=dm[:, :],
                           scalar1=1.0, scalar2=None, op0=ALU.add)
                      for dc in range(2):
                          sf, sf_b = sin_f[dc]
                          if j == 0:
                              S.op("vector", "tensor_scalar", reads=[xb, cf_b], writes=[sf_b], out=sf[:, :],
                                   in0=xt[:, dc * 512:(dc + 1) * 512], scalar1=mj, scalar2=None, op0=ALU.mult)
                          else:
                              S.op("vector", "tensor_scalar", reads=[xb, cf_b], writes=[otmp_b], out=otmp[:, :],
                                   in0=xt[:, dc * 512:(dc + 1) * 512], scalar1=mj, scalar2=None, op0=ALU.mult)
                              S.op("vector", "scalar_tensor_tensor", reads=[sf_b, dm_b, otmp_b], writes=[sf_b],
                                   out=sf[:, :], in0=sf[:, :], scalar=dm[:, dc:dc + 1], in1=otmp[:, :],
                                   op0=ALU.mult, op1=ALU.add)
                  for dc in range(2):
                      S.op("scalar", "copy", reads=[sin_f[dc][1]], writes=[s_bf[dc][1]], out=s_bf[dc][0][:, :],
                           in_=sin_f[dc][0][:, :])
                  sinf_v = [(sin_f[dc][0][:, :], sin_f[dc][1]) for dc in range(2)]
                  for t in range(8):
                      c0, n = TILES[t]
                      gla_chunk(h, t, c0, n, sinf_v, s_bf, True, False)
                  S.dma("sync", [(sg_p[h, dc * 128:(dc + 1) * 128, :], sin_f[dc][0][:, :]) for dc in range(2)],
                        reads=[sin_f[0][1], sin_f[1][1]])
                  S.dma("sync", [(ss_f[dc][0][:, :], st_s[h, dc * 128:(dc + 1) * 128, :]) for dc in range(2)],
                        writes=[ss_f[0][1], ss_f[1][1]])
                  for dc in range(2):
                      S.op("scalar", "copy", reads=[ss_f[dc][1]], writes=[s_bf[dc][1]], out=s_bf[dc][0][:, :],
                           in_=ss_f[dc][0][:, :])
                  ssf_v = [(ss_f[dc][0][:, :], ss_f[dc][1]) for dc in range(2)]
                  gla_chunk(h, 8, 1024, 64, ssf_v, s_bf, True, False)
                  S.dma("sync", [(sg_s[h, dc * 128:(dc + 1) * 128, :], ss_f[dc][0][:, :]) for dc in range(2)],
                        reads=[ss_f[0][1], ss_f[1][1]])
              S.barrier()

        except _Skip:
            pass

        hh, hh_b = sb("hh", [128, 9, D], F32)
        hb = [Buf("h%d" % t) for t in range(9)]
        for t, (c0, n) in enumerate(TILES):
            S.dma("sync", [(hh[:n, t, :], x_all[c0:c0 + n, :])], writes=[hb[t]])

        def add_into_h(s):
            def ev(t, c0, n, ps, ps_b):
                S.op("vector", "tensor_tensor", reads=[ps_b, hb[t]], writes=[hb[t]],
                     out=hh[:n, t, s * 512:(s + 1) * 512], in0=ps[:n, :], in1=hh[:n, t, s * 512:(s + 1) * 512],
                     op=ALU.add)
            return ev

        if stage >= 1 and "gla" not in skip:
            for s in range(4):
                slab, slab_b = load_slab(gla_w_out, [(s * 512, 512, 0)])
                mm_tok(oT, oT_b, slab, slab_b, 512, add_into_h(s))
        S.barrier()

        def get_h(t, n):
            return hh[:n, t, :], hb[t]

        def ffn(layer):
            norm_phase(get_h, 1 + 3 * layer if layer == 0 else 4)
            w_in = ffn_w_in[layer]
            w_out = ffn_w_out[layer]
            with ExitStack() as sc:
                hT = [(aux[:, i * 4 * NTOK:(i + 1) * 4 * NTOK].rearrange("p (c t) -> p c t", c=4), Buf("hT%d" % i))
                      for i in range(2)]
                wo, wo_b = sb("wo", [128, 4, D], BF16, sc)
                sgt = [sb("sgt%d" % i, [128, 512], BF16, sc) for i in range(2)]
                for g in range(DFF // 512):
                    ht, ht_b = hT[g % 2]
                    for sub in range(2):
                        slab, slab_b = load_slab(w_in, [(g * 512 + sub * 256, 256, 0),
                                                        (DFF + g * 512 + sub * 256, 256, 256)])
                        for j in range(2):
                            for gi, (g0, gn) in enumerate(GROUPS):
                                pg, pg_b = bank()
                                pu, pu_b = bank()
                                for kc in range(KC):
                                    mm(pg[:, :gn], pg_b, slab[:, kc, j * 128:(j + 1) * 128], uT[:, kc, g0:g0 + gn],
                                       kc == 0, kc == KC - 1, [uT_b, slab_b])
                                for kc in range(KC):
                                    mm(pu[:, :gn], pu_b, slab[:, kc, 256 + j * 128:256 + (j + 1) * 128],
                                       uT[:, kc, g0:g0 + gn], kc == 0, kc == KC - 1, [uT_b, slab_b])
                                st, st_b = sgt[gi % 2]
                                S.op("scalar", "activation", reads=[pg_b], writes=[st_b], out=st[:, :gn],
                                     in_=pg[:, :gn], func=AF.Silu)
                                S.op("vector", "tensor_tensor", reads=[st_b, pu_b], writes=[ht_b],
                                     out=ht[:, sub * 2 + j, g0:g0 + gn], in0=st[:, :gn], in1=pu[:, :gn], op=ALU.mult)
                    wov = w_out[g * 512:(g + 1) * 512, :].rearrange("(kc p) c -> p kc c", p=128)
                    S.dma("gpsimd", [(wo[:, 0:2, :], wov[:, 0:2, :]), (wo[:, 2:4, :], wov[:, 2:4, :])], writes=[wo_b])
                    for s in range(4):
                        mm_tok(ht, ht_b, wo, wo_b, 512, add_into_h(s), kcn=4, col0=s * 512)
            S.barrier()

        if stage >= 2 and "ffn" not in skip:
            ffn(0)

        gk_bc = cf[:, C_GK:C_GK + 128]
        gq_bc = cf[:, C_GQ:C_GQ + 128]

        def headnorm_evac(dst, dst_b, gvec):
            def ev(t, c0, n, ps, ps_b, junk, junk_b):
                for j in range(4):
                    S.op("scalar", "activation", reads=[ps_b], writes=[junk_b, ss_b], out=junk[:n, 0:128],
                         in_=ps[:n, j * 128:(j + 1) * 128], func=AF.Square, accum_out=ss[:n, j:j + 1])
                rstd_from_ss(n, 4, 128.0)
                for j in range(4):
                    S.op("vector", "scalar_tensor_tensor", reads=[ps_b, rstd_b, cf_b], writes=[dst_b],
                         out=dst[:n, j * 128:(j + 1) * 128], in0=ps[:n, j * 128:(j + 1) * 128],
                         scalar=rstd[:n, j:j + 1], in1=gvec[:n, :], op0=ALU.mult, op1=ALU.mult)
            return ev

        if stage >= 3:
            norm_phase(get_h, 2)
            with ExitStack() as sc:
                kf = [sb("kf%d" % i, [128, 512], F32, sc) for i in range(2)]
                kb = [sb("kb%d" % i, [128, 512], BF16, sc) for i in range(2)]
                kst = [sb("kst%d" % i, [128, 4, 128], BF16, sc) for i in range(2)]
                junk, junk_b = sb("junk3", [128, 128], BF16, sc)
                lft, lft_b = sb("lft", [128, 16], F32, sc)
                lfo = [sb("lfo%d" % i, [128, 16], F32, sc) for i in range(2)]
                cnt = {"i": 0}
                ktin_b = Buf("ktin")
                for s in range(4):
                    slab, slab_b = load_slab(kv_w, [(s * 512, 512, 0)])

                    def ev_k(t, c0, n, ps, ps_b, s=s):
                        i = cnt["i"] % 2
                        cnt["i"] += 1
                        kft, kf_b = kf[i]
                        kbt, kb_b = kb[i]
                        kt, kt_b = kst[i]
                        headnorm_evac(kft, kf_b, gk_bc)(t, c0, n, ps, ps_b, junk, junk_b)
                        S.dma("sync", [(k_o[c0:c0 + n, s * 512:(s + 1) * 512], kft[:n, :])], reads=[kf_b])
                        S.op("scalar", "copy", reads=[kf_b], writes=[kb_b], out=kbt[:n, :], in_=kft[:n, :])
                        tp, tp_b = tb()
                        for j in range(4):
                            tr(tp[:, j * 128:j * 128 + n], tp_b, kbt[:n, j * 128:(j + 1) * 128], ident_b[:n, :n],
                               [kb_b, cb_b])
                        S.op("vector", "tensor_copy", reads=[tp_b], writes=[kt_b], out=kt[:, :, :n],
                             in_=tp[:, 0:512].rearrange("p (j c) -> p j c", j=4)[:, :, :n])
                        pairs = []
                        for j in range(4):
                            r0 = (4 * s + j) * 128
                            if t < 8:
                                pairs.append((kt_in.ap()[r0:r0 + 128, c0:c0 + n], kt[:, j, :n]))
                            else:
                                pairs.append((kts_in.ap()[r0:r0 + 128, 0:n], kt[:, j, :n]))
                        S.dma("sync", pairs, reads=[kt_b], acc=[ktin_b])

                    mm_tok(uT, uT_b, slab, slab_b, 512, ev_k)
                vin_b = Buf("vin")
                for s in range(4):
                    slab, slab_b = load_slab(kv_w, [(2048 + s * 512, 512, 0)])

                    def ev_v2(t, c0, n, ps, ps_b, s=s):
                        i = cnt["i"] % 2
                        cnt["i"] += 1
                        kft, kf_b = kf[i]
                        kbt, kb_b = kb[i]
                        S.op("scalar", "copy", reads=[ps_b], writes=[kf_b], out=kft[:n, :], in_=ps[:n, :])
                        S.dma("sync", [(v_o[c0:c0 + n, s * 512:(s + 1) * 512], kft[:n, :])], reads=[kf_b])
                        S.op("vector", "tensor_copy", reads=[kf_b], writes=[kb_b], out=kbt[:n, :], in_=kft[:n, :])
                        if t < 8:
                            dst = v_in.ap()[c0:c0 + n, s * 512:(s + 1) * 512]
                        else:
                            dst = vs_in.ap()[0:n, s * 512:(s + 1) * 512]
                        S.dma("sync", [(dst, kbt[:n, :])], reads=[kb_b], acc=[vin_b])

                    mm_tok(uT, uT_b, slab, slab_b, 512, ev_v2)
                slab, slab_b = load_slab(kv_w, [(4096, 16, 0)])
                lfin_b = Buf("lfin")

                def ev_f(t, c0, n, ps, ps_b):
                    i = cnt["i"] % 2
                    cnt["i"] += 1
                    lo, lo_b = lfo[i]
                    S.op("vector", "tensor_tensor", reads=[ps_b, cf_b], writes=[lft_b], out=lft[:n, :],
                         in0=ps[:n, 0:16], in1=cf[:n, C_BF:C_BF + 16], op=ALU.add)
                    S.op("scalar", "activation", reads=[lft_b], writes=[lft_b], out=lft[:n, :], in_=lft[:n, :],
                         func=AF.Exp, scale=-1.0)
                    S.op("scalar", "activation", reads=[lft_b], writes=[lft_b], out=lft[:n, :], in_=lft[:n, :],
                         func=AF.Ln, bias=1.0)
                    S.op("vector", "tensor_scalar", reads=[lft_b], writes=[lo_b], out=lo[:n, :], in0=lft[:n, :],
                         scalar1=-1.0, scalar2=None, op0=ALU.mult)
                    dst = lf_in.ap()[c0:c0 + n, :] if t < 8 else lfs_in.ap()[0:n, :]
                    S.dma("sync", [(lf_o[c0:c0 + n, :], lo[:n, :]), (dst, lo[:n, :])], reads=[lo_b], acc=[lfin_b])

                mm_tok(uT, uT_b, slab, slab_b, 16, ev_f)
                ktout_b, vout_b, lfout_b = Buf("ktout"), Buf("vout"), Buf("lfout")
                if stage >= 3.5:
                    S.collective(kt_in.ap().opt(), kt_out.ap().opt(), reads=[ktin_b], writes=[ktout_b])
                    S.collective(v_in.ap().opt(), v_out.ap().opt(), reads=[vin_b], writes=[vout_b])
                    S.collective(lf_in.ap().opt(), lf_out.ap().opt(), reads=[lfin_b], writes=[lfout_b])
            S.barrier()

        if stage >= 4:
            norm_phase(get_h, 3)
            fox_sc = ExitStack()
            with fox_sc:
                QT, QT_b = aux[:, :].rearrange("p (c t) -> p c t", c=KC), Buf("QT")
                with ExitStack() as sc:
                    qf = [sb("qf%d" % i, [128, 512], F32, sc) for i in range(2)]
                    qb = [sb("qb%d" % i, [128, 512], BF16, sc) for i in range(2)]
                    junk, junk_b = sb("junk4", [128, 128], BF16, sc)
                    cnt = {"i": 0}
                    for s in range(4):
                        slab, slab_b = load_slab(fox_w_q, [(s * 512, 512, 0)])

                        def ev_q2(t, c0, n, ps, ps_b, s=s):
                            i = cnt["i"] % 2
                            cnt["i"] += 1
                            qft, qf_b = qf[i]
                            qbt, qb_b = qb[i]
                            headnorm_evac(qft, qf_b, gq_bc)(t, c0, n, ps, ps_b, junk, junk_b)
                            S.op("scalar", "copy", reads=[qf_b], writes=[qb_b], out=qbt[:n, :], in_=qft[:n, :])
                            tp, tp_b = tb()
                            for j in range(4):
                                tr(tp[:, j * 128:j * 128 + n], tp_b, qbt[:n, j * 128:(j + 1) * 128],
                                   ident_b[:n, :n], [qb_b, cb_b])
                            S.op("vector", "tensor_copy", reads=[tp_b], writes=[QT_b],
                                 out=QT[:, 4 * s:4 * s + 4, c0:c0 + n],
                                 in_=tp[:, 0:512].rearrange("p (j c) -> p j c", j=4)[:, :, :n])

                        mm_tok(uT, uT_b, slab, slab_b, 512, ev_q2)
                S.barrier()

                S.barrier()
                uflat = uT[:, :, :].rearrange("p c t -> p (c t)")
                uflat32 = uflat.bitcast(F32)

                def carve(flat, off, shape):
                    n = 1
                    for d_ in shape:
                        n *= d_
                    v = flat[:, off:off + n]
                    if len(shape) == 2:
                        v = v.rearrange("p (a b) -> p a b", a=shape[0])
                    return v

                lfa, lfa_b = carve(uflat32, 0, [64, 16]), Buf("lfa")
                tot, tot_b = carve(uflat32, 1024, [64, 16]), Buf("tot")
                inc, inc_b = carve(uflat32, 2048, [64, 16]), Buf("inc")
                ckk, ck_b = carve(uflat32, 3072, [64, 16]), Buf("ckk")
                tmpc, tmpc_b = carve(uflat32, 4096, [64, 16]), Buf("tmpc")
                lfs, lfs_b = carve(uflat32, 5120, [17, 16]), Buf("lfs")
                tots, tots_b = carve(uflat32, 5120 + 272, [17, 16]), Buf("tots")
                incs, incs_b = carve(uflat32, 5120 + 544, [17, 16]), Buf("incs")
                cref, cref_b = sb("cref", [128, 3, 16], F32, fox_sc)
                ckown, ckown_b = sb("ckown", [128, 8, 16], F32, fox_sc)
                biasG = [sb("biasG%d" % i, [128, 64, 16], F32, fox_sc) for i in range(2)]
                biasO = [sb("biasO%d" % i, [128, 8, 16], F32, fox_sc) for i in range(2)]
                biasS, biasS_b = sb("biasS", [128, 17, 16], F32, fox_sc)

                def cumsum_tiles(src, src_b, nt_, tot_t, tot_tb, inc_t, inc_tb, dst, dst_b):
                    flat = src[:, :, :].rearrange("p t h -> p (t h)")
                    totf = tot_t[:, :, :].rearrange("p t h -> p (t h)")
                    ncol = nt_ * 16
                    for c0 in range(0, ncol, 512):
                        cn = min(512, ncol - c0)
                        ps, ps_b = bank()
                        mm(ps[:, :cn], ps_b, ones_f, flat[:, c0:c0 + cn], True, True, [cf_b, src_b])
                        S.op("scalar", "copy", reads=[ps_b], writes=[tot_tb], out=totf[:, c0:c0 + cn], in_=ps[:, :cn])
                    for hd in range(16):
                        S.op("vector", "tensor_tensor_scan", reads=[tot_tb, cf_b], writes=[inc_tb],
                             out=inc_t[:, :, hd], data0=ones_f[:, 0:nt_], data1=tot_t[:, :, hd], initial=0.0,
                             op0=ALU.mult, op1=ALU.add)
                    S.op("vector", "tensor_tensor", reads=[inc_tb, tot_tb], writes=[inc_tb], out=inc_t[:, :, :],
                         in0=inc_t[:, :, :], in1=tot_t[:, :, :], op=ALU.subtract)
                    dstf = dst[:, :, :].rearrange("p t h -> p (t h)")
                    incf = inc_t[:, :, :].rearrange("p t h -> p (t h)")
                    for c0 in range(0, nt_, 32):
                        cn = min(32, nt_ - c0)
                        ps, ps_b = bank()
                        for t in range(cn):
                            mm(ps[:, t * 16:(t + 1) * 16], ps_b, tri_f, src[:, c0 + t, :], True, True, [cf_b, src_b])
                        S.op("vector", "tensor_tensor", reads=[ps_b, inc_tb], writes=[dst_b],
                             out=dstf[:, c0 * 16:(c0 + cn) * 16], in0=ps[:, :cn * 16],
                             in1=incf[:, c0 * 16:(c0 + cn) * 16], op=ALU.add)

                lfv = lf_out.ap().rearrange("(t p) h -> p t h", p=128)
                S.dma("sync", [(lfa[:, 8 * r:8 * r + 8, :], lfv[:, 8 * r:8 * r + 8, :]) for r in range(8)],
                      reads=[lfout_b], writes=[lfa_b])
                cumsum_tiles(lfa, lfa_b, 64, tot, tot_b, inc, inc_b, ckk, ck_b)
                for qt in range(2):
                    wqv = cf[:, C_WQ + 64 * qt:C_WQ + 64 * (qt + 1)]
                    S.op("vector", "tensor_tensor", reads=[tot_b, cf_b], writes=[tmpc_b], out=tmpc[:, :, :],
                         in0=tot[:, :, :], in1=wqv.unsqueeze(2).to_broadcast([128, 64, 16]), op=ALU.mult)
                    S.op("vector", "tensor_reduce", reads=[tmpc_b], writes=[cref_b], out=cref[:, qt, :],
                         in_=tmpc[:, :, :].rearrange("p t h -> p h t"), axis=AX.X, op=ALU.add)
                for j in range(8):
                    selv = cf[:, C_SEL + 64 * j:C_SEL + 64 * (j + 1)]
                    S.op("vector", "tensor_tensor", reads=[ck_b, cf_b], writes=[tmpc_b], out=tmpc[:, :, :],
                         in0=ckk[:, :, :], in1=selv.unsqueeze(2).to_broadcast([128, 64, 16]), op=ALU.mult)
                    S.op("vector", "tensor_reduce", reads=[tmpc_b], writes=[ckown_b], out=ckown[:, j, :],
                         in_=tmpc[:, :, :].rearrange("p t h -> p h t"), axis=AX.X, op=ALU.add)
                rbv = cf[:, C_RANKB:C_RANKB + 64]
                for qt in range(2):
                    bg, bg_b = biasG[qt]
                    S.op("vector", "tensor_tensor", reads=[cref_b, ck_b], writes=[bg_b], out=bg[:, :, :],
                         in0=cref[:, qt:qt + 1, :].to_broadcast([128, 64, 16]), in1=ckk[:, :, :], op=ALU.subtract)
                    S.op("vector", "tensor_tensor", reads=[bg_b, cf_b], writes=[bg_b], out=bg[:, :, :],
                         in0=bg[:, :, :], in1=rbv.unsqueeze(2).to_broadcast([128, 64, 16]), op=ALU.add)
                    bo, bo_b = biasO[qt]
                    S.op("vector", "tensor_tensor", reads=[cref_b, ckown_b], writes=[bo_b], out=bo[:, :, :],
                         in0=cref[:, qt:qt + 1, :].to_broadcast([128, 8, 16]), in1=ckown[:, :, :], op=ALU.subtract)
                S.op("vector", "memset", writes=[lfs_b], ap=lfs[:, :, :], constant=0.0)
                clv = cl_c.rearrange("(t p) h -> p t h", p=128)
                S.dma("sync", [(lfs[:, 0:8, :], clv[:, 0:8, :]), (lfs[:, 8:16, :], clv[:, 8:16, :]),
                               (lfs[:64, 16, :], lfs_in.ap()[:, :])], reads=[lfin_b], writes=[lfs_b])
                cumsum_tiles(lfs, lfs_b, 17, tots, tots_b, incs, incs_b, biasS, biasS_b)
                S.op("vector", "tensor_reduce", reads=[tots_b], writes=[cref_b], out=cref[:, 2, :],
                     in_=tots[:, :, :].rearrange("p t h -> p h t"), axis=AX.X, op=ALU.add)
                S.op("vector", "tensor_tensor", reads=[cref_b, biasS_b], writes=[biasS_b], out=biasS[:, :, :],
                     in0=cref[:, 2:3, :].to_broadcast([128, 17, 16]), in1=biasS[:, :, :], op=ALU.subtract)

                S.barrier()
                KTh, KTh_b = carve(uflat, 0, [8, 1024]), Buf("KTh")
                Vh, Vh_b = carve(uflat, 8192, [64, 130]), Buf("Vh")
                w0 = wp[0][0][:, :, :].rearrange("p c t -> p (c t)")
                w1 = wp[1][0][:, :, :].rearrange("p c t -> p (c t)")
                KTo, KTo_b = carve(w0, 0, [NTOK]), Buf("KTo")
                Vo, Vo_b = carve(w0, 1088, [9, 130]), Buf("Vo")
                kct, kct_b = carve(w0, 2304, [16, 128]), Buf("kct")
                KTs, KTs_b = carve(w0, 4352, [2048]), Buf("KTs")
                Vs, Vs_b = carve(w1, 0, [16, 130]), Buf("Vs")
                PT = [(carve(w1, 2080 + 512 * i, [512]), Buf("PT%d" % i)) for i in range(3)]
                OTh, OTh_b = carve(w1, 3616, [NTOK]), Buf("OTh")
                woh, woh_b = carve(w1, 4704, [D]), Buf("woh")
                onb, onb_b = carve(w1, 6752, [128]), Buf("onb")
                rec, rec_b = sb("rec", [128, 1], F32, fox_sc)
                S.op("vector", "memset", writes=[Vh_b], ap=Vh[:, :, 128:130], constant=1.0)
                S.op("vector", "memset", writes=[Vo_b], ap=Vo[:, :, 128:130], constant=1.0)
                S.op("vector", "memset", writes=[Vs_b], ap=Vs[:, :, 128:130], constant=1.0)
                pcnt = {"i": 0}
                SC = 128.0 ** -0.5

                def att_step(h, kmat, kb_, n_k, qcol0, qn, lo, diag, bias_ap, bias_b, vmat, vb_, obanks, first, last,
                             sub_n):
                    ps, ps_b = bank(4, 6)
                    c_lo = lo * 128
                    if diag:
                        dn_ = min(128, qn - c_lo)
                        mm(ps[:n_k, c_lo:c_lo + dn_], ps_b, kmat, QT[:, h, qcol0 + c_lo:qcol0 + c_lo + dn_], True, False,
                           [kb_, QT_b])
                        mm(ps[:n_k, c_lo:c_lo + dn_], ps_b, ident_b[:n_k, :n_k], negm_b[:n_k, :dn_], False, True, [cb_b])
                        if c_lo + dn_ < qn:
                            mm(ps[:n_k, c_lo + dn_:qn], ps_b, kmat, QT[:, h, qcol0 + c_lo + dn_:qcol0 + qn], True, True,
                               [kb_, QT_b])
                    else:
                        mm(ps[:n_k, c_lo:qn], ps_b, kmat, QT[:, h, qcol0 + c_lo:qcol0 + qn], True, True, [kb_, QT_b])
                    pt, pt_b = PT[pcnt["i"] % 3]
                    pcnt["i"] += 1
                    S.op("scalar", "activation", reads=[ps_b, bias_b], writes=[pt_b], out=pt[:n_k, c_lo:qn],
                         in_=ps[:n_k, c_lo:qn], func=AF.Exp, scale=SC, bias=bias_ap)
                    nsub = (qn + 127) // 128
                    for sub in range(lo, nsub):
                        ob, ob_b = obanks[sub]
                        mm(ob[:sub_n, 0:129], ob_b, pt[:n_k, sub * 128:sub * 128 + sub_n], vmat, first, last, [pt_b, vb_])

                def att_evac(h, obanks, qcol0, nsub, sub_n):
                    for sub in range(nsub):
                        ob, ob_b = obanks[sub]
                        S.op("vector", "reciprocal", reads=[ob_b], writes=[rec_b], out=rec[:sub_n, :],
                             in_=ob[:sub_n, 128:129])
                        S.op("vector", "tensor_scalar", reads=[ob_b, rec_b], writes=[onb_b], out=onb[:sub_n, :],
                             in0=ob[:sub_n, 0:128], scalar1=rec[:sub_n, 0:1], scalar2=None, op0=ALU.mult)
                        tp, tp_b = tb()
                        tr(tp[:, 0:sub_n], tp_b, onb[:sub_n, :], ident_b[:sub_n, :sub_n], [onb_b, cb_b])
                        S.op("scalar", "copy", reads=[tp_b], writes=[OTh_b],
                             out=OTh[:, qcol0 + sub * 128:qcol0 + sub * 128 + sub_n], in_=tp[:, 0:sub_n])

                obanks = [pbank[i] for i in range(4)]
                ktv = kt_out.ap().rearrange("(r hd) t -> hd r t", r=8)
                vv = v_out.ap().rearrange("(t p) c -> p t c", p=128)
                for h in range(16):
                    hs = slice(h * 128, (h + 1) * 128)
                    S.dma("sync", [(KTh[:, 0:4, :], ktv[hs, 0:4, :]), (KTh[:, 4:8, :], ktv[hs, 4:8, :])],
                          reads=[ktout_b], writes=[KTh_b])
                    S.dma("sync", [(Vh[:, 8 * r:8 * r + 8, 0:128], vv[:, 8 * r:8 * r + 8, hs]) for r in range(8)],
                          reads=[vout_b], writes=[Vh_b])
                    S.dma("sync", [(KTo[:, 0:1024], kt_in.ap()[hs, :]), (KTo[:, 1024:NTOK], kts_in.ap()[hs, :])],
                          reads=[ktin_b], writes=[KTo_b])
                    S.dma("sync", [(Vo[:, 0:8, 0:128], v_in.ap().rearrange("(t p) c -> p t c", p=128)[:, :, hs]),
                                   (Vo[:64, 8, 0:128], vs_in.ap()[:, hs])], reads=[vin_b], writes=[Vo_b])
                    ckv = ck_c.rearrange("(t p) c -> p t c", p=128)
                    cvv = cv_c.rearrange("(t p) c -> p t c", p=128)
                    S.dma("gpsimd", [(kct[:, 8 * r:8 * r + 8, :], ckv[:, 8 * r:8 * r + 8, hs]) for r in range(2)],
                          writes=[kct_b])
                    S.dma("gpsimd", [(Vs[:, 8 * r:8 * r + 8, 0:128], cvv[:, 8 * r:8 * r + 8, hs]) for r in range(2)],
                          writes=[Vs_b])
                    S.dma("gpsimd", [(woh[:, :], fox_w_o[hs, :])], writes=[woh_b])
                    for half in range(2):
                        tp, tp_b = tb()
                        for j in range(8):
                            tr(tp[:, j * 128:(j + 1) * 128], tp_b, kct[:, half * 8 + j, :], ident_b[:, :], [kct_b, cb_b])
                        S.op("vector" if half else "scalar", "tensor_copy" if half else "copy", reads=[tp_b],
                             writes=[KTs_b], out=KTs[:, half * 1024:(half + 1) * 1024], in_=tp[:, :])
                    for qt in range(2):
                        q0 = qt * 512
                        nown = 4 * qt + 4
                        for j in range(nown):
                            lo = max(0, j - 4 * qt)
                            diag = (j - 4 * qt) >= 0
                            att_step(h, KTo[:, j * 128:(j + 1) * 128], KTo_b, 128, q0, 512, lo, diag,
                                     biasO[qt][0][:, j, h:h + 1], biasO[qt][1], Vo[:, j, 0:129], Vo_b, obanks,
                                     j == 0, False, 128)
                        for t in range(64):
                            att_step(h, KTh[:, t // 8, (t % 8) * 128:(t % 8 + 1) * 128], KTh_b, 128, q0, 512, 0, False,
                                     biasG[qt][0][:, t, h:h + 1], biasG[qt][1], Vh[:, t, 0:129], Vh_b, obanks,
                                     False, t == 63, 128)
                        att_evac(h, obanks, q0, 4, 128)
                    for t in range(16):
                        att_step(h, KTs[:, t * 128:(t + 1) * 128], KTs_b, 128, 1024, 64, 0, False,
                                 biasS[:, t, h:h + 1], biasS_b, Vs[:, t, 0:129], Vs_b, obanks, t == 0, False, 64)
                    att_step(h, KTo[:, 1024:NTOK], KTo_b, 64, 1024, 64, 0, True, biasS[:64, 16, h:h + 1], biasS_b,
                             Vo[:64, 8, 0:129], Vo_b, obanks, False, True, 64)
                    att_evac(h, obanks, 1024, 1, 64)
                    for s in range(4):
                        for t, (c0, n) in enumerate(TILES):
                            ps, ps_b = bank(4, 6)
                            mm(ps[:n, :], ps_b, OTh[:, c0:c0 + n], woh[:, s * 512:(s + 1) * 512], True, True,
                               [OTh_b, woh_b])
                            add_into_h(s)(t, c0, n, ps, ps_b)
            S.barrier()

        if stage >= 5:
            ffn(1)

        for t, (c0, n) in enumerate(TILES):
            S.dma("sync", [(y_all[c0:c0 + n, :], hh[:n, t, :])], reads=[hb[t]])
        S.barrier()
        block = es.enter_context(nc.Block())
        S.emit(block)
    return nc


def _consts(c):
    cfm = np.zeros((128, NCONST), np.float32)
    cfm[:, C_IDENT:C_IDENT + 128] = np.eye(128, dtype=np.float32)
    tri = np.triu(np.ones((128, 128), np.float32))
    cfm[:, C_TRI:C_TRI + 128] = tri
    rm = np.ones(NTOK, np.float32)
    rm[0:1024:128] = 0.0
    rm[1024] = 0.0
    cfm[:, C_RESET:C_RESET + NTOK] = rm[None, :]
    cfm[:, C_CMASK:C_CMASK + 8] = (np.arange(8) < c).astype(np.float32)[None, :]
    rb = np.where((np.arange(64) // 8) < c, 0.0, NEG).astype(np.float32)
    cfm[:, C_RANKB:C_RANKB + 64] = rb[None, :]
    for qt in range(2):
        w = (np.arange(64) < 8 * c + 4 * (qt + 1)).astype(np.float32)
        cfm[:, C_WQ + 64 * qt:C_WQ + 64 * (qt + 1)] = w[None, :]
    for j in range(8):
        sel = (np.arange(64) == 8 * c + j).astype(np.float32)
        cfm[:, C_SEL + 64 * j:C_SEL + 64 * (j + 1)] = sel[None, :]
    cfm[:, C_ONES:C_ONES + 128] = 1.0
    cbm = np.zeros((128, 256), np.float32)
    cbm[:, 0:128] = np.eye(128, dtype=np.float32)
    cbm[:, 128:256] = (tri - 1.0) * (-NEG)
    return cfm, cbm.astype(ml_dtypes.bfloat16)


_NC_CACHE = {}


def kernel(x_prompt, x_sample, state_gla, cache_k, cache_v, cache_logf, g_mix, g_ffn,
           gla_w_in, gla_w_gk2, gla_b_gk, gla_g_out, gla_w_out, kv_g, kv_w, kv_b_f, kv_g_k,
           fox_w_q, fox_g_q, fox_w_o, ffn_w_in, ffn_w_out, _stage=99):
    f = lambda a: np.ascontiguousarray(np.asarray(a, dtype=np.float32))
    x_prompt, x_sample, state_gla = f(x_prompt), f(x_sample), f(state_gla)
    cache_k, cache_v, cache_logf = f(cache_k), f(cache_v), f(cache_logf)
    if _stage not in _NC_CACHE:
        _NC_CACHE[_stage] = build(_stage)
    nc = _NC_CACHE[_stage]

    def fm(v):
        return f(v).reshape(16, 128).T

    gains = np.concatenate([fm(g_mix[0]), fm(g_ffn[0]), fm(kv_g), fm(g_mix[1]), fm(g_ffn[1])], axis=1)
    shared = {
        "gla_w_in": f(gla_w_in[0]), "gla_w_gk2": f(gla_w_gk2[0]), "gla_w_out": f(gla_w_out[0]),
        "kv_w": f(kv_w), "fox_w_q": f(fox_w_q[0]), "fox_w_o": f(fox_w_o[0]),
        "ffn_w_in": f(ffn_w_in), "ffn_w_out": f(ffn_w_out),
    }
    in_maps = []
    for c in range(NCORES):
        cfm, cbm = _consts(c)
        cfm[:, C_GAINS:C_GAINS + 80] = gains
        cfm[:, C_BGK:C_BGK + 8] = f(gla_b_gk[0]).reshape(8, 128).T
        cfm[:, C_GOUT:C_GOUT + 512] = f(gla_g_out[0])[None, :]
        cfm[:, C_GK:C_GK + 128] = f(kv_g_k)[None, :]
        cfm[:, C_GQ:C_GQ + 128] = f(fox_g_q[0])[None, :]
        cfm[:, C_BF:C_BF + 16] = f(kv_b_f)[None, :]
        m = dict(shared)
        m["x_all"] = np.ascontiguousarray(np.concatenate([x_prompt[0, c * 1024:(c + 1) * 1024], x_sample[c]], axis=0))
        m["st_s"] = np.ascontiguousarray(state_gla[0, c])
        m["ck_c"] = np.ascontiguousarray(cache_k[c].reshape(2048, 2048))
        m["cv_c"] = np.ascontiguousarray(cache_v[c].reshape(2048, 2048))
        m["cl_c"] = np.ascontiguousarray(cache_logf[c])
        m["constf"] = cfm
        m["constb"] = cbm
        in_maps.append(m)
    res = run_bass_kernel_spmd(nc, in_maps, core_ids=list(range(NCORES)))
    R = res.results
    cat = lambda k, lo, hi: np.concatenate([np.asarray(R[c][k], np.float32)[lo:hi] for c in range(NCORES)], axis=0)
    y_p = cat("y_all", 0, 1024)[None]
    y_s = np.stack([np.asarray(R[c]["y_all"], np.float32)[1024:NTOK] for c in range(NCORES)])
    sgp = np.asarray(R[NCORES - 1]["sg_p"], np.float32)[None, None]
    k_p = cat("k_o", 0, 1024).reshape(1, 8192, 16, 128)
    v_p = cat("v_o", 0, 1024).reshape(1, 8192, 16, 128)
    lf_p = cat("lf_o", 0, 1024).reshape(1, 8192, 16)
    sgs = np.stack([np.asarray(R[c]["sg_s"], np.float32) for c in range(NCORES)])[None]
    k_s = np.stack([np.asarray(R[c]["k_o"], np.float32)[1024:NTOK] for c in range(NCORES)]).reshape(8, 64, 16, 128)
    v_s = np.stack([np.asarray(R[c]["v_o"], np.float32)[1024:NTOK] for c in range(NCORES)]).reshape(8, 64, 16, 128)
    lf_s = np.stack([np.asarray(R[c]["lf_o"], np.float32)[1024:NTOK] for c in range(NCORES)])
    return (y_p, y_s, sgp, k_p, v_p, lf_p, sgs, k_s, v_s, lf_s)
```

```python
import numpy as np
import ml_dtypes
from contextlib import ExitStack
import concourse.bass as bass
import concourse.mybir as mybir
from concourse.bass_utils import run_bass_kernel_spmd

F32 = mybir.dt.float32
BF16 = mybir.dt.bfloat16
ALU = mybir.AluOpType
AF = mybir.ActivationFunctionType
AX = mybir.AxisListType

NCORES = 8
D = 2048
KC = 16
NTOK = 1088
TILES = [(t * 128, 128) for t in range(8)] + [(1024, 64)]
GROUPS = [(0, 512), (512, 512), (1024, 64)]
DFF = 5632
EPS = 1e-6
NEG = -30000.0
GLA_IN = 6160
STRICT = True

C_IDENT = 0
C_TRI = 128
C_RESET = 256
C_GAINS = C_RESET + NTOK
C_BGK = C_GAINS + 80
C_GOUT = C_BGK + 8
C_GK = C_GOUT + 512
C_GQ = C_GK + 128
C_BF = C_GQ + 128
C_CMASK = C_BF + 16
C_RANKB = C_CMASK + 8
C_WQ = C_RANKB + 64
C_SEL = C_WQ + 128
C_ONES = C_SEL + 512
NCONST = C_ONES + 128


class Buf:
    __slots__ = ("name", "lw", "rd")

    def __init__(self, name):
        self.name = name
        self.lw = []
        self.rd = {}


class Sched:
    ENG = ("tensor", "vector", "scalar", "gpsimd", "sync")
    DQ = ("sync", "gpsimd")
    R = 8

    def __init__(self, nc, es):
        self.nc = nc
        self.streams = {e: [] for e in self.ENG}
        self.sem = {}
        for e in ("tensor", "vector", "scalar", "gpsimd"):
            self.sem[e] = es.enter_context(nc.semaphore("s_" + e))
        for q in self.DQ:
            for s in range(self.R):
                self.sem[(q, s)] = es.enter_context(nc.semaphore("d_%s%d" % (q, s)))
        self.cnt = {e: 0 for e in self.ENG}
        self.seen = {e: {} for e in self.ENG}
        self.dn = {q: 0 for q in self.DQ}
        self.last = {}
        self.ncc = 0
        self.es = es
        self.ccb = Buf("ccchain")

    def _wait(self, eng, ev):
        k, v = ev
        if self.seen[eng].get(k, 0) >= v:
            return
        self.seen[eng][k] = v
        self.streams[eng].append(("wait", self.sem[k], v))

    def _deps(self, eng, reads, writes):
        for b in reads:
            for ev in b.lw:
                if ev[0] == eng and eng == "tensor":
                    continue
                self._wait(eng, ev)
        strict = STRICT and eng != "tensor"
        for b in writes:
            for ev in b.lw:
                if ev[0] != eng or strict:
                    self._wait(eng, ev)
            for k, v in b.rd.items():
                if k != eng or strict:
                    self._wait(eng, (k, v))

    def _mark(self, evs, reads, writes):
        for b in reads:
            for ev in evs:
                b.rd[ev[0]] = max(b.rd.get(ev[0], 0), ev[1])
        for b in writes:
            b.lw = list(evs)
            b.rd = {}

    def op(self, eng, meth, reads=(), writes=(), **kw):
        self._deps(eng, reads, writes)
        self.cnt[eng] += 1
        ev = (eng, self.cnt[eng])
        self.last[eng] = self.cnt[eng]
        self.streams[eng].append(("op", meth, kw, self.sem[eng], 1))
        self._mark([ev], reads, writes)
        return ev

    def dma(self, q, pairs, reads=(), writes=(), acc=()):
        self._deps(q, reads, writes)
        evs = []
        for (o, i) in pairs:
            n = self.dn[q]
            self.dn[q] += 1
            slot, rnd = n % self.R, n // self.R
            key = (q, slot)
            if rnd > 0:
                self._wait(q, (key, 16 * rnd))
            self.streams[q].append(("op", "dma_start", dict(out=o, in_=i), self.sem[key], 16))
            ev = (key, 16 * (rnd + 1))
            self.last[key] = ev[1]
            evs.append(ev)
        self._mark(evs, reads, writes)
        for b in acc:
            b.lw = b.lw + list(evs)
        return evs

    def collective(self, in_ap, out_ap, reads=(), writes=()):
        q = "gpsimd"
        reads = list(reads) + [self.ccb]
        writes = list(writes) + [self.ccb]
        self._deps(q, reads, writes)
        sem = self.es.enter_context(self.nc.semaphore("cc%d" % self.ncc))
        key = ("cc", self.ncc)
        self.ncc += 1
        self.sem[key] = sem
        self.streams[q].append(("cc", in_ap, out_ap, sem))
        ev = (key, 1)
        self.last[key] = 1
        self._mark([ev], reads, writes)

    def barrier(self):
        for e in self.ENG:
            for k, v in list(self.last.items()):
                if k != e:
                    self._wait(e, (k, v))

    def emit(self, block):
        nc = self.nc

        def run(e, stream):
            for it in stream:
                if it[0] == "wait":
                    e.wait_ge(it[1], it[2])
                elif it[0] == "op":
                    getattr(e, it[1])(**it[2]).then_inc(it[3], it[4])
                else:
                    e.collective_compute(
                        "AllGather", ALU.bypass, replica_groups=[list(range(NCORES))],
                        ins=[it[1]], outs=[it[2]]).then_inc(it[3])

        @block.tensor
        def _(e):
            run(e, self.streams["tensor"])

        @block.vector
        def _(e):
            run(e, self.streams["vector"])

        @block.scalar
        def _(e):
            run(e, self.streams["scalar"])

        @block.gpsimd
        def _(e):
            run(e, self.streams["gpsimd"])

        @block.sync
        def _(e):
            run(e, self.streams["sync"])


class _Skip(Exception):
    pass


def build(stage=99, skip=()):
    nc = bass.Bass("TRN2", target_bir_lowering=False)

    def din(name, shape, dt=F32):
        return nc.dram_tensor(name, list(shape), dt, kind="ExternalInput").ap()

    def dout(name, shape, dt=F32):
        return nc.dram_tensor(name, list(shape), dt, kind="ExternalOutput").ap()

    def dint(name, shape, dt=F32):
        return nc.dram_tensor(name, list(shape), dt)

    x_all = din("x_all", [NTOK, D])
    st_s = din("st_s", [4, 256, 512])
    ck_c = din("ck_c", [2048, 2048])
    cv_c = din("cv_c", [2048, 2048])
    cl_c = din("cl_c", [2048, 16])
    constf = din("constf", [128, NCONST])
    constb = din("constb", [128, 256], BF16)
    gla_w_in = din("gla_w_in", [D, GLA_IN])
    gla_w_gk2 = din("gla_w_gk2", [16, 1024])
    gla_w_out = din("gla_w_out", [D, D])
    kv_w = din("kv_w", [D, 4112])
    fox_w_q = din("fox_w_q", [D, D])
    fox_w_o = din("fox_w_o", [D, D])
    ffn_w_in = din("ffn_w_in", [2, D, 2 * DFF])
    ffn_w_out = din("ffn_w_out", [2, DFF, D])

    y_all = dout("y_all", [NTOK, D])
    k_o = dout("k_o", [NTOK, D])
    v_o = dout("v_o", [NTOK, D])
    lf_o = dout("lf_o", [NTOK, 16])
    sg_p = dout("sg_p", [4, 256, 512])
    sg_s = dout("sg_s", [4, 256, 512])

    xg_in = [dint("xg_in%d" % h, [128, 1026]) for h in range(4)]
    xg_out = [dint("xg_out%d" % h, [1024, 1026]) for h in range(4)]
    kt_in = dint("kt_in", [2048, 1024], BF16)
    kt_out = dint("kt_out", [8 * 2048, 1024], BF16)
    kts_in = dint("kts_in", [2048, 64], BF16)
    v_in = dint("v_in", [1024, 2048], BF16)
    vs_in = dint("vs_in", [64, 2048], BF16)
    v_out = dint("v_out", [8192, 2048], BF16)
    lf_in = dint("lf_in", [1024, 16])
    lfs_in = dint("lfs_in", [64, 16])
    lf_out = dint("lf_out", [8192, 16])

    es = ExitStack()
    with es:
        S = Sched(nc, es)

        sbn = {"i": 0}

        def sb(name, shape, dt, scope=None):
            sbn["i"] += 1
            t = (scope or es).enter_context(nc.sbuf_tensor("%s_%d" % (name, sbn["i"]), list(shape), dt))
            return t, Buf(name)

        cf, cf_b = sb("cf", [128, NCONST], F32)
        cb, cb_b = sb("cb", [128, 256], BF16)
        uT, uT_b = sb("uT", [128, KC, NTOK], BF16)
        NW = 2
        wp = [sb("wp%d" % i, [128, KC, 512], BF16) for i in range(NW)]
        aux, aux_b = sb("aux", [128, KC * NTOK], BF16)
        oT, oT_b = aux[:, :].rearrange("p (c t) -> p c t", c=KC), aux_b
        wstate = {"i": 0}
        ss, ss_b = sb("ss", [128, 8], F32)
        rstd, rstd_b = sb("rstd", [128, 8], F32)

        ident_f = cf[:, C_IDENT:C_IDENT + 128]
        tri_f = cf[:, C_TRI:C_TRI + 128]
        resetm = cf[:, C_RESET:C_RESET + NTOK]
        ones_f = cf[:, C_ONES:C_ONES + 128]
        ident_b = cb[:, 0:128]
        negm_b = cb[:, 128:256]

        def gain(i, c):
            return cf[:, C_GAINS + 16 * i + c:C_GAINS + 16 * i + c + 1]

        pbank = []
        for i in range(6):
            t = es.enter_context(nc.psum_tensor("pb%d" % i, [128, 512], F32))
            pbank.append((t, Buf("pb%d" % i)))
        tbank = []
        for i in range(2):
            t = es.enter_context(nc.psum_tensor("tb%d" % i, [128, 1024], BF16))
            tbank.append((t, Buf("tb%d" % i)))
        pstate = {"i": 0, "t": 0}

        def bank(lo=0, hi=6):
            i = pstate["i"]
            if i < lo or i >= hi:
                i = lo
            pstate["i"] = i + 1 if i + 1 < hi else lo
            return pbank[i]

        def tb():
            i = pstate["t"]
            pstate["t"] = 1 - i
            return tbank[i]

        S.dma("sync", [(cf[:, :], constf[:, :])], writes=[cf_b])
        S.dma("sync", [(cb[:, :], constb[:, :])], writes=[cb_b])

        def mm(ps, ps_b, lhsT, rhs, start, stop, reads):
            S.op("tensor", "matmul", reads=reads, writes=[ps_b], out=ps, lhsT=lhsT, rhs=rhs,
                 start=start, stop=stop)

        def tr(ps, ps_b, in_, ident, reads):
            S.op("tensor", "transpose", reads=reads, writes=[ps_b], out=ps, in_=in_, identity=ident)

        def load_slab(w2d, specs, kc=KC, q="gpsimd"):
            i = wstate["i"]
            wstate["i"] = (i + 1) % NW
            t, b = wp[i]
            wv = w2d.rearrange("(kc p) c -> p kc c", p=128)
            pairs = []
            hk = max(kc // 2, 1)
            for (c0, n, d0) in specs:
                for k0 in range(0, kc, hk):
                    pairs.append((t[:, k0:k0 + hk, d0:d0 + n], wv[:, k0:k0 + hk, c0:c0 + n]))
            S.dma(q, pairs, writes=[b])
            return t, b

        def rstd_from_ss(n, ncol, denom):
            S.op("vector", "tensor_scalar", reads=[ss_b], writes=[rstd_b], out=rstd[:n, :ncol], in0=ss[:n, :ncol],
                 scalar1=1.0 / denom, scalar2=EPS, op0=ALU.mult, op1=ALU.add)
            S.op("scalar", "activation", reads=[rstd_b], writes=[rstd_b], out=rstd[:n, :ncol], in_=rstd[:n, :ncol],
                 func=AF.Ln)
            S.op("scalar", "activation", reads=[rstd_b], writes=[rstd_b], out=rstd[:n, :ncol], in_=rstd[:n, :ncol],
                 func=AF.Exp, scale=-0.5)

        def norm_phase(get_tile, gain_idx):
            with ExitStack() as sc:
                xn, xn_b = sb("xn", [128, D], BF16, sc)
                junk, junk_b = sb("junk", [128, D], BF16, sc)
                for t, (c0, n) in enumerate(TILES):
                    src, src_b = get_tile(t, n)
                    S.op("scalar", "activation", reads=[src_b], writes=[junk_b, ss_b], out=junk[:n, :], in_=src,
                         func=AF.Square, accum_out=ss[:n, 0:1])
                    rstd_from_ss(n, 1, float(D))
                    S.op("vector", "tensor_scalar", reads=[src_b, rstd_b], writes=[xn_b], out=xn[:n, :], in0=src,
                         scalar1=rstd[:n, 0:1], scalar2=None, op0=ALU.mult)
                    for half in range(2):
                        tp, tp_b = tb()
                        for j in range(8):
                            c = half * 8 + j
                            tr(tp[:, j * 128:j * 128 + n], tp_b, xn[:n, c * 128:(c + 1) * 128], ident_b[:n, :n],
                               [xn_b, cb_b])
                        for j in range(8):
                            c = half * 8 + j
                            eng = "scalar" if half == 0 else "vector"
                            if eng == "scalar":
                                S.op("scalar", "mul", reads=[tp_b, cf_b], writes=[uT_b], out=uT[:, c, c0:c0 + n],
                                     in_=tp[:, j * 128:j * 128 + n], mul=gain(gain_idx, c))
                            else:
                                S.op("vector", "tensor_scalar", reads=[tp_b, cf_b], writes=[uT_b],
                                     out=uT[:, c, c0:c0 + n], in0=tp[:, j * 128:j * 128 + n],
                                     scalar1=gain(gain_idx, c), scalar2=None, op0=ALU.mult)
            S.barrier()

        def mm_tok(act, act_b, slab, slab_b, ncols, evac, kcn=KC, col0=0, tiles=None):
            for t, (c0, n) in enumerate(TILES):
                if tiles is not None and t not in tiles:
                    continue
                ps, ps_b = bank()
                for kc in range(kcn):
                    mm(ps[:n, :ncols], ps_b, act[:, kc, c0:c0 + n], slab[:, kc, col0:col0 + ncols], kc == 0,
                       kc == kcn - 1, [act_b, slab_b])
                evac(t, c0, n, ps, ps_b)

        def mm_feat(act, act_b, slab, slab_b, col0, m, evac, kcn=KC):
            for (g0, gn) in GROUPS:
                ps, ps_b = bank()
                for kc in range(kcn):
                    mm(ps[:m, :gn], ps_b, slab[:, kc, col0:col0 + m], act[:, kc, g0:g0 + gn], kc == 0, kc == kcn - 1,
                       [act_b, slab_b])
                evac(g0, gn, ps, ps_b)


        with ExitStack() as sc0:
            xs = [sb("xs%d" % i, [128, D], F32, sc0) for i in range(2)]

            def get_x(t, n):
                xt, xb = xs[t % 2]
                c0 = TILES[t][0]
                S.dma("sync", [(xt[:n, :], x_all[c0:c0 + n, :])], writes=[xb])
                return xt[:n, :], xb

            norm_phase(get_x, 0)
        S.barrier()

        gla_sc = ExitStack()
        try:
          with gla_sc:
              if "gla" in skip:
                  raise _Skip()
              gklT, gklT_b = sb("gklT", [16, NTOK], BF16, gla_sc)
              wgk2, wgk2_b = sb("wgk2", [16, 1024], BF16, gla_sc)
              negb, negb_b = sb("negb", [128, 8], F32, gla_sc)
              qtT, qtT_b = sb("qtT", [128, 2, NTOK], BF16, gla_sc)
              ktT, ktT_b = sb("ktT", [128, 2, NTOK], BF16, gla_sc)
              khT, khT_b = sb("khT", [128, 2, NTOK], BF16, gla_sc)
              khat, khat_b = sb("khat", [128, 9, 256], BF16, gla_sc)
              v_sb, v_sbb = sb("v_sb", [128, 9, 512], BF16, gla_sc)
              g_sb, g_sbb = sb("g_sb", [128, 9, 512], BF16, gla_sc)
              eb = [sb("eb%d" % i, [128, NTOK], F32, gla_sc) for i in range(2)]
              tA, tA_b = sb("tA", [128, NTOK], F32, gla_sc)
              tB, tB_b = sb("tB", [128, NTOK], F32, gla_sc)
              tC, tC_b = sb("tC", [128, NTOK], F32, gla_sc)
              xsb, xsb_b = sb("xsb", [128, 1026], F32, gla_sc)
              xr = [sb("xr%d" % i, [128, 1026], F32, gla_sc) for i in range(2)]
              sin_f = [sb("sinf%d" % i, [128, 512], F32, gla_sc) for i in range(2)]
              s_bf = [sb("sbf%d" % i, [128, 512], BF16, gla_sc) for i in range(2)]
              ss_f = [sb("ssf%d" % i, [128, 512], F32, gla_sc) for i in range(2)]
              attT, attT_b = sb("attT", [128, 128], BF16, gla_sc)
              otmp, otmp_b = sb("otmp", [128, 512], F32, gla_sc)
              og, og_b = sb("og", [128, 512], BF16, gla_sc)
              junk2, junk2_b = sb("junk2", [128, 512], BF16, gla_sc)
              dm, dm_b = sb("dm", [128, 2], F32, gla_sc)

              S.dma("gpsimd", [(wgk2[:, :], gla_w_gk2[:, :])], writes=[wgk2_b])
              S.op("vector", "tensor_scalar", reads=[cf_b], writes=[negb_b], out=negb[:, :],
                   in0=cf[:, C_BGK:C_BGK + 8], scalar1=-1.0, scalar2=None, op0=ALU.mult)

              slab, slab_b = load_slab(gla_w_in, [(6144, 16, 0)])

              def ev_gkl(g0, gn, ps, ps_b):
                  S.op("scalar", "copy", reads=[ps_b], writes=[gklT_b], out=gklT[:, g0:g0 + gn], in_=ps[:16, :gn])

              mm_feat(uT, uT_b, slab, slab_b, 0, 16, ev_gkl)

              xs_f = [(xsb[:, 0:512], xsb_b), (xsb[:, 512:1024], xsb_b)]

              def gla_chunk(h, t, c0, n, Sf, Sb, with_out, first):
                  if with_out:
                      pa, pa_b = bank()
                      for dc in range(2):
                          mm(pa[:n, :n], pa_b, ktT[:, dc, c0:c0 + n], qtT[:, dc, c0:c0 + n], dc == 0, dc == 1,
                             [ktT_b, qtT_b])
                      S.op("vector", "tensor_tensor", reads=[pa_b, cf_b], writes=[attT_b], out=attT[:n, :n],
                           in0=pa[:n, :n], in1=tri_f[:n, :n], op=ALU.mult)
                      po, po_b = bank()
                      mm(po[:n, :], po_b, attT[:n, :n], v_sb[:n, t, :], True, False, [attT_b, v_sbb])
                      for dc in range(2):
                          mm(po[:n, :], po_b, qtT[:, dc, c0:c0 + n], Sb[dc][0][:, :], False, dc == 1,
                             [qtT_b, Sb[dc][1]])
                      S.op("scalar", "activation", reads=[po_b], writes=[junk2_b, ss_b], out=junk2[:n, :],
                           in_=po[:n, :], func=AF.Square, accum_out=ss[:n, 0:1])
                      rstd_from_ss(n, 1, 512.0)
                      S.op("vector", "scalar_tensor_tensor", reads=[po_b, rstd_b, cf_b], writes=[otmp_b],
                           out=otmp[:n, :], in0=po[:n, :], scalar=rstd[:n, 0:1], in1=cf[:n, C_GOUT:C_GOUT + 512],
                           op0=ALU.mult, op1=ALU.mult)
                      S.op("vector", "tensor_tensor", reads=[otmp_b, g_sbb], writes=[og_b], out=og[:n, :],
                           in0=otmp[:n, :], in1=g_sb[:n, t, :], op=ALU.mult)
                      tp, tp_b = tb()
                      for j in range(4):
                          tr(tp[:, j * 128:j * 128 + n], tp_b, og[:n, j * 128:(j + 1) * 128], ident_b[:n, :n],
                             [og_b, cb_b])
                      for j in range(4):
                          S.op("scalar", "copy", reads=[tp_b], writes=[oT_b], out=oT[:, 4 * h + j, c0:c0 + n],
                               in_=tp[:, j * 128:j * 128 + n])
                  for dc in range(2):
                      pd, pd_b = bank()
                      mm(pd[:, :], pd_b, khat[:n, t, dc * 128:(dc + 1) * 128], v_sb[:n, t, :], True, True,
                         [khat_b, v_sbb])
                      sf, sf_b = Sf[dc]
                      if first:
                          S.op("vector", "tensor_copy", reads=[pd_b], writes=[sf_b], out=sf, in_=pd[:, :])
                      else:
                          S.op("vector", "scalar_tensor_tensor", reads=[pd_b, sf_b, eb[dc][1]], writes=[sf_b],
                               out=sf, in0=sf, scalar=eb[dc][0][:, c0 + n - 1:c0 + n], in1=pd[:, :],
                               op0=ALU.mult, op1=ALU.add)
                      if with_out:
                          S.op("scalar", "copy", reads=[sf_b], writes=[Sb[dc][1]], out=Sb[dc][0][:, :], in_=sf)

              for h in range(4):
                  slab, slab_b = load_slab(gla_w_in, [(h * 256, 256, 0), (1024 + h * 256, 256, 256)])
                  for dc in range(2):
                      fc = 2 * h + dc
                      ebt, ebb = eb[dc]
                      for (g0, gn) in GROUPS:
                          ps, ps_b = bank()
                          mm(ps[:, :gn], ps_b, wgk2[:, fc * 128:(fc + 1) * 128], gklT[:, g0:g0 + gn], True, True,
                             [wgk2_b, gklT_b])
                          S.op("scalar", "activation", reads=[ps_b, negb_b], writes=[tA_b], out=tA[:, g0:g0 + gn],
                               in_=ps[:, :gn], func=AF.Exp, scale=-1.0, bias=negb[:, fc:fc + 1])
                      S.op("scalar", "activation", reads=[tA_b], writes=[tB_b], out=tB[:, :], in_=tA[:, :],
                           func=AF.Ln, bias=1.0)
                      S.op("vector", "tensor_tensor_scan", reads=[tB_b, cf_b], writes=[tC_b], out=tC[:, :],
                           data0=resetm, data1=tB[:, :], initial=0.0, op0=ALU.mult, op1=ALU.add)
                      S.op("scalar", "activation", reads=[tC_b], writes=[ebb], out=ebt[:, :], in_=tC[:, :],
                           func=AF.Exp, scale=-1.0 / 16.0)
                      S.op("scalar", "activation", reads=[tC_b], writes=[tA_b], out=tA[:, :], in_=tC[:, :],
                           func=AF.Exp, scale=1.0 / 16.0)

                      def ev_q(g0, gn, ps, ps_b, dc=dc, ebt=ebt, ebb=ebb):
                          S.op("vector", "scalar_tensor_tensor", reads=[ps_b, ebb], writes=[qtT_b],
                               out=qtT[:, dc, g0:g0 + gn], in0=ps[:, :gn], scalar=0.0625, in1=ebt[:, g0:g0 + gn],
                               op0=ALU.mult, op1=ALU.mult)

                      def ev_k(g0, gn, ps, ps_b, dc=dc, ebt=ebt, ebb=ebb):
                          S.op("vector", "tensor_tensor", reads=[ps_b, tA_b], writes=[ktT_b],
                               out=ktT[:, dc, g0:g0 + gn], in0=ps[:, :gn], in1=tA[:, g0:g0 + gn], op=ALU.mult)
                          for (c0, n) in TILES:
                              if c0 < g0 or c0 >= g0 + gn:
                                  continue
                              S.op("vector", "scalar_tensor_tensor", reads=[ps_b, tA_b, ebb], writes=[khT_b],
                                   out=khT[:, dc, c0:c0 + n], in0=ps[:, c0 - g0:c0 - g0 + n],
                                   scalar=ebt[:, c0 + n - 1:c0 + n], in1=tA[:, c0:c0 + n], op0=ALU.mult, op1=ALU.mult)

                      mm_feat(uT, uT_b, slab, slab_b, dc * 128, 128, ev_q)
                      mm_feat(uT, uT_b, slab, slab_b, 256 + dc * 128, 128, ev_k)
                  for t, (c0, n) in enumerate(TILES):
                      tp, tp_b = tb()
                      for dc in range(2):
                          tr(tp[:n, dc * 128:(dc + 1) * 128], tp_b, khT[:, dc, c0:c0 + n], ident_b[:, :],
                             [khT_b, cb_b])
                      S.op("scalar", "copy", reads=[tp_b], writes=[khat_b], out=khat[:n, t, :], in_=tp[:n, 0:256])
                  slab, slab_b = load_slab(gla_w_in, [(2048 + h * 512, 512, 0)])

                  def ev_v(t, c0, n, ps, ps_b):
                      S.op("scalar", "copy", reads=[ps_b], writes=[v_sbb], out=v_sb[:n, t, :], in_=ps[:n, :])

                  mm_tok(uT, uT_b, slab, slab_b, 512, ev_v)
                  slab, slab_b = load_slab(gla_w_in, [(4096 + h * 512, 512, 0)])

                  def ev_g(t, c0, n, ps, ps_b):
                      S.op("scalar", "activation", reads=[ps_b], writes=[otmp_b], out=otmp[:n, :], in_=ps[:n, :],
                           func=AF.Exp, scale=-1.0)
                      S.op("vector", "tensor_scalar", reads=[otmp_b], writes=[otmp_b], out=otmp[:n, :],
                           in0=otmp[:n, :], scalar1=1.0, scalar2=None, op0=ALU.add)
                      S.op("vector", "reciprocal", reads=[otmp_b], writes=[otmp_b], out=otmp[:n, :], in_=otmp[:n, :])
                      S.op("vector", "tensor_tensor", reads=[otmp_b, ps_b], writes=[g_sbb], out=g_sb[:n, t, :],
                           in0=otmp[:n, :], in1=ps[:n, :], op=ALU.mult)

                  mm_tok(uT, uT_b, slab, slab_b, 512, ev_g)

                  for t in range(8):
                      c0, n = TILES[t]
                      gla_chunk(h, t, c0, n, xs_f, None, False, t == 0)
                  for dc in range(2):
                      ebt, ebb = eb[dc]
                      S.op("vector", "tensor_copy", reads=[ebb], writes=[xsb_b], out=xsb[:, 1024 + dc:1025 + dc],
                           in_=ebt[:, 127:128])
                      for t in range(1, 8):
                          S.op("vector", "tensor_tensor", reads=[ebb, xsb_b], writes=[xsb_b],
                               out=xsb[:, 1024 + dc:1025 + dc], in0=xsb[:, 1024 + dc:1025 + dc],
                               in1=ebt[:, t * 128 + 127:t * 128 + 128], op=ALU.mult)
                  xin_b, xout_b = Buf("xin"), Buf("xout")
                  S.dma("sync", [(xg_in[h].ap()[:, :], xsb[:, :])], reads=[xsb_b], writes=[xin_b])
                  S.collective(xg_in[h].ap().opt(), xg_out[h].ap().opt(), reads=[xin_b], writes=[xout_b])
                  for j in range(7):
                      xt, xb = xr[j % 2]
                      S.dma("sync", [(xt[:, :], xg_out[h].ap()[j * 128:(j + 1) * 128, :])], reads=[xout_b], writes=[xb])
                      mj = cf[:, C_CMASK + j:C_CMASK + j + 1]
                      S.op("vector", "tensor_scalar", reads=[xb, cf_b], writes=[dm_b], out=dm[:, :],
                           in0=xt[:, 1024:1026], scalar1=-1.0, scalar2=mj, op0=ALU.add, op1=ALU.mult)
                      S.op("vector", "tensor_scalar", reads=[dm_b], writes=[dm_b], out=dm[:, :], in0=dm[:, :],
                           scalar1=1.0, scalar2=None, op0=ALU.add)
                      for dc in range(2):
                          sf, sf_b = sin_f[dc]
                          if j == 0:
                              S.op("vector", "tensor_scalar", reads=[xb, cf_b], writes=[sf_b], out=sf[:, :],
                                   in0=xt[:, dc * 512:(dc + 1) * 512], scalar1=mj, scalar2=None, op0=ALU.mult)
                          else:
                              S.op("vector", "tensor_scalar", reads=[xb, cf_b], writes=[otmp_b], out=otmp[:, :],
                                   in0=xt[:, dc * 512:(dc + 1) * 512], scalar1=mj, scalar2=None, op0=ALU.mult)
                              S.op("vector", "scalar_tensor_tensor", reads=[sf_b, dm_b, otmp_b], writes=[sf_b],
                                   out=sf[:, :], in0=sf[:, :], scalar=dm[:, dc:dc + 1], in1=otmp[:, :],
                                   op0=ALU.mult, op1=ALU.add)
                  for dc in range(2):
                      S.op("scalar", "copy", reads=[sin_f[dc][1]], writes=[s_bf[dc][1]], out=s_bf[dc][0][:, :],
                           in_=sin_f[dc][0][:, :])
                  sinf_v = [(sin_f[dc][0][:, :], sin_f[dc][1]) for dc in range(2)]
                  for t in range(8):
                      c0, n = TILES[t]
                      gla_chunk(h, t, c0, n, sinf_v, s_bf, True, False)
                  S.dma("sync", [(sg_p[h, dc * 128:(dc + 1) * 128, :], sin_f[dc][0][:, :]) for dc in range(2)],
                        reads=[sin_f[0][1], sin_f[1][1]])
                  S.dma("sync", [(ss_f[dc][0][:, :], st_s[h, dc * 128:(dc + 1) * 128, :]) for dc in range(2)],
                        writes=[ss_f[0][1], ss_f[1][1]])
                  for dc in range(2):
                      S.op("scalar", "copy", reads=[ss_f[dc][1]], writes=[s_bf[dc][1]], out=s_bf[dc][0][:, :],
                           in_=ss_f[dc][0][:, :])
                  ssf_v = [(ss_f[dc][0][:, :], ss_f[dc][1]) for dc in range(2)]
                  gla_chunk(h, 8, 1024, 64, ssf_v, s_bf, True, False)
                  S.dma("sync", [(sg_s[h, dc * 128:(dc + 1) * 128, :], ss_f[dc][0][:, :]) for dc in range(2)],
                        reads=[ss_f[0][1], ss_f[1][1]])
              S.barrier()

        except _Skip:
            pass

        hh, hh_b = sb("hh", [128, 9, D], F32)
        hb = [Buf("h%d" % t) for t in range(9)]
        for t, (c0, n) in enumerate(TILES):
            S.dma("sync", [(hh[:n, t, :], x_all[c0:c0 + n, :])], writes=[hb[t]])

        def add_into_h(s):
            def ev(t, c0, n, ps, ps_b):
                S.op("vector", "tensor_tensor", reads=[ps_b, hb[t]], writes=[hb[t]],
                     out=hh[:n, t, s * 512:(s + 1) * 512], in0=ps[:n, :], in1=hh[:n, t, s * 512:(s + 1) * 512],
                     op=ALU.add)
            return ev

        if stage >= 1 and "gla" not in skip:
            for s in range(4):
                slab, slab_b = load_slab(gla_w_out, [(s * 512, 512, 0)])
                mm_tok(oT, oT_b, slab, slab_b, 512, add_into_h(s))
        S.barrier()

        def get_h(t, n):
            return hh[:n, t, :], hb[t]

        def ffn(layer):
            norm_phase(get_h, 1 + 3 * layer if layer == 0 else 4)
            w_in = ffn_w_in[layer]
            w_out = ffn_w_out[layer]
            with ExitStack() as sc:
                hT = [(aux[:, i * 4 * NTOK:(i + 1) * 4 * NTOK].rearrange("p (c t) -> p c t", c=4), Buf("hT%d" % i))
                      for i in range(2)]
                wo, wo_b = sb("wo", [128, 4, D], BF16, sc)
                sgt = [sb("sgt%d" % i, [128, 512], BF16, sc) for i in range(2)]
                for g in range(DFF // 512):
                    ht, ht_b = hT[g % 2]
                    for sub in range(2):
                        slab, slab_b = load_slab(w_in, [(g * 512 + sub * 256, 256, 0),
                                                        (DFF + g * 512 + sub * 256, 256, 256)])
                        for j in range(2):
                            for gi, (g0, gn) in enumerate(GROUPS):
                                pg, pg_b = bank()
                                pu, pu_b = bank()
                                for kc in range(KC):
                                    mm(pg[:, :gn], pg_b, slab[:, kc, j * 128:(j + 1) * 128], uT[:, kc, g0:g0 + gn],
                                       kc == 0, kc == KC - 1, [uT_b, slab_b])
                                for kc in range(KC):
                                    mm(pu[:, :gn], pu_b, slab[:, kc, 256 + j * 128:256 + (j + 1) * 128],
                                       uT[:, kc, g0:g0 + gn], kc == 0, kc == KC - 1, [uT_b, slab_b])
                                st, st_b = sgt[gi % 2]
                                S.op("scalar", "activation", reads=[pg_b], writes=[st_b], out=st[:, :gn],
                                     in_=pg[:, :gn], func=AF.Silu)
                                S.op("vector", "tensor_tensor", reads=[st_b, pu_b], writes=[ht_b],
                                     out=ht[:, sub * 2 + j, g0:g0 + gn], in0=st[:, :gn], in1=pu[:, :gn], op=ALU.mult)
                    wov = w_out[g * 512:(g + 1) * 512, :].rearrange("(kc p) c -> p kc c", p=128)
                    S.dma("gpsimd", [(wo[:, 0:2, :], wov[:, 0:2, :]), (wo[:, 2:4, :], wov[:, 2:4, :])], writes=[wo_b])
                    for s in range(4):
                        mm_tok(ht, ht_b, wo, wo_b, 512, add_into_h(s), kcn=4, col0=s * 512)
            S.barrier()

        if stage >= 2 and "ffn" not in skip:
            ffn(0)

        gk_bc = cf[:, C_GK:C_GK + 128]
        gq_bc = cf[:, C_GQ:C_GQ + 128]

        def headnorm_evac(dst, dst_b, gvec):
            def ev(t, c0, n, ps, ps_b, junk, junk_b):
                for j in range(4):
                    S.op("scalar", "activation", reads=[ps_b], writes=[junk_b, ss_b], out=junk[:n, 0:128],
                         in_=ps[:n, j * 128:(j + 1) * 128], func=AF.Square, accum_out=ss[:n, j:j + 1])
                rstd_from_ss(n, 4, 128.0)
                for j in range(4):
                    S.op("vector", "scalar_tensor_tensor", reads=[ps_b, rstd_b, cf_b], writes=[dst_b],
                         out=dst[:n, j * 128:(j + 1) * 128], in0=ps[:n, j * 128:(j + 1) * 128],
                         scalar=rstd[:n, j:j + 1], in1=gvec[:n, :], op0=ALU.mult, op1=ALU.mult)
            return ev

        if stage >= 3:
            norm_phase(get_h, 2)
            with ExitStack() as sc:
                kf = [sb("kf%d" % i, [128, 512], F32, sc) for i in range(2)]
                kb = [sb("kb%d" % i, [128, 512], BF16, sc) for i in range(2)]
                kst = [sb("kst%d" % i, [128, 4, 128], BF16, sc) for i in range(2)]
                junk, junk_b = sb("junk3", [128, 128], BF16, sc)
                lft, lft_b = sb("lft", [128, 16], F32, sc)
                lfo = [sb("lfo%d" % i, [128, 16], F32, sc) for i in range(2)]
                cnt = {"i": 0}
                ktin_b = Buf("ktin")
                for s in range(4):
                    slab, slab_b = load_slab(kv_w, [(s * 512, 512, 0)])

                    def ev_k(t, c0, n, ps, ps_b, s=s):
                        i = cnt["i"] % 2
                        cnt["i"] += 1
                        kft, kf_b = kf[i]
                        kbt, kb_b = kb[i]
                        kt, kt_b = kst[i]
                        headnorm_evac(kft, kf_b, gk_bc)(t, c0, n, ps, ps_b, junk, junk_b)
                        S.dma("sync", [(k_o[c0:c0 + n, s * 512:(s + 1) * 512], kft[:n, :])], reads=[kf_b])
                        S.op("scalar", "copy", reads=[kf_b], writes=[kb_b], out=kbt[:n, :], in_=kft[:n, :])
                        tp, tp_b = tb()
                        for j in range(4):
                            tr(tp[:, j * 128:j * 128 + n], tp_b, kbt[:n, j * 128:(j + 1) * 128], ident_b[:n, :n],
                               [kb_b, cb_b])
                        S.op("vector", "tensor_copy", reads=[tp_b], writes=[kt_b], out=kt[:, :, :n],
                             in_=tp[:, 0:512].rearrange("p (j c) -> p j c", j=4)[:, :, :n])
                        pairs = []
                        for j in range(4):
                            r0 = (4 * s + j) * 128
                            if t < 8:
                                pairs.append((kt_in.ap()[r0:r0 + 128, c0:c0 + n], kt[:, j, :n]))
                            else:
                                pairs.append((kts_in.ap()[r0:r0 + 128, 0:n], kt[:, j, :n]))
                        S.dma("sync", pairs, reads=[kt_b], acc=[ktin_b])

                    mm_tok(uT, uT_b, slab, slab_b, 512, ev_k)
                vin_b = Buf("vin")
                for s in range(4):
                    slab, slab_b = load_slab(kv_w, [(2048 + s * 512, 512, 0)])

                    def ev_v2(t, c0, n, ps, ps_b, s=s):
                        i = cnt["i"] % 2
                        cnt["i"] += 1
                        kft, kf_b = kf[i]
                        kbt, kb_b = kb[i]
                        S.op("scalar", "copy", reads=[ps_b], writes=[kf_b], out=kft[:n, :], in_=ps[:n, :])
                        S.dma("sync", [(v_o[c0:c0 + n, s * 512:(s + 1) * 512], kft[:n, :])], reads=[kf_b])
                        S.op("vector", "tensor_copy", reads=[kf_b], writes=[kb_b], out=kbt[:n, :], in_=kft[:n, :])
                        if t < 8:
                            dst = v_in.ap()[c0:c0 + n, s * 512:(s + 1) * 512]
                        else:
                            dst = vs_in.ap()[0:n, s * 512:(s + 1) * 512]
                        S.dma("sync", [(dst, kbt[:n, :])], reads=[kb_b], acc=[vin_b])

                    mm_tok(uT, uT_b, slab, slab_b, 512, ev_v2)
                slab, slab_b = load_slab(kv_w, [(4096, 16, 0)])
                lfin_b = Buf("lfin")

                def ev_f(t, c0, n, ps, ps_b):
                    i = cnt["i"] % 2
                    cnt["i"] += 1
                    lo, lo_b = lfo[i]
                    S.op("vector", "tensor_tensor", reads=[ps_b, cf_b], writes=[lft_b], out=lft[:n, :],
                         in0=ps[:n, 0:16], in1=cf[:n, C_BF:C_BF + 16], op=ALU.add)
                    S.op("scalar", "activation", reads=[lft_b], writes=[lft_b], out=lft[:n, :], in_=lft[:n, :],
                         func=AF.Exp, scale=-1.0)
                    S.op("scalar", "activation", reads=[lft_b], writes=[lft_b], out=lft[:n, :], in_=lft[:n, :],
                         func=AF.Ln, bias=1.0)
                    S.op("vector", "tensor_scalar", reads=[lft_b], writes=[lo_b], out=lo[:n, :], in0=lft[:n, :],
                         scalar1=-1.0, scalar2=None, op0=ALU.mult)
                    dst = lf_in.ap()[c0:c0 + n, :] if t < 8 else lfs_in.ap()[0:n, :]
                    S.dma("sync", [(lf_o[c0:c0 + n, :], lo[:n, :]), (dst, lo[:n, :])], reads=[lo_b], acc=[lfin_b])

                mm_tok(uT, uT_b, slab, slab_b, 16, ev_f)
                ktout_b, vout_b, lfout_b = Buf("ktout"), Buf("vout"), Buf("lfout")
                if stage >= 3.5:
                    S.collective(kt_in.ap().opt(), kt_out.ap().opt(), reads=[ktin_b], writes=[ktout_b])
                    S.collective(v_in.ap().opt(), v_out.ap().opt(), reads=[vin_b], writes=[vout_b])
                    S.collective(lf_in.ap().opt(), lf_out.ap().opt(), reads=[lfin_b], writes=[lfout_b])
            S.barrier()

        if stage >= 4:
            norm_phase(get_h, 3)
            fox_sc = ExitStack()
            with fox_sc:
                QT, QT_b = aux[:, :].rearrange("p (c t) -> p c t", c=KC), Buf("QT")
                with ExitStack() as sc:
                    qf = [sb("qf%d" % i, [128, 512], F32, sc) for i in range(2)]
                    qb = [sb("qb%d" % i, [128, 512], BF16, sc) for i in range(2)]
                    junk, junk_b = sb("junk4", [128, 128], BF16, sc)
                    cnt = {"i": 0}
                    for s in range(4):
                        slab, slab_b = load_slab(fox_w_q, [(s * 512, 512, 0)])

                        def ev_q2(t, c0, n, ps, ps_b, s=s):
                            i = cnt["i"] % 2
                            cnt["i"] += 1
                            qft, qf_b = qf[i]
                            qbt, qb_b = qb[i]
                            headnorm_evac(qft, qf_b, gq_bc)(t, c0, n, ps, ps_b, junk, junk_b)
                            S.op("scalar", "copy", reads=[qf_b], writes=[qb_b], out=qbt[:n, :], in_=qft[:n, :])
                            tp, tp_b = tb()
                            for j in range(4):
                                tr(tp[:, j * 128:j * 128 + n], tp_b, qbt[:n, j * 128:(j + 1) * 128],
                                   ident_b[:n, :n], [qb_b, cb_b])
                            S.op("vector", "tensor_copy", reads=[tp_b], writes=[QT_b],
                                 out=QT[:, 4 * s:4 * s + 4, c0:c0 + n],
                                 in_=tp[:, 0:512].rearrange("p (j c) -> p j c", j=4)[:, :, :n])

                        mm_tok(uT, uT_b, slab, slab_b, 512, ev_q2)
                S.barrier()

                S.barrier()
                uflat = uT[:, :, :].rearrange("p c t -> p (c t)")
                uflat32 = uflat.bitcast(F32)

                def carve(flat, off, shape):
                    n = 1
                    for d_ in shape:
                        n *= d_
                    v = flat[:, off:off + n]
                    if len(shape) == 2:
                        v = v.rearrange("p (a b) -> p a b", a=shape[0])
                    return v

                lfa, lfa_b = carve(uflat32, 0, [64, 16]), Buf("lfa")
                tot, tot_b = carve(uflat32, 1024, [64, 16]), Buf("tot")
                inc, inc_b = carve(uflat32, 2048, [64, 16]), Buf("inc")
                ckk, ck_b = carve(uflat32, 3072, [64, 16]), Buf("ckk")
                tmpc, tmpc_b = carve(uflat32, 4096, [64, 16]), Buf("tmpc")
                lfs, lfs_b = carve(uflat32, 5120, [17, 16]), Buf("lfs")
                tots, tots_b = carve(uflat32, 5120 + 272, [17, 16]), Buf("tots")
                incs, incs_b = carve(uflat32, 5120 + 544, [17, 16]), Buf("incs")
                cref, cref_b = sb("cref", [128, 3, 16], F32, fox_sc)
                ckown, ckown_b = sb("ckown", [128, 8, 16], F32, fox_sc)
                biasG = [sb("biasG%d" % i, [128, 64, 16], F32, fox_sc) for i in range(2)]
                biasO = [sb("biasO%d" % i, [128, 8, 16], F32, fox_sc) for i in range(2)]
                biasS, biasS_b = sb("biasS", [128, 17, 16], F32, fox_sc)

                def cumsum_tiles(src, src_b, nt_, tot_t, tot_tb, inc_t, inc_tb, dst, dst_b):
                    flat = src[:, :, :].rearrange("p t h -> p (t h)")
                    totf = tot_t[:, :, :].rearrange("p t h -> p (t h)")
                    ncol = nt_ * 16
                    for c0 in range(0, ncol, 512):
                        cn = min(512, ncol - c0)
                        ps, ps_b = bank()
                        mm(ps[:, :cn], ps_b, ones_f, flat[:, c0:c0 + cn], True, True, [cf_b, src_b])
                        S.op("scalar", "copy", reads=[ps_b], writes=[tot_tb], out=totf[:, c0:c0 + cn], in_=ps[:, :cn])
                    for hd in range(16):
                        S.op("vector", "tensor_tensor_scan", reads=[tot_tb, cf_b], writes=[inc_tb],
                             out=inc_t[:, :, hd], data0=ones_f[:, 0:nt_], data1=tot_t[:, :, hd], initial=0.0,
                             op0=ALU.mult, op1=ALU.add)
                    S.op("vector", "tensor_tensor", reads=[inc_tb, tot_tb], writes=[inc_tb], out=inc_t[:, :, :],
                         in0=inc_t[:, :, :], in1=tot_t[:, :, :], op=ALU.subtract)
                    dstf = dst[:, :, :].rearrange("p t h -> p (t h)")
                    incf = inc_t[:, :, :].rearrange("p t h -> p (t h)")
                    for c0 in range(0, nt_, 32):
                        cn = min(32, nt_ - c0)
                        ps, ps_b = bank()
                        for t in range(cn):
                            mm(ps[:, t * 16:(t + 1) * 16], ps_b, tri_f, src[:, c0 + t, :], True, True, [cf_b, src_b])
                        S.op("vector", "tensor_tensor", reads=[ps_b, inc_tb], writes=[dst_b],
                             out=dstf[:, c0 * 16:(c0 + cn) * 16], in0=ps[:, :cn * 16],
                             in1=incf[:, c0 * 16:(c0 + cn) * 16], op=ALU.add)

                lfv = lf_out.ap().rearrange("(t p) h -> p t h", p=128)
                S.dma("sync", [(lfa[:, 8 * r:8 * r + 8, :], lfv[:, 8 * r:8 * r + 8, :]) for r in range(8)],
                      reads=[lfout_b], writes=[lfa_b])
                cumsum_tiles(lfa, lfa_b, 64, tot, tot_b, inc, inc_b, ckk, ck_b)
                for qt in range(2):
                    wqv = cf[:, C_WQ + 64 * qt:C_WQ + 64 * (qt + 1)]
                    S.op("vector", "tensor_tensor", reads=[tot_b, cf_b], writes=[tmpc_b], out=tmpc[:, :, :],
                         in0=tot[:, :, :], in1=wqv.unsqueeze(2).to_broadcast([128, 64, 16]), op=ALU.mult)
                    S.op("vector", "tensor_reduce", reads=[tmpc_b], writes=[cref_b], out=cref[:, qt, :],
                         in_=tmpc[:, :, :].rearrange("p t h -> p h t"), axis=AX.X, op=ALU.add)
                for j in range(8):
                    selv = cf[:, C_SEL + 64 * j:C_SEL + 64 * (j + 1)]
                    S.op("vector", "tensor_tensor", reads=[ck_b, cf_b], writes=[tmpc_b], out=tmpc[:, :, :],
                         in0=ckk[:, :, :], in1=selv.unsqueeze(2).to_broadcast([128, 64, 16]), op=ALU.mult)
                    S.op("vector", "tensor_reduce", reads=[tmpc_b], writes=[ckown_b], out=ckown[:, j, :],
                         in_=tmpc[:, :, :].rearrange("p t h -> p h t"), axis=AX.X, op=ALU.add)
                rbv = cf[:, C_RANKB:C_RANKB + 64]
                for qt in range(2):
                    bg, bg_b = biasG[qt]
                    S.op("vector", "tensor_tensor", reads=[cref_b, ck_b], writes=[bg_b], out=bg[:, :, :],
                         in0=cref[:, qt:qt + 1, :].to_broadcast([128, 64, 16]), in1=ckk[:, :, :], op=ALU.subtract)
                    S.op("vector", "tensor_tensor", reads=[bg_b, cf_b], writes=[bg_b], out=bg[:, :, :],
                         in0=bg[:, :, :], in1=rbv.unsqueeze(2).to_broadcast([128, 64, 16]), op=ALU.add)
                    bo, bo_b = biasO[qt]
                    S.op("vector", "tensor_tensor", reads=[cref_b, ckown_b], writes=[bo_b], out=bo[:, :, :],
                         in0=cref[:, qt:qt + 1, :].to_broadcast([128, 8, 16]), in1=ckown[:, :, :], op=ALU.subtract)
                S.op("vector", "memset", writes=[lfs_b], ap=lfs[:, :, :], constant=0.0)
                clv = cl_c.rearrange("(t p) h -> p t h", p=128)
                S.dma("sync", [(lfs[:, 0:8, :], clv[:, 0:8, :]), (lfs[:, 8:16, :], clv[:, 8:16, :]),
                               (lfs[:64, 16, :], lfs_in.ap()[:, :])], reads=[lfin_b], writes=[lfs_b])
                cumsum_tiles(lfs, lfs_b, 17, tots, tots_b, incs, incs_b, biasS, biasS_b)
                S.op("vector", "tensor_reduce", reads=[tots_b], writes=[cref_b], out=cref[:, 2, :],
                     in_=tots[:, :, :].rearrange("p t h -> p h t"), axis=AX.X, op=ALU.add)
                S.op("vector", "tensor_tensor", reads=[cref_b, biasS_b], writes=[biasS_b], out=biasS[:, :, :],
                     in0=cref[:, 2:3, :].to_broadcast([128, 17, 16]), in1=biasS[:, :, :], op=ALU.subtract)

                S.barrier()
                KTh, KTh_b = carve(uflat, 0, [8, 1024]), Buf("KTh")
                Vh, Vh_b = carve(uflat, 8192, [64, 130]), Buf("Vh")
                w0 = wp[0][0][:, :, :].rearrange("p c t -> p (c t)")
                w1 = wp[1][0][:, :, :].rearrange("p c t -> p (c t)")
                KTo, KTo_b = carve(w0, 0, [NTOK]), Buf("KTo")
                Vo, Vo_b = carve(w0, 1088, [9, 130]), Buf("Vo")
                kct, kct_b = carve(w0, 2304, [16, 128]), Buf("kct")
                KTs, KTs_b = carve(w0, 4352, [2048]), Buf("KTs")
                Vs, Vs_b = carve(w1, 0, [16, 130]), Buf("Vs")
                PT = [(carve(w1, 2080 + 512 * i, [512]), Buf("PT%d" % i)) for i in range(3)]
                OTh, OTh_b = carve(w1, 3616, [NTOK]), Buf("OTh")
                woh, woh_b = carve(w1, 4704, [D]), Buf("woh")
                onb, onb_b = carve(w1, 6752, [128]), Buf("onb")
                rec, rec_b = sb("rec", [128, 1], F32, fox_sc)
                S.op("vector", "memset", writes=[Vh_b], ap=Vh[:, :, 128:130], constant=1.0)
                S.op("vector", "memset", writes=[Vo_b], ap=Vo[:, :, 128:130], constant=1.0)
                S.op("vector", "memset", writes=[Vs_b], ap=Vs[:, :, 128:130], constant=1.0)
                pcnt = {"i": 0}
                SC = 128.0 ** -0.5

                def att_qk(h, kmat, kb_, n_k, qcol0, qn, lo, diag, bias_ap, bias_b, vmat, vb_, obanks, first, last,
                           sub_n):
                    ps, ps_b = bank(4, 6)
                    c_lo = lo * 128
                    if diag:
                        dn_ = min(128, qn - c_lo)
                        mm(ps[:n_k, c_lo:c_lo + dn_], ps_b, kmat, QT[:, h, qcol0 + c_lo:qcol0 + c_lo + dn_], True, False,
                           [kb_, QT_b])
                        mm(ps[:n_k, c_lo:c_lo + dn_], ps_b, ident_b[:n_k, :n_k], negm_b[:n_k, :dn_], False, True, [cb_b])
                        if c_lo + dn_ < qn:
                            mm(ps[:n_k, c_lo + dn_:qn], ps_b, kmat, QT[:, h, qcol0 + c_lo + dn_:qcol0 + qn], True, True,
                               [kb_, QT_b])
                    else:
                        mm(ps[:n_k, c_lo:qn], ps_b, kmat, QT[:, h, qcol0 + c_lo:qcol0 + qn], True, True, [kb_, QT_b])
                    pt, pt_b = PT[pcnt["i"] % 3]
                    pcnt["i"] += 1
                    S.op("scalar", "activation", reads=[ps_b, bias_b], writes=[pt_b], out=pt[:n_k, c_lo:qn],
                         in_=ps[:n_k, c_lo:qn], func=AF.Exp, scale=SC, bias=bias_ap)
                    nsub = (qn + 127) // 128

                    def pv():
                        for sub in range(lo, nsub):
                            ob, ob_b = obanks[sub]
                            mm(ob[:sub_n, 0:129], ob_b, pt[:n_k, sub * 128:sub * 128 + sub_n], vmat, first, last,
                               [pt_b, vb_])
                    return pv

                def run_steps(steps):
                    pend = None
                    for args in steps:
                        nxt = att_qk(*args)
                        if pend is not None:
                            pend()
                        pend = nxt
                    if pend is not None:
                        pend()

                def att_evac(h, obanks, qcol0, nsub, sub_n):
                    for sub in range(nsub):
                        ob, ob_b = obanks[sub]
                        S.op("vector", "reciprocal", reads=[ob_b], writes=[rec_b], out=rec[:sub_n, :],
                             in_=ob[:sub_n, 128:129])
                        S.op("vector", "tensor_scalar", reads=[ob_b, rec_b], writes=[onb_b], out=onb[:sub_n, :],
                             in0=ob[:sub_n, 0:128], scalar1=rec[:sub_n, 0:1], scalar2=None, op0=ALU.mult)
                        tp, tp_b = tb()
                        tr(tp[:, 0:sub_n], tp_b, onb[:sub_n, :], ident_b[:sub_n, :sub_n], [onb_b, cb_b])
                        S.op("scalar", "copy", reads=[tp_b], writes=[OTh_b],
                             out=OTh[:, qcol0 + sub * 128:qcol0 + sub * 128 + sub_n], in_=tp[:, 0:sub_n])

                obanks = [pbank[i] for i in range(4)]
                ktv = kt_out.ap().rearrange("(r hd) t -> hd r t", r=8)
                vv = v_out.ap().rearrange("(t p) c -> p t c", p=128)
                for h in range(16):
                    hs = slice(h * 128, (h + 1) * 128)
                    S.dma("sync", [(KTh[:, 0:4, :], ktv[hs, 0:4, :]), (KTh[:, 4:8, :], ktv[hs, 4:8, :])],
                          reads=[ktout_b], writes=[KTh_b])
                    S.dma("sync", [(Vh[:, 8 * r:8 * r + 8, 0:128], vv[:, 8 * r:8 * r + 8, hs]) for r in range(8)],
                          reads=[vout_b], writes=[Vh_b])
                    S.dma("sync", [(KTo[:, 0:1024], kt_in.ap()[hs, :]), (KTo[:, 1024:NTOK], kts_in.ap()[hs, :])],
                          reads=[ktin_b], writes=[KTo_b])
                    S.dma("sync", [(Vo[:, 0:8, 0:128], v_in.ap().rearrange("(t p) c -> p t c", p=128)[:, :, hs]),
                                   (Vo[:64, 8, 0:128], vs_in.ap()[:, hs])], reads=[vin_b], writes=[Vo_b])
                    ckv = ck_c.rearrange("(t p) c -> p t c", p=128)
                    cvv = cv_c.rearrange("(t p) c -> p t c", p=128)
                    S.dma("gpsimd", [(kct[:, 8 * r:8 * r + 8, :], ckv[:, 8 * r:8 * r + 8, hs]) for r in range(2)],
                          writes=[kct_b])
                    S.dma("gpsimd", [(Vs[:, 8 * r:8 * r + 8, 0:128], cvv[:, 8 * r:8 * r + 8, hs]) for r in range(2)],
                          writes=[Vs_b])
                    S.dma("gpsimd", [(woh[:, :], fox_w_o[hs, :])], writes=[woh_b])
                    for half in range(2):
                        tp, tp_b = tb()
                        for j in range(8):
                            tr(tp[:, j * 128:(j + 1) * 128], tp_b, kct[:, half * 8 + j, :], ident_b[:, :], [kct_b, cb_b])
                        S.op("vector" if half else "scalar", "tensor_copy" if half else "copy", reads=[tp_b],
                             writes=[KTs_b], out=KTs[:, half * 1024:(half + 1) * 1024], in_=tp[:, :])
                    for qt in range(2):
                        q0 = qt * 512
                        nown = 4 * qt + 4
                        steps = []
                        for j in range(nown):
                            lo = max(0, j - 4 * qt)
                            diag = (j - 4 * qt) >= 0
                            steps.append((h, KTo[:, j * 128:(j + 1) * 128], KTo_b, 128, q0, 512, lo, diag,
                                          biasO[qt][0][:, j, h:h + 1], biasO[qt][1], Vo[:, j, 0:129], Vo_b, obanks,
                                          j == 0, False, 128))
                        for t in range(64):
                            steps.append((h, KTh[:, t // 8, (t % 8) * 128:(t % 8 + 1) * 128], KTh_b, 128, q0, 512, 0,
                                          False, biasG[qt][0][:, t, h:h + 1], biasG[qt][1], Vh[:, t, 0:129], Vh_b,
                                          obanks, False, t == 63, 128))
                        run_steps(steps)
                        att_evac(h, obanks, q0, 4, 128)
                    steps = []
                    for t in range(16):
                        steps.append((h, KTs[:, t * 128:(t + 1) * 128], KTs_b, 128, 1024, 64, 0, False,
                                      biasS[:, t, h:h + 1], biasS_b, Vs[:, t, 0:129], Vs_b, obanks, t == 0, False, 64))
                    steps.append((h, KTo[:, 1024:NTOK], KTo_b, 64, 1024, 64, 0, True, biasS[:64, 16, h:h + 1], biasS_b,
                                  Vo[:64, 8, 0:129], Vo_b, obanks, False, True, 64))
                    run_steps(steps)
                    att_evac(h, obanks, 1024, 1, 64)
                    for s in range(4):
                        for t, (c0, n) in enumerate(TILES):
                            ps, ps_b = bank(4, 6)
                            mm(ps[:n, :], ps_b, OTh[:, c0:c0 + n], woh[:, s * 512:(s + 1) * 512], True, True,
                               [OTh_b, woh_b])
                            add_into_h(s)(t, c0, n, ps, ps_b)
            S.barrier()

        if stage >= 5:
            ffn(1)

        for t, (c0, n) in enumerate(TILES):
            S.dma("sync", [(y_all[c0:c0 + n, :], hh[:n, t, :])], reads=[hb[t]])
        S.barrier()
        block = es.enter_context(nc.Block())
        S.emit(block)
    return nc


def _consts(c):
    cfm = np.zeros((128, NCONST), np.float32)
    cfm[:, C_IDENT:C_IDENT + 128] = np.eye(128, dtype=np.float32)
    tri = np.triu(np.ones((128, 128), np.float32))
    cfm[:, C_TRI:C_TRI + 128] = tri
    rm = np.ones(NTOK, np.float32)
    rm[0:1024:128] = 0.0
    rm[1024] = 0.0
    cfm[:, C_RESET:C_RESET + NTOK] = rm[None, :]
    cfm[:, C_CMASK:C_CMASK + 8] = (np.arange(8) < c).astype(np.float32)[None, :]
    rb = np.where((np.arange(64) // 8) < c, 0.0, NEG).astype(np.float32)
    cfm[:, C_RANKB:C_RANKB + 64] = rb[None, :]
    for qt in range(2):
        w = (np.arange(64) < 8 * c + 4 * (qt + 1)).astype(np.float32)
        cfm[:, C_WQ + 64 * qt:C_WQ + 64 * (qt + 1)] = w[None, :]
    for j in range(8):
        sel = (np.arange(64) == 8 * c + j).astype(np.float32)
        cfm[:, C_SEL + 64 * j:C_SEL + 64 * (j + 1)] = sel[None, :]
    cfm[:, C_ONES:C_ONES + 128] = 1.0
    cbm = np.zeros((128, 256), np.float32)
    cbm[:, 0:128] = np.eye(128, dtype=np.float32)
    cbm[:, 128:256] = (tri - 1.0) * (-NEG)
    return cfm, cbm.astype(ml_dtypes.bfloat16)


_NC_CACHE = {}


def kernel(x_prompt, x_sample, state_gla, cache_k, cache_v, cache_logf, g_mix, g_ffn,
           gla_w_in, gla_w_gk2, gla_b_gk, gla_g_out, gla_w_out, kv_g, kv_w, kv_b_f, kv_g_k,
           fox_w_q, fox_g_q, fox_w_o, ffn_w_in, ffn_w_out, _stage=99):
    f = lambda a: np.ascontiguousarray(np.asarray(a, dtype=np.float32))
    x_prompt, x_sample, state_gla = f(x_prompt), f(x_sample), f(state_gla)
    cache_k, cache_v, cache_logf = f(cache_k), f(cache_v), f(cache_logf)
    if _stage not in _NC_CACHE:
        _NC_CACHE[_stage] = build(_stage)
    nc = _NC_CACHE[_stage]

    def fm(v):
        return f(v).reshape(16, 128).T

    gains = np.concatenate([fm(g_mix[0]), fm(g_ffn[0]), fm(kv_g), fm(g_mix[1]), fm(g_ffn[1])], axis=1)
    shared = {
        "gla_w_in": f(gla_w_in[0]), "gla_w_gk2": f(gla_w_gk2[0]), "gla_w_out": f(gla_w_out[0]),
        "kv_w": f(kv_w), "fox_w_q": f(fox_w_q[0]), "fox_w_o": f(fox_w_o[0]),
        "ffn_w_in": f(ffn_w_in), "ffn_w_out": f(ffn_w_out),
    }
    in_maps = []
    for c in range(NCORES):
        cfm, cbm = _consts(c)
        cfm[:, C_GAINS:C_GAINS + 80] = gains
        cfm[:, C_BGK:C_BGK + 8] = f(gla_b_gk[0]).reshape(8, 128).T
        cfm[:, C_GOUT:C_GOUT + 512] = f(gla_g_out[0])[None, :]
        cfm[:, C_GK:C_GK + 128] = f(kv_g_k)[None, :]
        cfm[:, C_GQ:C_GQ + 128] = f(fox_g_q[0])[None, :]
        cfm[:, C_BF:C_BF + 16] = f(kv_b_f)[None, :]
        m = dict(shared)
        m["x_all"] = np.ascontiguousarray(np.concatenate([x_prompt[0, c * 1024:(c + 1) * 1024], x_sample[c]], axis=0))
        m["st_s"] = np.ascontiguousarray(state_gla[0, c])
        m["ck_c"] = np.ascontiguousarray(cache_k[c].reshape(2048, 2048))
        m["cv_c"] = np.ascontiguousarray(cache_v[c].reshape(2048, 2048))
        m["cl_c"] = np.ascontiguousarray(cache_logf[c])
        m["constf"] = cfm
        m["constb"] = cbm
        in_maps.append(m)
    res = run_bass_kernel_spmd(nc, in_maps, core_ids=list(range(NCORES)))
    R = res.results
    cat = lambda k, lo, hi: np.concatenate([np.asarray(R[c][k], np.float32)[lo:hi] for c in range(NCORES)], axis=0)
    y_p = cat("y_all", 0, 1024)[None]
    y_s = np.stack([np.asarray(R[c]["y_all"], np.float32)[1024:NTOK] for c in range(NCORES)])
    sgp = np.asarray(R[NCORES - 1]["sg_p"], np.float32)[None, None]
    k_p = cat("k_o", 0, 1024).reshape(1, 8192, 16, 128)
    v_p = cat("v_o", 0, 1024).reshape(1, 8192, 16, 128)
    lf_p = cat("lf_o", 0, 1024).reshape(1, 8192, 16)
    sgs = np.stack([np.asarray(R[c]["sg_s"], np.float32) for c in range(NCORES)])[None]
    k_s = np.stack([np.asarray(R[c]["k_o"], np.float32)[1024:NTOK] for c in range(NCORES)]).reshape(8, 64, 16, 128)
    v_s = np.stack([np.asarray(R[c]["v_o"], np.float32)[1024:NTOK] for c in range(NCORES)]).reshape(8, 64, 16, 128)
    lf_s = np.stack([np.asarray(R[c]["lf_o"], np.float32)[1024:NTOK] for c in range(NCORES)])
    return (y_p, y_s, sgp, k_p, v_p, lf_p, sgs, k_s, v_s, lf_s)
```

```python
import numpy as np
import ml_dtypes
from contextlib import ExitStack
import concourse.bass as bass
import concourse.mybir as mybir
from concourse.bass_utils import run_bass_kernel_spmd

F32 = mybir.dt.float32
BF16 = mybir.dt.bfloat16
ALU = mybir.AluOpType
AF = mybir.ActivationFunctionType
AX = mybir.AxisListType

NCORES = 8
D = 2048
KC = 16
NTOK = 1088
TILES = [(t * 128, 128) for t in range(8)] + [(1024, 64)]
GROUPS = [(0, 512), (512, 512), (1024, 64)]
DFF = 5632
EPS = 1e-6
NEG = -30000.0
GLA_IN = 6160
STRICT = True

C_IDENT = 0
C_TRI = 128
C_RESET = 256
C_GAINS = C_RESET + NTOK
C_BGK = C_GAINS + 80
C_GOUT = C_BGK + 8
C_GK = C_GOUT + 512
C_GQ = C_GK + 128
C_BF = C_GQ + 128
C_CMASK = C_BF + 16
C_RANKB = C_CMASK + 8
C_WQ = C_RANKB + 64
C_SEL = C_WQ + 128
C_ONES = C_SEL + 512
NCONST = C_ONES + 128


class Buf:
    __slots__ = ("name", "lw", "rd")

    def __init__(self, name):
        self.name = name
        self.lw = []
        self.rd = {}


class Sched:
    ENG = ("tensor", "vector", "scalar", "gpsimd", "sync")
    DQ = ("sync", "gpsimd")
    R = 8

    def __init__(self, nc, es):
        self.nc = nc
        self.streams = {e: [] for e in self.ENG}
        self.sem = {}
        for e in ("tensor", "vector", "scalar", "gpsimd"):
            self.sem[e] = es.enter_context(nc.semaphore("s_" + e))
        for q in self.DQ:
            for s in range(self.R):
                self.sem[(q, s)] = es.enter_context(nc.semaphore("d_%s%d" % (q, s)))
        self.cnt = {e: 0 for e in self.ENG}
        self.seen = {e: {} for e in self.ENG}
        self.dn = {q: 0 for q in self.DQ}
        self.last = {}
        self.ncc = 0
        self.es = es
        self.ccb = Buf("ccchain")

    def _wait(self, eng, ev):
        k, v = ev
        if self.seen[eng].get(k, 0) >= v:
            return
        self.seen[eng][k] = v
        self.streams[eng].append(("wait", self.sem[k], v))

    def _deps(self, eng, reads, writes):
        for b in reads:
            for ev in b.lw:
                if ev[0] == eng and eng == "tensor":
                    continue
                self._wait(eng, ev)
        strict = STRICT and eng != "tensor"
        for b in writes:
            for ev in b.lw:
                if ev[0] != eng or strict:
                    self._wait(eng, ev)
            for k, v in b.rd.items():
                if k != eng or strict:
                    self._wait(eng, (k, v))

    def _mark(self, evs, reads, writes):
        for b in reads:
            for ev in evs:
                b.rd[ev[0]] = max(b.rd.get(ev[0], 0), ev[1])
        for b in writes:
            b.lw = list(evs)
            b.rd = {}

    def op(self, eng, meth, reads=(), writes=(), **kw):
        self._deps(eng, reads, writes)
        self.cnt[eng] += 1
        ev = (eng, self.cnt[eng])
        self.last[eng] = self.cnt[eng]
        self.streams[eng].append(("op", meth, kw, self.sem[eng], 1))
        self._mark([ev], reads, writes)
        return ev

    def dma(self, q, pairs, reads=(), writes=(), acc=()):
        self._deps(q, reads, writes)
        evs = []
        for (o, i) in pairs:
            n = self.dn[q]
            self.dn[q] += 1
            slot, rnd = n % self.R, n // self.R
            key = (q, slot)
            if rnd > 0:
                self._wait(q, (key, 16 * rnd))
            self.streams[q].append(("op", "dma_start", dict(out=o, in_=i), self.sem[key], 16))
            ev = (key, 16 * (rnd + 1))
            self.last[key] = ev[1]
            evs.append(ev)
        self._mark(evs, reads, writes)
        for b in acc:
            b.lw = b.lw + list(evs)
        return evs

    def collective(self, in_ap, out_ap, reads=(), writes=()):
        q = "gpsimd"
        reads = list(reads) + [self.ccb]
        writes = list(writes) + [self.ccb]
        self._deps(q, reads, writes)
        sem = self.es.enter_context(self.nc.semaphore("cc%d" % self.ncc))
        key = ("cc", self.ncc)
        self.ncc += 1
        self.sem[key] = sem
        self.streams[q].append(("cc", in_ap, out_ap, sem))
        ev = (key, 1)
        self.last[key] = 1
        self._mark([ev], reads, writes)

    def barrier(self):
        for e in self.ENG:
            for k, v in list(self.last.items()):
                if k != e:
                    self._wait(e, (k, v))

    def emit(self, block):
        nc = self.nc

        def run(e, stream):
            for it in stream:
                if it[0] == "wait":
                    e.wait_ge(it[1], it[2])
                elif it[0] == "op":
                    getattr(e, it[1])(**it[2]).then_inc(it[3], it[4])
                else:
                    e.collective_compute(
                        "AllGather", ALU.bypass, replica_groups=[list(range(NCORES))],
                        ins=[it[1]], outs=[it[2]]).then_inc(it[3])

        @block.tensor
        def _(e):
            run(e, self.streams["tensor"])

        @block.vector
        def _(e):
            run(e, self.streams["vector"])

        @block.scalar
        def _(e):
            run(e, self.streams["scalar"])

        @block.gpsimd
        def _(e):
            run(e, self.streams["gpsimd"])

        @block.sync
        def _(e):
            run(e, self.streams["sync"])


class _Skip(Exception):
    pass


def build(stage=99, skip=()):
    nc = bass.Bass("TRN2", target_bir_lowering=False)

    def din(name, shape, dt=F32):
        return nc.dram_tensor(name, list(shape), dt, kind="ExternalInput").ap()

    def dout(name, shape, dt=F32):
        return nc.dram_tensor(name, list(shape), dt, kind="ExternalOutput").ap()

    def dint(name, shape, dt=F32):
        return nc.dram_tensor(name, list(shape), dt)

    x_all = din("x_all", [NTOK, D])
    st_s = din("st_s", [4, 256, 512])
    ck_c = din("ck_c", [2048, 2048])
    cv_c = din("cv_c", [2048, 2048])
    cl_c = din("cl_c", [2048, 16])
    constf = din("constf", [128, NCONST])
    constb = din("constb", [128, 256], BF16)
    gla_w_in = din("gla_w_in", [D, GLA_IN])
    gla_w_gk2 = din("gla_w_gk2", [16, 1024])
    gla_w_out = din("gla_w_out", [D, D])
    kv_w = din("kv_w", [D, 4112])
    fox_w_q = din("fox_w_q", [D, D])
    fox_w_o = din("fox_w_o", [D, D])
    ffn_w_in = din("ffn_w_in", [2, D, 2 * DFF])
    ffn_w_out = din("ffn_w_out", [2, DFF, D])

    y_all = dout("y_all", [NTOK, D])
    k_o = dout("k_o", [NTOK, D])
    v_o = dout("v_o", [NTOK, D])
    lf_o = dout("lf_o", [NTOK, 16])
    sg_p = dout("sg_p", [4, 256, 512])
    sg_s = dout("sg_s", [4, 256, 512])

    xg_in = [dint("xg_in%d" % h, [128, 1026]) for h in range(4)]
    xg_out = [dint("xg_out%d" % h, [1024, 1026]) for h in range(4)]
    kt_in = dint("kt_in", [2048, 1024], BF16)
    kt_out = dint("kt_out", [8 * 2048, 1024], BF16)
    kts_in = dint("kts_in", [2048, 64], BF16)
    v_in = dint("v_in", [1024, 2048], BF16)
    vs_in = dint("vs_in", [64, 2048], BF16)
    v_out = dint("v_out", [8192, 2048], BF16)
    lf_in = dint("lf_in", [1024, 16])
    lfs_in = dint("lfs_in", [64, 16])
    lf_out = dint("lf_out", [8192, 16])

    es = ExitStack()
    with es:
        S = Sched(nc, es)

        sbn = {"i": 0}

        def sb(name, shape, dt, scope=None):
            sbn["i"] += 1
            t = (scope or es).enter_context(nc.sbuf_tensor("%s_%d" % (name, sbn["i"]), list(shape), dt))
            return t, Buf(name)

        cf, cf_b = sb("cf", [128, NCONST], F32)
        cb, cb_b = sb("cb", [128, 256], BF16)
        uT, uT_b = sb("uT", [128, KC, NTOK], BF16)
        NW = 2
        wp = [sb("wp%d" % i, [128, KC, 512], BF16) for i in range(NW)]
        aux, aux_b = sb("aux", [128, KC * NTOK], BF16)
        oT, oT_b = aux[:, :].rearrange("p (c t) -> p c t", c=KC), aux_b
        wstate = {"i": 0}
        ss, ss_b = sb("ss", [128, 8], F32)
        rstd, rstd_b = sb("rstd", [128, 8], F32)

        ident_f = cf[:, C_IDENT:C_IDENT + 128]
        tri_f = cf[:, C_TRI:C_TRI + 128]
        resetm = cf[:, C_RESET:C_RESET + NTOK]
        ones_f = cf[:, C_ONES:C_ONES + 128]
        ident_b = cb[:, 0:128]
        negm_b = cb[:, 128:256]

        def gain(i, c):
            return cf[:, C_GAINS + 16 * i + c:C_GAINS + 16 * i + c + 1]

        pbank = []
        for i in range(7):
            t = es.enter_context(nc.psum_tensor("pb%d" % i, [128, 512], F32))
            pbank.append((t, Buf("pb%d" % i)))
        tbank = []
        for i in range(1):
            t = es.enter_context(nc.psum_tensor("tb%d" % i, [128, 1024], BF16))
            tbank.append((t, Buf("tb%d" % i)))
        pstate = {"i": 0, "t": 0}

        def bank(lo=0, hi=6):
            i = pstate["i"]
            if i < lo or i >= hi:
                i = lo
            pstate["i"] = i + 1 if i + 1 < hi else lo
            return pbank[i]

        def tb():
            return tbank[0]

        S.dma("sync", [(cf[:, :], constf[:, :])], writes=[cf_b])
        S.dma("sync", [(cb[:, :], constb[:, :])], writes=[cb_b])

        def mm(ps, ps_b, lhsT, rhs, start, stop, reads):
            S.op("tensor", "matmul", reads=reads, writes=[ps_b], out=ps, lhsT=lhsT, rhs=rhs,
                 start=start, stop=stop)

        def tr(ps, ps_b, in_, ident, reads):
            S.op("tensor", "transpose", reads=reads, writes=[ps_b], out=ps, in_=in_, identity=ident)

        def load_slab(w2d, specs, kc=KC, q="gpsimd"):
            i = wstate["i"]
            wstate["i"] = (i + 1) % NW
            t, b = wp[i]
            wv = w2d.rearrange("(kc p) c -> p kc c", p=128)
            pairs = []
            hk = max(kc // 2, 1)
            for (c0, n, d0) in specs:
                for k0 in range(0, kc, hk):
                    pairs.append((t[:, k0:k0 + hk, d0:d0 + n], wv[:, k0:k0 + hk, c0:c0 + n]))
            S.dma(q, pairs, writes=[b])
            return t, b

        def rstd_from_ss(n, ncol, denom):
            S.op("vector", "tensor_scalar", reads=[ss_b], writes=[rstd_b], out=rstd[:n, :ncol], in0=ss[:n, :ncol],
                 scalar1=1.0 / denom, scalar2=EPS, op0=ALU.mult, op1=ALU.add)
            S.op("scalar", "activation", reads=[rstd_b], writes=[rstd_b], out=rstd[:n, :ncol], in_=rstd[:n, :ncol],
                 func=AF.Ln)
            S.op("scalar", "activation", reads=[rstd_b], writes=[rstd_b], out=rstd[:n, :ncol], in_=rstd[:n, :ncol],
                 func=AF.Exp, scale=-0.5)

        def norm_phase(get_tile, gain_idx):
            with ExitStack() as sc:
                xn, xn_b = sb("xn", [128, D], BF16, sc)
                junk, junk_b = sb("junk", [128, D], BF16, sc)
                for t, (c0, n) in enumerate(TILES):
                    src, src_b = get_tile(t, n)
                    S.op("scalar", "activation", reads=[src_b], writes=[junk_b, ss_b], out=junk[:n, :], in_=src,
                         func=AF.Square, accum_out=ss[:n, 0:1])
                    rstd_from_ss(n, 1, float(D))
                    S.op("vector", "tensor_scalar", reads=[src_b, rstd_b], writes=[xn_b], out=xn[:n, :], in0=src,
                         scalar1=rstd[:n, 0:1], scalar2=None, op0=ALU.mult)
                    for half in range(2):
                        tp, tp_b = tb()
                        for j in range(8):
                            c = half * 8 + j
                            tr(tp[:, j * 128:j * 128 + n], tp_b, xn[:n, c * 128:(c + 1) * 128], ident_b[:n, :n],
                               [xn_b, cb_b])
                        for j in range(8):
                            c = half * 8 + j
                            eng = "scalar" if half == 0 else "vector"
                            if eng == "scalar":
                                S.op("scalar", "mul", reads=[tp_b, cf_b], writes=[uT_b], out=uT[:, c, c0:c0 + n],
                                     in_=tp[:, j * 128:j * 128 + n], mul=gain(gain_idx, c))
                            else:
                                S.op("vector", "tensor_scalar", reads=[tp_b, cf_b], writes=[uT_b],
                                     out=uT[:, c, c0:c0 + n], in0=tp[:, j * 128:j * 128 + n],
                                     scalar1=gain(gain_idx, c), scalar2=None, op0=ALU.mult)
            S.barrier()

        def mm_tok(act, act_b, slab, slab_b, ncols, evac, kcn=KC, col0=0, tiles=None):
            for t, (c0, n) in enumerate(TILES):
                if tiles is not None and t not in tiles:
                    continue
                ps, ps_b = bank()
                for kc in range(kcn):
                    mm(ps[:n, :ncols], ps_b, act[:, kc, c0:c0 + n], slab[:, kc, col0:col0 + ncols], kc == 0,
                       kc == kcn - 1, [act_b, slab_b])
                evac(t, c0, n, ps, ps_b)

        def mm_feat(act, act_b, slab, slab_b, col0, m, evac, kcn=KC):
            for (g0, gn) in GROUPS:
                ps, ps_b = bank()
                for kc in range(kcn):
                    mm(ps[:m, :gn], ps_b, slab[:, kc, col0:col0 + m], act[:, kc, g0:g0 + gn], kc == 0, kc == kcn - 1,
                       [act_b, slab_b])
                evac(g0, gn, ps, ps_b)


        with ExitStack() as sc0:
            xs = [sb("xs%d" % i, [128, D], F32, sc0) for i in range(2)]

            def get_x(t, n):
                xt, xb = xs[t % 2]
                c0 = TILES[t][0]
                S.dma("sync", [(xt[:n, :], x_all[c0:c0 + n, :])], writes=[xb])
                return xt[:n, :], xb

            norm_phase(get_x, 0)
        S.barrier()

        gla_sc = ExitStack()
        try:
          with gla_sc:
              if "gla" in skip:
                  raise _Skip()
              gklT, gklT_b = sb("gklT", [16, NTOK], BF16, gla_sc)
              wgk2, wgk2_b = sb("wgk2", [16, 1024], BF16, gla_sc)
              negb, negb_b = sb("negb", [128, 8], F32, gla_sc)
              qtT, qtT_b = sb("qtT", [128, 2, NTOK], BF16, gla_sc)
              ktT, ktT_b = sb("ktT", [128, 2, NTOK], BF16, gla_sc)
              khT, khT_b = sb("khT", [128, 2, NTOK], BF16, gla_sc)
              khat, khat_b = sb("khat", [128, 9, 256], BF16, gla_sc)
              v_sb, v_sbb = sb("v_sb", [128, 9, 512], BF16, gla_sc)
              g_sb, g_sbb = sb("g_sb", [128, 9, 512], BF16, gla_sc)
              eb = [sb("eb%d" % i, [128, NTOK], F32, gla_sc) for i in range(2)]
              tA, tA_b = sb("tA", [128, NTOK], F32, gla_sc)
              tB, tB_b = sb("tB", [128, NTOK], F32, gla_sc)
              tC, tC_b = sb("tC", [128, NTOK], F32, gla_sc)
              xsb, xsb_b = sb("xsb", [128, 1026], F32, gla_sc)
              xr = [sb("xr%d" % i, [128, 1026], F32, gla_sc) for i in range(2)]
              sin_f = [sb("sinf%d" % i, [128, 512], F32, gla_sc) for i in range(2)]
              s_bf = [sb("sbf%d" % i, [128, 512], BF16, gla_sc) for i in range(2)]
              ss_f = [sb("ssf%d" % i, [128, 512], F32, gla_sc) for i in range(2)]
              attT, attT_b = sb("attT", [128, 128], BF16, gla_sc)
              otmp, otmp_b = sb("otmp", [128, 512], F32, gla_sc)
              og, og_b = sb("og", [128, 512], BF16, gla_sc)
              junk2, junk2_b = sb("junk2", [128, 512], BF16, gla_sc)
              dm, dm_b = sb("dm", [128, 2], F32, gla_sc)

              S.dma("gpsimd", [(wgk2[:, :], gla_w_gk2[:, :])], writes=[wgk2_b])
              S.op("vector", "tensor_scalar", reads=[cf_b], writes=[negb_b], out=negb[:, :],
                   in0=cf[:, C_BGK:C_BGK + 8], scalar1=-1.0, scalar2=None, op0=ALU.mult)

              slab, slab_b = load_slab(gla_w_in, [(6144, 16, 0)])

              def ev_gkl(g0, gn, ps, ps_b):
                  S.op("scalar", "copy", reads=[ps_b], writes=[gklT_b], out=gklT[:, g0:g0 + gn], in_=ps[:16, :gn])

              mm_feat(uT, uT_b, slab, slab_b, 0, 16, ev_gkl)

              xs_f = [(xsb[:, 0:512], xsb_b), (xsb[:, 512:1024], xsb_b)]

              def gla_chunk(h, t, c0, n, Sf, Sb, with_out, first):
                  if with_out:
                      pa, pa_b = bank()
                      for dc in range(2):
                          mm(pa[:n, :n], pa_b, ktT[:, dc, c0:c0 + n], qtT[:, dc, c0:c0 + n], dc == 0, dc == 1,
                             [ktT_b, qtT_b])
                      S.op("vector", "tensor_tensor", reads=[pa_b, cf_b], writes=[attT_b], out=attT[:n, :n],
                           in0=pa[:n, :n], in1=tri_f[:n, :n], op=ALU.mult)
                      po, po_b = bank()
                      mm(po[:n, :], po_b, attT[:n, :n], v_sb[:n, t, :], True, False, [attT_b, v_sbb])
                      for dc in range(2):
                          mm(po[:n, :], po_b, qtT[:, dc, c0:c0 + n], Sb[dc][0][:, :], False, dc == 1,
                             [qtT_b, Sb[dc][1]])
                      S.op("scalar", "activation", reads=[po_b], writes=[junk2_b, ss_b], out=junk2[:n, :],
                           in_=po[:n, :], func=AF.Square, accum_out=ss[:n, 0:1])
                      rstd_from_ss(n, 1, 512.0)
                      S.op("vector", "scalar_tensor_tensor", reads=[po_b, rstd_b, cf_b], writes=[otmp_b],
                           out=otmp[:n, :], in0=po[:n, :], scalar=rstd[:n, 0:1], in1=cf[:n, C_GOUT:C_GOUT + 512],
                           op0=ALU.mult, op1=ALU.mult)
                      S.op("vector", "tensor_tensor", reads=[otmp_b, g_sbb], writes=[og_b], out=og[:n, :],
                           in0=otmp[:n, :], in1=g_sb[:n, t, :], op=ALU.mult)
                      tp, tp_b = tb()
                      for j in range(4):
                          tr(tp[:, j * 128:j * 128 + n], tp_b, og[:n, j * 128:(j + 1) * 128], ident_b[:n, :n],
                             [og_b, cb_b])
                      for j in range(4):
                          S.op("scalar", "copy", reads=[tp_b], writes=[oT_b], out=oT[:, 4 * h + j, c0:c0 + n],
                               in_=tp[:, j * 128:j * 128 + n])
                  for dc in range(2):
                      pd, pd_b = bank()
                      mm(pd[:, :], pd_b, khat[:n, t, dc * 128:(dc + 1) * 128], v_sb[:n, t, :], True, True,
                         [khat_b, v_sbb])
                      sf, sf_b = Sf[dc]
                      if first:
                          S.op("vector", "tensor_copy", reads=[pd_b], writes=[sf_b], out=sf, in_=pd[:, :])
                      else:
                          S.op("vector", "scalar_tensor_tensor", reads=[pd_b, sf_b, eb[dc][1]], writes=[sf_b],
                               out=sf, in0=sf, scalar=eb[dc][0][:, c0 + n - 1:c0 + n], in1=pd[:, :],
                               op0=ALU.mult, op1=ALU.add)
                      if with_out:
                          S.op("scalar", "copy", reads=[sf_b], writes=[Sb[dc][1]], out=Sb[dc][0][:, :], in_=sf)

              for h in range(4):
                  slab, slab_b = load_slab(gla_w_in, [(h * 256, 256, 0), (1024 + h * 256, 256, 256)])
                  for dc in range(2):
                      fc = 2 * h + dc
                      ebt, ebb = eb[dc]
                      for (g0, gn) in GROUPS:
                          ps, ps_b = bank()
                          mm(ps[:, :gn], ps_b, wgk2[:, fc * 128:(fc + 1) * 128], gklT[:, g0:g0 + gn], True, True,
                             [wgk2_b, gklT_b])
                          S.op("scalar", "activation", reads=[ps_b, negb_b], writes=[tA_b], out=tA[:, g0:g0 + gn],
                               in_=ps[:, :gn], func=AF.Exp, scale=-1.0, bias=negb[:, fc:fc + 1])
                      S.op("scalar", "activation", reads=[tA_b], writes=[tB_b], out=tB[:, :], in_=tA[:, :],
                           func=AF.Ln, bias=1.0)
                      S.op("vector", "tensor_tensor_scan", reads=[tB_b, cf_b], writes=[tC_b], out=tC[:, :],
                           data0=resetm, data1=tB[:, :], initial=0.0, op0=ALU.mult, op1=ALU.add)
                      S.op("scalar", "activation", reads=[tC_b], writes=[ebb], out=ebt[:, :], in_=tC[:, :],
                           func=AF.Exp, scale=-1.0 / 16.0)
                      S.op("scalar", "activation", reads=[tC_b], writes=[tA_b], out=tA[:, :], in_=tC[:, :],
                           func=AF.Exp, scale=1.0 / 16.0)

                      def ev_q(g0, gn, ps, ps_b, dc=dc, ebt=ebt, ebb=ebb):
                          S.op("vector", "scalar_tensor_tensor", reads=[ps_b, ebb], writes=[qtT_b],
                               out=qtT[:, dc, g0:g0 + gn], in0=ps[:, :gn], scalar=0.0625, in1=ebt[:, g0:g0 + gn],
                               op0=ALU.mult, op1=ALU.mult)

                      def ev_k(g0, gn, ps, ps_b, dc=dc, ebt=ebt, ebb=ebb):
                          S.op("vector", "tensor_tensor", reads=[ps_b, tA_b], writes=[ktT_b],
                               out=ktT[:, dc, g0:g0 + gn], in0=ps[:, :gn], in1=tA[:, g0:g0 + gn], op=ALU.mult)
                          for (c0, n) in TILES:
                              if c0 < g0 or c0 >= g0 + gn:
                                  continue
                              S.op("vector", "scalar_tensor_tensor", reads=[ps_b, tA_b, ebb], writes=[khT_b],
                                   out=khT[:, dc, c0:c0 + n], in0=ps[:, c0 - g0:c0 - g0 + n],
                                   scalar=ebt[:, c0 + n - 1:c0 + n], in1=tA[:, c0:c0 + n], op0=ALU.mult, op1=ALU.mult)

                      mm_feat(uT, uT_b, slab, slab_b, dc * 128, 128, ev_q)
                      mm_feat(uT, uT_b, slab, slab_b, 256 + dc * 128, 128, ev_k)
                  for t, (c0, n) in enumerate(TILES):
                      tp, tp_b = tb()
                      for dc in range(2):
                          tr(tp[:n, dc * 128:(dc + 1) * 128], tp_b, khT[:, dc, c0:c0 + n], ident_b[:, :],
                             [khT_b, cb_b])
                      S.op("scalar", "copy", reads=[tp_b], writes=[khat_b], out=khat[:n, t, :], in_=tp[:n, 0:256])
                  slab, slab_b = load_slab(gla_w_in, [(2048 + h * 512, 512, 0)])

                  def ev_v(t, c0, n, ps, ps_b):
                      S.op("scalar", "copy", reads=[ps_b], writes=[v_sbb], out=v_sb[:n, t, :], in_=ps[:n, :])

                  mm_tok(uT, uT_b, slab, slab_b, 512, ev_v)
                  gslab, gslab_b = load_slab(gla_w_in, [(4096 + h * 512, 512, 0)])

                  for t in range(8):
                      c0, n = TILES[t]
                      gla_chunk(h, t, c0, n, xs_f, None, False, t == 0)
                  for dc in range(2):
                      ebt, ebb = eb[dc]
                      S.op("vector", "tensor_copy", reads=[ebb], writes=[xsb_b], out=xsb[:, 1024 + dc:1025 + dc],
                           in_=ebt[:, 127:128])
                      for t in range(1, 8):
                          S.op("vector", "tensor_tensor", reads=[ebb, xsb_b], writes=[xsb_b],
                               out=xsb[:, 1024 + dc:1025 + dc], in0=xsb[:, 1024 + dc:1025 + dc],
                               in1=ebt[:, t * 128 + 127:t * 128 + 128], op=ALU.mult)
                  xin_b, xout_b = Buf("xin"), Buf("xout")
                  S.dma("sync", [(xg_in[h].ap()[:, :], xsb[:, :])], reads=[xsb_b], writes=[xin_b])
                  S.collective(xg_in[h].ap().opt(), xg_out[h].ap().opt(), reads=[xin_b], writes=[xout_b])

                  def ev_g(t, c0, n, ps, ps_b):
                      S.op("scalar", "activation", reads=[ps_b], writes=[otmp_b], out=otmp[:n, :], in_=ps[:n, :],
                           func=AF.Exp, scale=-1.0)
                      S.op("vector", "tensor_scalar", reads=[otmp_b], writes=[otmp_b], out=otmp[:n, :],
                           in0=otmp[:n, :], scalar1=1.0, scalar2=None, op0=ALU.add)
                      S.op("vector", "reciprocal", reads=[otmp_b], writes=[otmp_b], out=otmp[:n, :], in_=otmp[:n, :])
                      S.op("vector", "tensor_tensor", reads=[otmp_b, ps_b], writes=[g_sbb], out=g_sb[:n, t, :],
                           in0=otmp[:n, :], in1=ps[:n, :], op=ALU.mult)

                  mm_tok(uT, uT_b, gslab, gslab_b, 512, ev_g)

                  for j in range(7):
                      xt, xb = xr[j % 2]
                      S.dma("sync", [(xt[:, :], xg_out[h].ap()[j * 128:(j + 1) * 128, :])], reads=[xout_b], writes=[xb])
                      mj = cf[:, C_CMASK + j:C_CMASK + j + 1]
                      S.op("vector", "tensor_scalar", reads=[xb, cf_b], writes=[dm_b], out=dm[:, :],
                           in0=xt[:, 1024:1026], scalar1=-1.0, scalar2=mj, op0=ALU.add, op1=ALU.mult)
                      S.op("vector", "tensor_scalar", reads=[dm_b], writes=[dm_b], out=dm[:, :], in0=dm[:, :],
                           scalar1=1.0, scalar2=None, op0=ALU.add)
                      for dc in range(2):
                          sf, sf_b = sin_f[dc]
                          if j == 0:
                              S.op("vector", "tensor_scalar", reads=[xb, cf_b], writes=[sf_b], out=sf[:, :],
                                   in0=xt[:, dc * 512:(dc + 1) * 512], scalar1=mj, scalar2=None, op0=ALU.mult)
                          else:
                              S.op("vector", "tensor_scalar", reads=[xb, cf_b], writes=[otmp_b], out=otmp[:, :],
                                   in0=xt[:, dc * 512:(dc + 1) * 512], scalar1=mj, scalar2=None, op0=ALU.mult)
                              S.op("vector", "scalar_tensor_tensor", reads=[sf_b, dm_b, otmp_b], writes=[sf_b],
                                   out=sf[:, :], in0=sf[:, :], scalar=dm[:, dc:dc + 1], in1=otmp[:, :],
                                   op0=ALU.mult, op1=ALU.add)
                  for dc in range(2):
                      S.op("scalar", "copy", reads=[sin_f[dc][1]], writes=[s_bf[dc][1]], out=s_bf[dc][0][:, :],
                           in_=sin_f[dc][0][:, :])
                  sinf_v = [(sin_f[dc][0][:, :], sin_f[dc][1]) for dc in range(2)]
                  for t in range(8):
                      c0, n = TILES[t]
                      gla_chunk(h, t, c0, n, sinf_v, s_bf, True, False)
                  S.dma("sync", [(sg_p[h, dc * 128:(dc + 1) * 128, :], sin_f[dc][0][:, :]) for dc in range(2)],
                        reads=[sin_f[0][1], sin_f[1][1]])
                  S.dma("sync", [(ss_f[dc][0][:, :], st_s[h, dc * 128:(dc + 1) * 128, :]) for dc in range(2)],
                        writes=[ss_f[0][1], ss_f[1][1]])
                  for dc in range(2):
                      S.op("scalar", "copy", reads=[ss_f[dc][1]], writes=[s_bf[dc][1]], out=s_bf[dc][0][:, :],
                           in_=ss_f[dc][0][:, :])
                  ssf_v = [(ss_f[dc][0][:, :], ss_f[dc][1]) for dc in range(2)]
                  gla_chunk(h, 8, 1024, 64, ssf_v, s_bf, True, False)
                  S.dma("sync", [(sg_s[h, dc * 128:(dc + 1) * 128, :], ss_f[dc][0][:, :]) for dc in range(2)],
                        reads=[ss_f[0][1], ss_f[1][1]])
              S.barrier()

        except _Skip:
            pass

        hh, hh_b = sb("hh", [128, 9, D], F32)
        hb = [Buf("h%d" % t) for t in range(9)]
        for t, (c0, n) in enumerate(TILES):
            S.dma("sync", [(hh[:n, t, :], x_all[c0:c0 + n, :])], writes=[hb[t]])

        def add_into_h(s):
            def ev(t, c0, n, ps, ps_b):
                S.op("vector", "tensor_tensor", reads=[ps_b, hb[t]], writes=[hb[t]],
                     out=hh[:n, t, s * 512:(s + 1) * 512], in0=ps[:n, :], in1=hh[:n, t, s * 512:(s + 1) * 512],
                     op=ALU.add)
            return ev

        if stage >= 1 and "gla" not in skip:
            for s in range(4):
                slab, slab_b = load_slab(gla_w_out, [(s * 512, 512, 0)])
                mm_tok(oT, oT_b, slab, slab_b, 512, add_into_h(s))
        S.barrier()

        def get_h(t, n):
            return hh[:n, t, :], hb[t]

        def ffn(layer):
            norm_phase(get_h, 1 + 3 * layer if layer == 0 else 4)
            w_in = ffn_w_in[layer]
            w_out = ffn_w_out[layer]
            with ExitStack() as sc:
                hT = [(aux[:, i * 4 * NTOK:(i + 1) * 4 * NTOK].rearrange("p (c t) -> p c t", c=4), Buf("hT%d" % i))
                      for i in range(2)]
                wo, wo_b = sb("wo", [128, 4, D], BF16, sc)
                sgt = [sb("sgt%d" % i, [128, 512], BF16, sc) for i in range(2)]
                for g in range(DFF // 512):
                    ht, ht_b = hT[g % 2]
                    for sub in range(2):
                        slab, slab_b = load_slab(w_in, [(g * 512 + sub * 256, 256, 0),
                                                        (DFF + g * 512 + sub * 256, 256, 256)])
                        for j in range(2):
                            for gi, (g0, gn) in enumerate(GROUPS):
                                pg, pg_b = bank()
                                pu, pu_b = bank()
                                for kc in range(KC):
                                    mm(pg[:, :gn], pg_b, slab[:, kc, j * 128:(j + 1) * 128], uT[:, kc, g0:g0 + gn],
                                       kc == 0, kc == KC - 1, [uT_b, slab_b])
                                for kc in range(KC):
                                    mm(pu[:, :gn], pu_b, slab[:, kc, 256 + j * 128:256 + (j + 1) * 128],
                                       uT[:, kc, g0:g0 + gn], kc == 0, kc == KC - 1, [uT_b, slab_b])
                                st, st_b = sgt[gi % 2]
                                S.op("scalar", "activation", reads=[pg_b], writes=[st_b], out=st[:, :gn],
                                     in_=pg[:, :gn], func=AF.Silu)
                                S.op("vector", "tensor_tensor", reads=[st_b, pu_b], writes=[ht_b],
                                     out=ht[:, sub * 2 + j, g0:g0 + gn], in0=st[:, :gn], in1=pu[:, :gn], op=ALU.mult)
                    wov = w_out[g * 512:(g + 1) * 512, :].rearrange("(kc p) c -> p kc c", p=128)
                    S.dma("gpsimd", [(wo[:, 0:2, :], wov[:, 0:2, :]), (wo[:, 2:4, :], wov[:, 2:4, :])], writes=[wo_b])
                    for s in range(4):
                        mm_tok(ht, ht_b, wo, wo_b, 512, add_into_h(s), kcn=4, col0=s * 512)
            S.barrier()

        if stage >= 2 and "ffn" not in skip:
            ffn(0)

        gk_bc = cf[:, C_GK:C_GK + 128]
        gq_bc = cf[:, C_GQ:C_GQ + 128]

        def headnorm_evac(dst, dst_b, gvec):
            def ev(t, c0, n, ps, ps_b, junk, junk_b):
                for j in range(4):
                    S.op("scalar", "activation", reads=[ps_b], writes=[junk_b, ss_b], out=junk[:n, 0:128],
                         in_=ps[:n, j * 128:(j + 1) * 128], func=AF.Square, accum_out=ss[:n, j:j + 1])
                rstd_from_ss(n, 4, 128.0)
                for j in range(4):
                    S.op("vector", "scalar_tensor_tensor", reads=[ps_b, rstd_b, cf_b], writes=[dst_b],
                         out=dst[:n, j * 128:(j + 1) * 128], in0=ps[:n, j * 128:(j + 1) * 128],
                         scalar=rstd[:n, j:j + 1], in1=gvec[:n, :], op0=ALU.mult, op1=ALU.mult)
            return ev

        if stage >= 3:
            norm_phase(get_h, 2)
            with ExitStack() as sc:
                kf = [sb("kf%d" % i, [128, 512], F32, sc) for i in range(2)]
                kb = [sb("kb%d" % i, [128, 512], BF16, sc) for i in range(2)]
                kst = [sb("kst%d" % i, [128, 4, 128], BF16, sc) for i in range(2)]
                junk, junk_b = sb("junk3", [128, 128], BF16, sc)
                lft, lft_b = sb("lft", [128, 16], F32, sc)
                lfo = [sb("lfo%d" % i, [128, 16], F32, sc) for i in range(2)]
                cnt = {"i": 0}
                ktin_b = Buf("ktin")
                for s in range(4):
                    slab, slab_b = load_slab(kv_w, [(s * 512, 512, 0)])

                    def ev_k(t, c0, n, ps, ps_b, s=s):
                        i = cnt["i"] % 2
                        cnt["i"] += 1
                        kft, kf_b = kf[i]
                        kbt, kb_b = kb[i]
                        kt, kt_b = kst[i]
                        headnorm_evac(kft, kf_b, gk_bc)(t, c0, n, ps, ps_b, junk, junk_b)
                        S.dma("sync", [(k_o[c0:c0 + n, s * 512:(s + 1) * 512], kft[:n, :])], reads=[kf_b])
                        S.op("scalar", "copy", reads=[kf_b], writes=[kb_b], out=kbt[:n, :], in_=kft[:n, :])
                        tp, tp_b = tb()
                        for j in range(4):
                            tr(tp[:, j * 128:j * 128 + n], tp_b, kbt[:n, j * 128:(j + 1) * 128], ident_b[:n, :n],
                               [kb_b, cb_b])
                        S.op("vector", "tensor_copy", reads=[tp_b], writes=[kt_b], out=kt[:, :, :n],
                             in_=tp[:, 0:512].rearrange("p (j c) -> p j c", j=4)[:, :, :n])
                        pairs = []
                        for j in range(4):
                            r0 = (4 * s + j) * 128
                            if t < 8:
                                pairs.append((kt_in.ap()[r0:r0 + 128, c0:c0 + n], kt[:, j, :n]))
                            else:
                                pairs.append((kts_in.ap()[r0:r0 + 128, 0:n], kt[:, j, :n]))
                        S.dma("sync", pairs, reads=[kt_b], acc=[ktin_b])

                    mm_tok(uT, uT_b, slab, slab_b, 512, ev_k)
                ktout_b, vout_b, lfout_b = Buf("ktout"), Buf("vout"), Buf("lfout")
                vslab0 = load_slab(kv_w, [(2048, 512, 0)])
                if stage >= 3.5:
                    S.collective(kt_in.ap().opt(), kt_out.ap().opt(), reads=[ktin_b], writes=[ktout_b])
                vin_b = Buf("vin")
                for s in range(4):
                    slab, slab_b = vslab0 if s == 0 else load_slab(kv_w, [(2048 + s * 512, 512, 0)])

                    def ev_v2(t, c0, n, ps, ps_b, s=s):
                        i = cnt["i"] % 2
                        cnt["i"] += 1
                        kft, kf_b = kf[i]
                        kbt, kb_b = kb[i]
                        S.op("scalar", "copy", reads=[ps_b], writes=[kf_b], out=kft[:n, :], in_=ps[:n, :])
                        S.dma("sync", [(v_o[c0:c0 + n, s * 512:(s + 1) * 512], kft[:n, :])], reads=[kf_b])
                        S.op("vector", "tensor_copy", reads=[kf_b], writes=[kb_b], out=kbt[:n, :], in_=kft[:n, :])
                        if t < 8:
                            dst = v_in.ap()[c0:c0 + n, s * 512:(s + 1) * 512]
                        else:
                            dst = vs_in.ap()[0:n, s * 512:(s + 1) * 512]
                        S.dma("sync", [(dst, kbt[:n, :])], reads=[kb_b], acc=[vin_b])

                    mm_tok(uT, uT_b, slab, slab_b, 512, ev_v2)
                slab, slab_b = load_slab(kv_w, [(4096, 16, 0)])
                if stage >= 3.5:
                    S.collective(v_in.ap().opt(), v_out.ap().opt(), reads=[vin_b], writes=[vout_b])
                lfin_b = Buf("lfin")

                def ev_f(t, c0, n, ps, ps_b):
                    i = cnt["i"] % 2
                    cnt["i"] += 1
                    lo, lo_b = lfo[i]
                    S.op("vector", "tensor_tensor", reads=[ps_b, cf_b], writes=[lft_b], out=lft[:n, :],
                         in0=ps[:n, 0:16], in1=cf[:n, C_BF:C_BF + 16], op=ALU.add)
                    S.op("scalar", "activation", reads=[lft_b], writes=[lft_b], out=lft[:n, :], in_=lft[:n, :],
                         func=AF.Exp, scale=-1.0)
                    S.op("scalar", "activation", reads=[lft_b], writes=[lft_b], out=lft[:n, :], in_=lft[:n, :],
                         func=AF.Ln, bias=1.0)
                    S.op("vector", "tensor_scalar", reads=[lft_b], writes=[lo_b], out=lo[:n, :], in0=lft[:n, :],
                         scalar1=-1.0, scalar2=None, op0=ALU.mult)
                    dst = lf_in.ap()[c0:c0 + n, :] if t < 8 else lfs_in.ap()[0:n, :]
                    S.dma("sync", [(lf_o[c0:c0 + n, :], lo[:n, :]), (dst, lo[:n, :])], reads=[lo_b], acc=[lfin_b])

                mm_tok(uT, uT_b, slab, slab_b, 16, ev_f)
                if stage >= 3.5:
                    S.collective(lf_in.ap().opt(), lf_out.ap().opt(), reads=[lfin_b], writes=[lfout_b])
            S.barrier()

        if stage >= 4:
            norm_phase(get_h, 3)
            fox_sc = ExitStack()
            with fox_sc:
                QT, QT_b = aux[:, :].rearrange("p (c t) -> p c t", c=KC), Buf("QT")
                with ExitStack() as sc:
                    qf = [sb("qf%d" % i, [128, 512], F32, sc) for i in range(2)]
                    qb = [sb("qb%d" % i, [128, 512], BF16, sc) for i in range(2)]
                    junk, junk_b = sb("junk4", [128, 128], BF16, sc)
                    cnt = {"i": 0}
                    for s in range(4):
                        slab, slab_b = load_slab(fox_w_q, [(s * 512, 512, 0)])

                        def ev_q2(t, c0, n, ps, ps_b, s=s):
                            i = cnt["i"] % 2
                            cnt["i"] += 1
                            qft, qf_b = qf[i]
                            qbt, qb_b = qb[i]
                            headnorm_evac(qft, qf_b, gq_bc)(t, c0, n, ps, ps_b, junk, junk_b)
                            S.op("scalar", "copy", reads=[qf_b], writes=[qb_b], out=qbt[:n, :], in_=qft[:n, :])
                            tp, tp_b = tb()
                            for j in range(4):
                                tr(tp[:, j * 128:j * 128 + n], tp_b, qbt[:n, j * 128:(j + 1) * 128],
                                   ident_b[:n, :n], [qb_b, cb_b])
                            S.op("vector", "tensor_copy", reads=[tp_b], writes=[QT_b],
                                 out=QT[:, 4 * s:4 * s + 4, c0:c0 + n],
                                 in_=tp[:, 0:512].rearrange("p (j c) -> p j c", j=4)[:, :, :n])

                        mm_tok(uT, uT_b, slab, slab_b, 512, ev_q2)
                S.barrier()

                S.barrier()
                uflat = uT[:, :, :].rearrange("p c t -> p (c t)")
                uflat32 = uflat.bitcast(F32)

                def carve(flat, off, shape):
                    n = 1
                    for d_ in shape:
                        n *= d_
                    v = flat[:, off:off + n]
                    if len(shape) == 2:
                        v = v.rearrange("p (a b) -> p a b", a=shape[0])
                    return v

                lfa, lfa_b = carve(uflat32, 0, [64, 16]), Buf("lfa")
                tot, tot_b = carve(uflat32, 1024, [64, 16]), Buf("tot")
                inc, inc_b = carve(uflat32, 2048, [64, 16]), Buf("inc")
                ckk, ck_b = carve(uflat32, 3072, [64, 16]), Buf("ckk")
                tmpc, tmpc_b = carve(uflat32, 4096, [64, 16]), Buf("tmpc")
                lfs, lfs_b = carve(uflat32, 5120, [17, 16]), Buf("lfs")
                tots, tots_b = carve(uflat32, 5120 + 272, [17, 16]), Buf("tots")
                incs, incs_b = carve(uflat32, 5120 + 544, [17, 16]), Buf("incs")
                cref, cref_b = sb("cref", [128, 3, 16], F32, fox_sc)
                ckown, ckown_b = sb("ckown", [128, 8, 16], F32, fox_sc)
                biasG = [sb("biasG%d" % i, [128, 64, 16], F32, fox_sc) for i in range(2)]
                biasO = [sb("biasO%d" % i, [128, 8, 16], F32, fox_sc) for i in range(2)]
                biasS, biasS_b = sb("biasS", [128, 17, 16], F32, fox_sc)

                def cumsum_tiles(src, src_b, nt_, tot_t, tot_tb, inc_t, inc_tb, dst, dst_b):
                    flat = src[:, :, :].rearrange("p t h -> p (t h)")
                    totf = tot_t[:, :, :].rearrange("p t h -> p (t h)")
                    ncol = nt_ * 16
                    for c0 in range(0, ncol, 512):
                        cn = min(512, ncol - c0)
                        ps, ps_b = bank()
                        mm(ps[:, :cn], ps_b, ones_f, flat[:, c0:c0 + cn], True, True, [cf_b, src_b])
                        S.op("scalar", "copy", reads=[ps_b], writes=[tot_tb], out=totf[:, c0:c0 + cn], in_=ps[:, :cn])
                    for hd in range(16):
                        S.op("vector", "tensor_tensor_scan", reads=[tot_tb, cf_b], writes=[inc_tb],
                             out=inc_t[:, :, hd], data0=ones_f[:, 0:nt_], data1=tot_t[:, :, hd], initial=0.0,
                             op0=ALU.mult, op1=ALU.add)
                    S.op("vector", "tensor_tensor", reads=[inc_tb, tot_tb], writes=[inc_tb], out=inc_t[:, :, :],
                         in0=inc_t[:, :, :], in1=tot_t[:, :, :], op=ALU.subtract)
                    dstf = dst[:, :, :].rearrange("p t h -> p (t h)")
                    incf = inc_t[:, :, :].rearrange("p t h -> p (t h)")
                    for c0 in range(0, nt_, 32):
                        cn = min(32, nt_ - c0)
                        ps, ps_b = bank()
                        for t in range(cn):
                            mm(ps[:, t * 16:(t + 1) * 16], ps_b, tri_f, src[:, c0 + t, :], True, True, [cf_b, src_b])
                        S.op("vector", "tensor_tensor", reads=[ps_b, inc_tb], writes=[dst_b],
                             out=dstf[:, c0 * 16:(c0 + cn) * 16], in0=ps[:, :cn * 16],
                             in1=incf[:, c0 * 16:(c0 + cn) * 16], op=ALU.add)

                lfv = lf_out.ap().rearrange("(t p) h -> p t h", p=128)
                S.dma("sync", [(lfa[:, 8 * r:8 * r + 8, :], lfv[:, 8 * r:8 * r + 8, :]) for r in range(8)],
                      reads=[lfout_b], writes=[lfa_b])
                cumsum_tiles(lfa, lfa_b, 64, tot, tot_b, inc, inc_b, ckk, ck_b)
                for qt in range(2):
                    wqv = cf[:, C_WQ + 64 * qt:C_WQ + 64 * (qt + 1)]
                    S.op("vector", "tensor_tensor", reads=[tot_b, cf_b], writes=[tmpc_b], out=tmpc[:, :, :],
                         in0=tot[:, :, :], in1=wqv.unsqueeze(2).to_broadcast([128, 64, 16]), op=ALU.mult)
                    S.op("vector", "tensor_reduce", reads=[tmpc_b], writes=[cref_b], out=cref[:, qt, :],
                         in_=tmpc[:, :, :].rearrange("p t h -> p h t"), axis=AX.X, op=ALU.add)
                for j in range(8):
                    selv = cf[:, C_SEL + 64 * j:C_SEL + 64 * (j + 1)]
                    S.op("vector", "tensor_tensor", reads=[ck_b, cf_b], writes=[tmpc_b], out=tmpc[:, :, :],
                         in0=ckk[:, :, :], in1=selv.unsqueeze(2).to_broadcast([128, 64, 16]), op=ALU.mult)
                    S.op("vector", "tensor_reduce", reads=[tmpc_b], writes=[ckown_b], out=ckown[:, j, :],
                         in_=tmpc[:, :, :].rearrange("p t h -> p h t"), axis=AX.X, op=ALU.add)
                rbv = cf[:, C_RANKB:C_RANKB + 64]
                for qt in range(2):
                    bg, bg_b = biasG[qt]
                    S.op("vector", "tensor_tensor", reads=[cref_b, ck_b], writes=[bg_b], out=bg[:, :, :],
                         in0=cref[:, qt:qt + 1, :].to_broadcast([128, 64, 16]), in1=ckk[:, :, :], op=ALU.subtract)
                    S.op("vector", "tensor_tensor", reads=[bg_b, cf_b], writes=[bg_b], out=bg[:, :, :],
                         in0=bg[:, :, :], in1=rbv.unsqueeze(2).to_broadcast([128, 64, 16]), op=ALU.add)
                    bo, bo_b = biasO[qt]
                    S.op("vector", "tensor_tensor", reads=[cref_b, ckown_b], writes=[bo_b], out=bo[:, :, :],
                         in0=cref[:, qt:qt + 1, :].to_broadcast([128, 8, 16]), in1=ckown[:, :, :], op=ALU.subtract)
                S.op("vector", "memset", writes=[lfs_b], ap=lfs[:, :, :], constant=0.0)
                clv = cl_c.rearrange("(t p) h -> p t h", p=128)
                S.dma("sync", [(lfs[:, 0:8, :], clv[:, 0:8, :]), (lfs[:, 8:16, :], clv[:, 8:16, :]),
                               (lfs[:64, 16, :], lfs_in.ap()[:, :])], reads=[lfin_b], writes=[lfs_b])
                cumsum_tiles(lfs, lfs_b, 17, tots, tots_b, incs, incs_b, biasS, biasS_b)
                S.op("vector", "tensor_reduce", reads=[tots_b], writes=[cref_b], out=cref[:, 2, :],
                     in_=tots[:, :, :].rearrange("p t h -> p h t"), axis=AX.X, op=ALU.add)
                S.op("vector", "tensor_tensor", reads=[cref_b, biasS_b], writes=[biasS_b], out=biasS[:, :, :],
                     in0=cref[:, 2:3, :].to_broadcast([128, 17, 16]), in1=biasS[:, :, :], op=ALU.subtract)

                S.barrier()
                KTh, KTh_b = carve(uflat, 0, [8, 1024]), Buf("KTh")
                Vh, Vh_b = carve(uflat, 8192, [64, 130]), Buf("Vh")
                w0 = wp[0][0][:, :, :].rearrange("p c t -> p (c t)")
                w1 = wp[1][0][:, :, :].rearrange("p c t -> p (c t)")
                KTo, KTo_b = carve(w0, 0, [NTOK]), Buf("KTo")
                Vo, Vo_b = carve(w0, 1088, [9, 130]), Buf("Vo")
                kct, kct_b = carve(w0, 2304, [16, 128]), Buf("kct")
                KTs, KTs_b = carve(w0, 4352, [2048]), Buf("KTs")
                Vs, Vs_b = carve(w1, 0, [16, 130]), Buf("Vs")
                PT = [(carve(w1, 2080 + 512 * i, [512]), Buf("PT%d" % i)) for i in range(3)]
                OTh, OTh_b = carve(w1, 3616, [NTOK]), Buf("OTh")
                woh, woh_b = carve(w1, 4704, [D]), Buf("woh")
                onb, onb_b = carve(w1, 6752, [128]), Buf("onb")
                rec, rec_b = sb("rec", [128, 1], F32, fox_sc)
                S.op("vector", "memset", writes=[Vh_b], ap=Vh[:, :, 128:130], constant=1.0)
                S.op("vector", "memset", writes=[Vo_b], ap=Vo[:, :, 128:130], constant=1.0)
                S.op("vector", "memset", writes=[Vs_b], ap=Vs[:, :, 128:130], constant=1.0)
                pcnt = {"i": 0}
                SC = 128.0 ** -0.5

                def att_qk(h, kmat, kb_, n_k, qcol0, qn, lo, diag, bias_ap, bias_b, vmat, vb_, obanks, first, last,
                           sub_n):
                    ps, ps_b = bank(4, 7)
                    c_lo = lo * 128
                    if diag:
                        dn_ = min(128, qn - c_lo)
                        mm(ps[:n_k, c_lo:c_lo + dn_], ps_b, kmat, QT[:, h, qcol0 + c_lo:qcol0 + c_lo + dn_], True, False,
                           [kb_, QT_b])
                        mm(ps[:n_k, c_lo:c_lo + dn_], ps_b, ident_b[:n_k, :n_k], negm_b[:n_k, :dn_], False, True, [cb_b])
                        if c_lo + dn_ < qn:
                            mm(ps[:n_k, c_lo + dn_:qn], ps_b, kmat, QT[:, h, qcol0 + c_lo + dn_:qcol0 + qn], True, True,
                               [kb_, QT_b])
                    else:
                        mm(ps[:n_k, c_lo:qn], ps_b, kmat, QT[:, h, qcol0 + c_lo:qcol0 + qn], True, True, [kb_, QT_b])
                    pt, pt_b = PT[pcnt["i"] % 3]
                    pcnt["i"] += 1
                    S.op("scalar", "activation", reads=[ps_b, bias_b], writes=[pt_b], out=pt[:n_k, c_lo:qn],
                         in_=ps[:n_k, c_lo:qn], func=AF.Exp, scale=SC, bias=bias_ap)
                    nsub = (qn + 127) // 128

                    def pv():
                        for sub in range(lo, nsub):
                            ob, ob_b = obanks[sub]
                            mm(ob[:sub_n, 0:129], ob_b, pt[:n_k, sub * 128:sub * 128 + sub_n], vmat, first, last,
                               [pt_b, vb_])
                    return pv

                def run_steps(steps):
                    pend = []
                    for args in steps:
                        pend.append(att_qk(*args))
                        if len(pend) > 2:
                            pend.pop(0)()
                    for p_ in pend:
                        p_()

                def att_evac(h, obanks, qcol0, nsub, sub_n):
                    for sub in range(nsub):
                        ob, ob_b = obanks[sub]
                        S.op("vector", "reciprocal", reads=[ob_b], writes=[rec_b], out=rec[:sub_n, :],
                             in_=ob[:sub_n, 128:129])
                        S.op("vector", "tensor_scalar", reads=[ob_b, rec_b], writes=[onb_b], out=onb[:sub_n, :],
                             in0=ob[:sub_n, 0:128], scalar1=rec[:sub_n, 0:1], scalar2=None, op0=ALU.mult)
                        tp, tp_b = tb()
                        tr(tp[:, 0:sub_n], tp_b, onb[:sub_n, :], ident_b[:sub_n, :sub_n], [onb_b, cb_b])
                        S.op("scalar", "copy", reads=[tp_b], writes=[OTh_b],
                             out=OTh[:, qcol0 + sub * 128:qcol0 + sub * 128 + sub_n], in_=tp[:, 0:sub_n])

                obanks = [pbank[i] for i in range(4)]
                ktv = kt_out.ap().rearrange("(r hd) t -> hd r t", r=8)
                vv = v_out.ap().rearrange("(t p) c -> p t c", p=128)
                for h in range(16):
                    hs = slice(h * 128, (h + 1) * 128)
                    S.dma("sync", [(KTh[:, 0:4, :], ktv[hs, 0:4, :]), (KTh[:, 4:8, :], ktv[hs, 4:8, :])],
                          reads=[ktout_b], writes=[KTh_b])
                    S.dma("sync", [(Vh[:, 8 * r:8 * r + 8, 0:128], vv[:, 8 * r:8 * r + 8, hs]) for r in range(8)],
                          reads=[vout_b], writes=[Vh_b])
                    S.dma("sync", [(KTo[:, 0:1024], kt_in.ap()[hs, :]), (KTo[:, 1024:NTOK], kts_in.ap()[hs, :])],
                          reads=[ktin_b], writes=[KTo_b])
                    S.dma("sync", [(Vo[:, 0:8, 0:128], v_in.ap().rearrange("(t p) c -> p t c", p=128)[:, :, hs]),
                                   (Vo[:64, 8, 0:128], vs_in.ap()[:, hs])], reads=[vin_b], writes=[Vo_b])
                    ckv = ck_c.rearrange("(t p) c -> p t c", p=128)
                    cvv = cv_c.rearrange("(t p) c -> p t c", p=128)
                    S.dma("gpsimd", [(kct[:, 8 * r:8 * r + 8, :], ckv[:, 8 * r:8 * r + 8, hs]) for r in range(2)],
                          writes=[kct_b])
                    S.dma("gpsimd", [(Vs[:, 8 * r:8 * r + 8, 0:128], cvv[:, 8 * r:8 * r + 8, hs]) for r in range(2)],
                          writes=[Vs_b])
                    S.dma("gpsimd", [(woh[:, :], fox_w_o[hs, :])], writes=[woh_b])
                    for half in range(2):
                        tp, tp_b = tb()
                        for j in range(8):
                            tr(tp[:, j * 128:(j + 1) * 128], tp_b, kct[:, half * 8 + j, :], ident_b[:, :], [kct_b, cb_b])
                        S.op("vector" if half else "scalar", "tensor_copy" if half else "copy", reads=[tp_b],
                             writes=[KTs_b], out=KTs[:, half * 1024:(half + 1) * 1024], in_=tp[:, :])
                    for qt in range(2):
                        q0 = qt * 512
                        nown = 4 * qt + 4
                        steps = []
                        for j in range(nown):
                            lo = max(0, j - 4 * qt)
                            diag = (j - 4 * qt) >= 0
                            steps.append((h, KTo[:, j * 128:(j + 1) * 128], KTo_b, 128, q0, 512, lo, diag,
                                          biasO[qt][0][:, j, h:h + 1], biasO[qt][1], Vo[:, j, 0:129], Vo_b, obanks,
                                          j == 0, False, 128))
                        for t in range(64):
                            steps.append((h, KTh[:, t // 8, (t % 8) * 128:(t % 8 + 1) * 128], KTh_b, 128, q0, 512, 0,
                                          False, biasG[qt][0][:, t, h:h + 1], biasG[qt][1], Vh[:, t, 0:129], Vh_b,
                                          obanks, False, t == 63, 128))
                        run_steps(steps)
                        att_evac(h, obanks, q0, 4, 128)
                    steps = []
                    for t in range(16):
                        steps.append((h, KTs[:, t * 128:(t + 1) * 128], KTs_b, 128, 1024, 64, 0, False,
                                      biasS[:, t, h:h + 1], biasS_b, Vs[:, t, 0:129], Vs_b, obanks, t == 0, False, 64))
                    steps.append((h, KTo[:, 1024:NTOK], KTo_b, 64, 1024, 64, 0, True, biasS[:64, 16, h:h + 1], biasS_b,
                                  Vo[:64, 8, 0:129], Vo_b, obanks, False, True, 64))
                    run_steps(steps)
                    att_evac(h, obanks, 1024, 1, 64)
                    for s in range(4):
                        for t, (c0, n) in enumerate(TILES):
                            ps, ps_b = bank(4, 6)
                            mm(ps[:n, :], ps_b, OTh[:, c0:c0 + n], woh[:, s * 512:(s + 1) * 512], True, True,
                               [OTh_b, woh_b])
                            add_into_h(s)(t, c0, n, ps, ps_b)
            S.barrier()

        if stage >= 5:
            ffn(1)

        for t, (c0, n) in enumerate(TILES):
            S.dma("sync", [(y_all[c0:c0 + n, :], hh[:n, t, :])], reads=[hb[t]])
        S.barrier()
        block = es.enter_context(nc.Block())
        S.emit(block)
    return nc


def _consts(c):
    cfm = np.zeros((128, NCONST), np.float32)
    cfm[:, C_IDENT:C_IDENT + 128] = np.eye(128, dtype=np.float32)
    tri = np.triu(np.ones((128, 128), np.float32))
    cfm[:, C_TRI:C_TRI + 128] = tri
    rm = np.ones(NTOK, np.float32)
    rm[0:1024:128] = 0.0
    rm[1024] = 0.0
    cfm[:, C_RESET:C_RESET + NTOK] = rm[None, :]
    cfm[:, C_CMASK:C_CMASK + 8] = (np.arange(8) < c).astype(np.float32)[None, :]
    rb = np.where((np.arange(64) // 8) < c, 0.0, NEG).astype(np.float32)
    cfm[:, C_RANKB:C_RANKB + 64] = rb[None, :]
    for qt in range(2):
        w = (np.arange(64) < 8 * c + 4 * (qt + 1)).astype(np.float32)
        cfm[:, C_WQ + 64 * qt:C_WQ + 64 * (qt + 1)] = w[None, :]
    for j in range(8):
        sel = (np.arange(64) == 8 * c + j).astype(np.float32)
        cfm[:, C_SEL + 64 * j:C_SEL + 64 * (j + 1)] = sel[None, :]
    cfm[:, C_ONES:C_ONES + 128] = 1.0
    cbm = np.zeros((128, 256), np.float32)
    cbm[:, 0:128] = np.eye(128, dtype=np.float32)
    cbm[:, 128:256] = (tri - 1.0) * (-NEG)
    return cfm, cbm.astype(ml_dtypes.bfloat16)


_NC_CACHE = {}


def kernel(x_prompt, x_sample, state_gla, cache_k, cache_v, cache_logf, g_mix, g_ffn,
           gla_w_in, gla_w_gk2, gla_b_gk, gla_g_out, gla_w_out, kv_g, kv_w, kv_b_f, kv_g_k,
           fox_w_q, fox_g_q, fox_w_o, ffn_w_in, ffn_w_out, _stage=99):
    f = lambda a: np.ascontiguousarray(np.asarray(a, dtype=np.float32))
    x_prompt, x_sample, state_gla = f(x_prompt), f(x_sample), f(state_gla)
    cache_k, cache_v, cache_logf = f(cache_k), f(cache_v), f(cache_logf)
    if _stage not in _NC_CACHE:
        _NC_CACHE[_stage] = build(_stage)
    nc = _NC_CACHE[_stage]

    def fm(v):
        return f(v).reshape(16, 128).T

    gains = np.concatenate([fm(g_mix[0]), fm(g_ffn[0]), fm(kv_g), fm(g_mix[1]), fm(g_ffn[1])], axis=1)
    shared = {
        "gla_w_in": f(gla_w_in[0]), "gla_w_gk2": f(gla_w_gk2[0]), "gla_w_out": f(gla_w_out[0]),
        "kv_w": f(kv_w), "fox_w_q": f(fox_w_q[0]), "fox_w_o": f(fox_w_o[0]),
        "ffn_w_in": f(ffn_w_in), "ffn_w_out": f(ffn_w_out),
    }
    in_maps = []
    for c in range(NCORES):
        cfm, cbm = _consts(c)
        cfm[:, C_GAINS:C_GAINS + 80] = gains
        cfm[:, C_BGK:C_BGK + 8] = f(gla_b_gk[0]).reshape(8, 128).T
        cfm[:, C_GOUT:C_GOUT + 512] = f(gla_g_out[0])[None, :]
        cfm[:, C_GK:C_GK + 128] = f(kv_g_k)[None, :]
        cfm[:, C_GQ:C_GQ + 128] = f(fox_g_q[0])[None, :]
        cfm[:, C_BF:C_BF + 16] = f(kv_b_f)[None, :]
        m = dict(shared)
        m["x_all"] = np.ascontiguousarray(np.concatenate([x_prompt[0, c * 1024:(c + 1) * 1024], x_sample[c]], axis=0))
        m["st_s"] = np.ascontiguousarray(state_gla[0, c])
        m["ck_c"] = np.ascontiguousarray(cache_k[c].reshape(2048, 2048))
        m["cv_c"] = np.ascontiguousarray(cache_v[c].reshape(2048, 2048))
        m["cl_c"] = np.ascontiguousarray(cache_logf[c])
        m["constf"] = cfm
        m["constb"] = cbm
        in_maps.append(m)
    res = run_bass_kernel_spmd(nc, in_maps, core_ids=list(range(NCORES)))
    R = res.results
    cat = lambda k, lo, hi: np.concatenate([np.asarray(R[c][k], np.float32)[lo:hi] for c in range(NCORES)], axis=0)
    y_p = cat("y_all", 0, 1024)[None]
    y_s = np.stack([np.asarray(R[c]["y_all"], np.float32)[1024:NTOK] for c in range(NCORES)])
    sgp = np.asarray(R[NCORES - 1]["sg_p"], np.float32)[None, None]
    k_p = cat("k_o", 0, 1024).reshape(1, 8192, 16, 128)
    v_p = cat("v_o", 0, 1024).reshape(1, 8192, 16, 128)
    lf_p = cat("lf_o", 0, 1024).reshape(1, 8192, 16)
    sgs = np.stack([np.asarray(R[c]["sg_s"], np.float32) for c in range(NCORES)])[None]
    k_s = np.stack([np.asarray(R[c]["k_o"], np.float32)[1024:NTOK] for c in range(NCORES)]).reshape(8, 64, 16, 128)
    v_s = np.stack([np.asarray(R[c]["v_o"], np.float32)[1024:NTOK] for c in range(NCORES)]).reshape(8, 64, 16, 128)
    lf_s = np.stack([np.asarray(R[c]["lf_o"], np.float32)[1024:NTOK] for c in range(NCORES)])
    return (y_p, y_s, sgp, k_p, v_p, lf_p, sgs, k_s, v_s, lf_s)
```
